# Optimizing a Trainium2 kernel written in Bass

```python
import jax, jax.numpy as jnp
from jax import lax
import numpy as np

D_MODEL = 1024
BATCH = 16
SEQ = 2048
DEPTH = 4
DEC_BATCH = 16
DEC_SEQ = 64
PAST_LEN = 2048

CHUNK = 64
D_PLE = 256
D_MIX = 2 * D_MODEL
C_A = D_MIX // 2
C_B = D_MIX - C_A
H_A = 8
DH_A = C_A // H_A
GMLP_CHUNK = 128
CONV_W = 31
EPS = 1e-6

kernel_name = "hymba_gmlp_conformer_stream_step"


def _rms_norm(x, g):
    xf = x.astype(jnp.float32)
    y = xf * lax.rsqrt(jnp.mean(xf * xf, axis=-1, keepdims=True) + EPS)
    return (y * g.astype(jnp.float32)).astype(x.dtype)


def _layer_norm(x, g, b):
    xf = x.astype(jnp.float32)
    xc = xf - jnp.mean(xf, axis=-1, keepdims=True)
    y = xc * lax.rsqrt(jnp.mean(xc * xc, axis=-1, keepdims=True) + EPS)
    return (y * g.astype(jnp.float32) + b.astype(jnp.float32)).astype(x.dtype)


def _masked_spatial(w_s):
    pos = jnp.arange(GMLP_CHUNK) // CHUNK
    mask = pos[:, None] >= pos[None, :]
    return jnp.where(mask[None], w_s, jnp.zeros((), w_s.dtype))


def _branch_inputs(h, g_pre, w_in, ln_v_g, ln_v_b):
    xn = _rms_norm(h, g_pre)
    proj = jnp.einsum('bsd,de->bse', xn, w_in)
    u, v, z_a, a_lin, a_gate, z_b = jnp.split(
        proj, [C_A, 2 * C_A, 3 * C_A, 3 * C_A + C_B, 3 * C_A + 2 * C_B], axis=-1)
    u = jax.nn.gelu(u)
    v = _layer_norm(jax.nn.gelu(v), ln_v_g, ln_v_b)
    a = a_lin * jax.nn.sigmoid(a_gate)
    return u, v, z_a, a, z_b


def _gmlp_chunks(u, v, w_s, b_s):
    bsz, s, _ = u.shape
    n_chunks = s // GMLP_CHUNK
    w_m = _masked_spatial(w_s)
    v5 = v.reshape(bsz, n_chunks, GMLP_CHUNK, H_A, DH_A)
    mixed = jnp.einsum('hts,bcshd->bcthd', w_m, v5) + b_s.T[:, :, None]
    return u * mixed.reshape(bsz, s, C_A)


def _gmlp_first_rows(u, v, w_s, b_s):
    bsz, t, _ = u.shape
    w_m = _masked_spatial(w_s)[:, :t, :t]
    v4 = v.reshape(bsz, t, H_A, DH_A)
    mixed = jnp.einsum('hts,bshd->bthd', w_m, v4) + b_s[:, :t].T[:, :, None]
    return u * mixed.reshape(bsz, t, C_A)


def _conv_tail(a_ext, conv_w, conv_b, ln_c_g, ln_c_b):
    c = lax.conv_general_dilated(
        a_ext, conv_w[:, None, :].astype(a_ext.dtype), window_strides=(1,), padding='VALID',
        dimension_numbers=('NWC', 'WIO', 'NWC'), feature_group_count=C_B) + conv_b
    return jax.nn.silu(_layer_norm(c, ln_c_g, ln_c_b))


def _mix_out(h, y_a, z_a, y_b, z_b, w_out, g_post, p_i, w_pe, w_pg):
    m = jnp.concatenate([y_a * jax.nn.silu(z_a), y_b * jax.nn.silu(z_b)], axis=-1)
    m = jnp.einsum('bse,ed->bsd', m, w_out)
    h = h + _rms_norm(m, g_post)
    gate = jax.nn.sigmoid(jnp.einsum('bsd,de->bse', h, w_pg))
    return h + gate * jnp.einsum('bsp,pd->bsd', p_i, w_pe)


def setup_inputs(seed: int = 0) -> dict:
    key = jax.random.key(seed)
    ks = jax.random.split(key, 24)
    f32 = jnp.float32
    nrm = lambda k, shape, s: jax.random.normal(k, shape, f32) * s
    return {
        "x_prompt": nrm(ks[0], (BATCH, SEQ, D_MODEL), 1.0),
        "x_sample": nrm(ks[1], (DEC_BATCH, DEC_SEQ, D_MODEL), 1.0),
        "state_conv": nrm(ks[2], (DEPTH, DEC_BATCH, CONV_W - 1, C_B), 0.5),
        "p_prompt": nrm(ks[3], (DEPTH, BATCH, SEQ, D_PLE), 1.0),
        "p_sample": nrm(ks[4], (DEPTH, DEC_BATCH, DEC_SEQ, D_PLE), 1.0),
        "g_pre": 1.0 + nrm(ks[5], (DEPTH, D_MODEL), 0.05),
        "w_in": nrm(ks[6], (DEPTH, D_MODEL, 3 * C_A + 3 * C_B), D_MODEL ** -0.5),
        "ln_v_g": 1.0 + nrm(ks[7], (DEPTH, C_A), 0.05),
        "ln_v_b": nrm(ks[8], (DEPTH, C_A), 0.02),
        "w_s": nrm(ks[9], (DEPTH, H_A, GMLP_CHUNK, GMLP_CHUNK), GMLP_CHUNK ** -0.5),
        "b_s": 1.0 + nrm(ks[10], (DEPTH, H_A, GMLP_CHUNK), 0.05),
        "conv_w": nrm(ks[11], (DEPTH, CONV_W, C_B), CONV_W ** -0.5),
        "conv_b": nrm(ks[12], (DEPTH, C_B), 0.02),
        "ln_c_g": 1.0 + nrm(ks[13], (DEPTH, C_B), 0.05),
        "ln_c_b": nrm(ks[14], (DEPTH, C_B), 0.02),
        "w_out": nrm(ks[15], (DEPTH, D_MIX, D_MODEL), D_MIX ** -0.5),
        "g_post": 1.0 + nrm(ks[16], (DEPTH, D_MODEL), 0.05),
        "w_pe": nrm(ks[17], (DEPTH, D_PLE, D_MODEL), D_PLE ** -0.5),
        "w_pg": nrm(ks[18], (DEPTH, D_MODEL, D_MODEL), D_MODEL ** -0.5),
    }


def reference(x_prompt, x_sample, state_conv, p_prompt, p_sample, g_pre, w_in, ln_v_g, ln_v_b,
              w_s, b_s, conv_w, conv_b, ln_c_g, ln_c_b, w_out, g_post, w_pe, w_pg):
    hp, hs = x_prompt, x_sample
    conv_p, conv_s, v_s = [], [], []
    for i in range(DEPTH):
        u, v, z_a, a, z_b = _branch_inputs(hp, g_pre[i], w_in[i], ln_v_g[i], ln_v_b[i])
        y_a = _gmlp_chunks(u, v, w_s[i], b_s[i])
        a_ext = jnp.pad(a, ((0, 0), (CONV_W - 1, 0), (0, 0)))
        y_b = _conv_tail(a_ext, conv_w[i], conv_b[i], ln_c_g[i], ln_c_b[i])
        conv_p.append(a[:, a.shape[1] - (CONV_W - 1):])
        hp = _mix_out(hp, y_a, z_a, y_b, z_b, w_out[i], g_post[i], p_prompt[i], w_pe[i], w_pg[i])

        u, v, z_a, a, z_b = _branch_inputs(hs, g_pre[i], w_in[i], ln_v_g[i], ln_v_b[i])
        y_a = _gmlp_first_rows(u, v, w_s[i], b_s[i])
        a_ext = jnp.concatenate([state_conv[i].astype(a.dtype), a], axis=1)
        y_b = _conv_tail(a_ext, conv_w[i], conv_b[i], ln_c_g[i], ln_c_b[i])
        conv_s.append(a_ext[:, a_ext.shape[1] - (CONV_W - 1):])
        v_s.append(v)
        hs = _mix_out(hs, y_a, z_a, y_b, z_b, w_out[i], g_post[i], p_sample[i], w_pe[i], w_pg[i])

    new_conv_prompt = jnp.stack(conv_p)
    new_conv_sample = jnp.stack(conv_s)
    new_gmlp_v_sample = jnp.stack(v_s)
    return (hp, hs, new_conv_prompt, new_conv_sample, new_gmlp_v_sample)
```

```python
import contextlib
import numpy as np
import concourse.bass as bass
import concourse.mybir as mybir
from concourse.bass_utils import run_bass_kernel_spmd

F32 = mybir.dt.float32
BF16 = mybir.dt.bfloat16
AF = mybir.ActivationFunctionType
ALU = mybir.AluOpType

NCORES = 8
D = 1024
DEPTH = 4
SEQ = 2048
DEC = 64
CW = 31
HIST = CW - 1
DPLE = 256
EPS = 1e-6
ST_T = 512
RING = 5

ENGS = ("pe", "act", "dve", "pool", "sp")


class Op:
    __slots__ = ("eng", "fn", "reads", "writes", "dma", "deps", "marked", "cnt", "idx", "dmaval")

    def __init__(self, eng, fn, reads, writes, dma):
        self.eng, self.fn, self.reads, self.writes, self.dma = eng, fn, reads, writes, dma
        self.deps = []
        self.marked = False
        self.cnt = 0
        self.dmaval = 0


class Prog:
    def __init__(self, nc):
        self.nc = nc
        self.ops = []

    def add(self, eng, fn, reads=(), writes=(), dma=None):
        o = Op(eng, fn, tuple(reads), tuple(writes), dma)
        o.idx = len(self.ops)
        self.ops.append(o)
        return o

    def analyze(self):
        last_w = {}
        readers = {}
        ops = self.ops
        for o in ops:
            deps = set()
            for b in o.reads:
                w = last_w.get(b)
                if w is not None:
                    deps.add(w)
            for b in o.writes:
                w = last_w.get(b)
                if w is not None:
                    deps.add(w)
                for r in readers.get(b, ()):
                    deps.add(r)
            deps.discard(o.idx)
            best = {}
            for d in deps:
                p = ops[d]
                if p.dma is not None:
                    o.deps.append(d)
                    continue
                if o.dma is None and p.eng == o.eng:
                    if p.eng == "pe":
                        continue
                    if not any(b in p.writes for b in o.reads):
                        continue
                if d > best.get(p.eng, -1):
                    best[p.eng] = d
            for d in best.values():
                o.deps.append(d)
                ops[d].marked = True
            for b in o.reads:
                readers.setdefault(b, []).append(o.idx)
            for b in o.writes:
                last_w[b] = o.idx
                readers[b] = []

    def emit(self, final_wait_bufs=()):
        nc = self.nc
        self.analyze()
        cnt = {e: 0 for e in ENGS}
        dmacnt = {}
        for o in self.ops:
            if o.dma is None:
                if o.marked:
                    cnt[o.eng] += 1
                o.cnt = cnt[o.eng]
            else:
                dmacnt[o.dma] = dmacnt.get(o.dma, 0) + 16
                o.dmaval = dmacnt[o.dma]
        ops = self.ops
        with contextlib.ExitStack() as st:
            esem = {e: st.enter_context(nc.semaphore("s_" + e)) for e in ENGS}
            dsem = {b: st.enter_context(nc.semaphore("d_%d" % i)) for i, b in enumerate(dmacnt)}
            block = st.enter_context(nc.Block())

            def mk(ename):
                def body(eng):
                    waited = {}
                    for o in ops:
                        if o.eng != ename:
                            continue
                        need = {}
                        for d in o.deps:
                            p = ops[d]
                            if p.dma is None:
                                s, v = esem[p.eng], p.cnt
                            else:
                                s, v = dsem[p.dma], p.dmaval
                            k = id(s)
                            if v > need.get(k, (None, 0))[1]:
                                need[k] = (s, v)
                        for k, (s, v) in need.items():
                            if waited.get(k, 0) >= v:
                                continue
                            eng.wait_ge(s, v)
                            waited[k] = v
                        ins = o.fn(eng)
                        if o.dma is not None:
                            ins.then_inc(dsem[o.dma], 16)
                        elif o.marked:
                            ins.then_inc(esem[ename], 1)
                    if ename == "sp":
                        for b in final_wait_bufs:
                            eng.wait_ge(dsem[b], dmacnt[b])
                return body

            block.tensor(mk("pe"))
            block.scalar(mk("act"))
            block.vector(mk("dve"))
            block.gpsimd(mk("pool"))
            block.sync(mk("sp"))


class STInfo:
    def __init__(self, kind, b, t0, T, segs, seq_start, final):
        self.kind, self.b, self.t0, self.T, self.segs = kind, b, t0, T, segs
        self.seq_start, self.final = seq_start, final


def build_program():
    nc = bass.Bass("TRN2", target_bir_lowering=False)

    def din(name, shape):
        return nc.dram_tensor(name, list(shape), F32, kind="ExternalInput").ap()

    def dout(name, shape):
        return nc.dram_tensor(name, list(shape), F32, kind="ExternalOutput").ap()

    xp = din("xp", [2, SEQ, D])
    xsm = din("xsm", [2, DEC, D])
    sc = din("sc", [DEPTH, 2, HIST, D])
    pp = din("pp", [DEPTH, 2, SEQ, DPLE])
    psm = din("psm", [DEPTH, 2, DEC, DPLE])
    g_pre = din("g_pre", [DEPTH, D])
    w_in = din("w_in", [DEPTH, D, 6 * D])
    ln_v_g = din("ln_v_g", [DEPTH, D])
    ln_v_b = din("ln_v_b", [DEPTH, D])
    w_s = din("w_s", [DEPTH, 8, 128, 128])
    b_s = din("b_s", [DEPTH, 8, 128])
    conv_w = din("conv_w", [DEPTH, CW, D])
    conv_b = din("conv_b", [DEPTH, D])
    ln_c_g = din("ln_c_g", [DEPTH, D])
    ln_c_b = din("ln_c_b", [DEPTH, D])
    w_out = din("w_out", [DEPTH, 2 * D, D])
    g_post = din("g_post", [DEPTH, D])
    w_pe = din("w_pe", [DEPTH, DPLE, D])
    w_pg = din("w_pg", [DEPTH, D, D])

    y_p = dout("y_p", [2, SEQ, D])
    y_s = dout("y_s", [2, DEC, D])
    ncp = dout("ncp", [DEPTH, 2, HIST, D])
    ncs = dout("ncs", [DEPTH, 2, HIST, D])
    vs_o = dout("vs_o", [DEPTH, 2, DEC, D])

    win_bf = nc.dram_tensor("win_bf", [DEPTH, D, 6 * D], BF16, kind="Internal").ap()
    wout_bf = nc.dram_tensor("wout_bf", [DEPTH, 2 * D, D], BF16, kind="Internal").ap()
    wpg_bf = nc.dram_tensor("wpg_bf", [DEPTH, D, D], BF16, kind="Internal").ap()
    wpe_bf = nc.dram_tensor("wpe_bf", [DEPTH, DPLE, D], BF16, kind="Internal").ap()
    wp_bf = nc.dram_tensor("wp_bf", [DEPTH, 2, 128, 4096], BF16, kind="Internal").ap()

    with contextlib.ExitStack() as es:
        def sb(name, shape, dt):
            return es.enter_context(nc.sbuf_tensor(name, list(shape), dt))

        ident = sb("ident", [128, 128], F32)
        identb = sb("identb", [128, 128], BF16)
        ones_bf = sb("ones_bf", [128, 128], BF16)
        ones2 = sb("ones2", [128, 128], BF16)
        epst = sb("epst", [128, 1], F32)
        dummy = sb("dummy_ln", [128, 2], F32)
        vecs = sb("vecs", [128, 8, 20], F32)
        cwT = sb("cwT", [128, 8, 128], F32)
        wsT = sb("wsT", [128, DEPTH, 8, 128], BF16)
        Bs = sb("Bs", [128, DEPTH, 8, 128], BF16)
        ring = [sb("ring%d" % i, [128, 4096], BF16) for i in range(RING)]
        hT = sb("hT", [128, 8, ST_T], F32)
        yst = sb("yst", [128, D], F32)
        yst2 = [yst, sb("yst_b", [128, D], F32)]
        xnT = sb("xnT", [128, 8, ST_T], BF16)
        sqt = [sb("sqt%d" % i, [128, ST_T], BF16) for i in range(2)]
        vst = [sb("vst%d" % i, [128, D], F32) for i in range(4)]
        vhat = sb("vhat", [128, 4, D], BF16)
        lnv = sb("lnv", [128, 2, D], F32)
        bnst = [sb("bnst%d" % i, [128, 2, 6], F32) for i in range(4)]
        mv4 = sb("mv4", [128, 4, 2], F32)
        sd4 = sb("sd4", [128, 4], F32)
        rs4 = sb("rs4", [128, 4], F32)
        NTMP = 6
        tmps = [sb("tmp%d" % i, [128, ST_T], F32) for i in range(NTMP)]
        rstdb = sb("rstdb", [128, ST_T], F32)
        meanb = sb("meanb", [128, ST_T], F32)
        ATW = HIST + ST_T + 1
        aT = sb("aT", [128, 8, ATW], BF16)
        tails = sb("tails", [128, DEPTH, 8, HIST], BF16)
        afin = sb("afin", [128, 8, 2, HIST], F32)
        mT = sb("mT", [128, 16, ST_T], BF16)
        cbuf = sb("cbuf", [128, 8, ST_T], F32)
        cbt = [sb("cbt%d" % i, [128, ST_T], BF16) for i in range(2)]
        pst = sb("pst", [128, 4, DPLE], F32)
        pT = sb("pT", [128, 2, ST_T], BF16)
        A4 = [sb("A4_%d" % i, [128, 4, 540], BF16) for i in range(2)]
        m4 = sb("m4", [128, 4], F32)
        irep = sb("irep", [128, 32], F32)
        banks = [es.enter_context(nc.psum_tensor("bank%d" % i, [128, 512], F32)) for i in range(8)]

        P = Prog(nc)
        state = {"rot": 0, "tmp": 0, "chunk": 0}
        out_keys = []

        def rot():
            b = state["rot"] % 6
            state["rot"] += 1
            return b

        def tmp_next():
            i = state["tmp"] % NTMP
            state["tmp"] += 1
            return i

        def bk(b):
            return ("bank", b)

        def ring_view(s, k):
            if k == 8:
                return ring[s][:, :].rearrange("p (k e) -> p k e", k=8)
            if k == "wp":
                return ring[s][:, :].rearrange("p (j m q c) -> p j m q c", j=4, m=8, q=4)
            return ring[s][:, 0:2048].rearrange("p (k e) -> p k e", k=2)

        def load_chunk(src_ap, k, key):
            s = state["chunk"] % RING
            state["chunk"] += 1
            view = ring_view(s, k)
            dst = ring[s][:, :] if k == "wp" else view
            P.add("sp", lambda e, dst=dst, src=src_ap: e.dma_start(out=dst, in_=src),
                  reads=[key], writes=[("ring", s)], dma=("ring", s))
            return s, view

        P.add("dve", lambda e: e.memset(ident[:], 0.0), writes=["ident"])
        P.add("pool", lambda e: e.affine_select(out=ident[:], in_=ident[:], pattern=[[-1, 128]], compare_op=ALU.not_equal,
                                                fill=1.0, base=0, channel_multiplier=1), reads=["ident"], writes=["ident"])
        P.add("dve", lambda e: e.tensor_copy(out=identb[:], in_=ident[:]), reads=["ident"], writes=["identb"])
        P.add("dve", lambda e: e.memset(ones_bf[:], 1.0), writes=["ones_bf"])
        P.add("dve", lambda e: e.memset(ones2[:], 0.0), writes=["ones2"])
        P.add("dve", lambda e: e.memset(ones2[0:2, :], 1.0), writes=["ones2"])
        P.add("dve", lambda e: e.memset(epst[:], EPS), writes=["epst"])

        cast_hist = []
        CAST_DEPTH = 2

        def cast_dma2(out_ap, in_ap, key):
            rd = [cast_hist[-CAST_DEPTH]] if len(cast_hist) >= CAST_DEPTH else []
            P.add("pool", lambda e: e.dma_start(out=out_ap, in_=in_ap), reads=rd, writes=[key], dma=key)
            cast_hist.append(key)

        def emit_casts(l):
            for g in (1, 3, 4, 0, 2, 5):
                cast_dma2(win_bf[l][:, g * D:(g + 1) * D], w_in[l][:, g * D:(g + 1) * D], ("s_win", l, g))
            for hh in range(2):
                cast_dma2(wout_bf[l][hh * D:(hh + 1) * D, :], w_out[l][hh * D:(hh + 1) * D, :], ("s_wout", l, hh))
            cast_dma2(wpg_bf[l], w_pg[l], ("s_wpg", l))
            cast_dma2(wpe_bf[l], w_pe[l], ("s_wpe", l))

        for l_ in range(DEPTH):
            emit_casts(l_)

        for r, src in enumerate([g_pre, conv_b, ln_c_g, ln_c_b, g_post]):
            P.add("act", lambda e, r=r, src=src: e.dma_start(out=vst[0][4 * r:4 * r + 4, :], in_=src),
                  writes=["vst0"], dma="vst0")
        P.add("act", lambda e: e.dma_start(out=vst[1][0:DEPTH * CW, :], in_=conv_w.rearrange("l k c -> (l k) c")),
              writes=["vst1"], dma="vst1")
        for half in range(2):
            b = rot()
            for kk in range(4):
                k = half * 4 + kk
                P.add("pe", lambda e, b=b, k=k, kk=kk: e.transpose(banks[b][:, kk * 20:(kk + 1) * 20], vst[0][0:20, k * 128:(k + 1) * 128], ident[0:20, 0:20]),
                      reads=["vst0", "ident"], writes=[bk(b)])
            P.add("dve", lambda e, b=b, half=half: e.tensor_copy(out=vecs[:, half * 4:(half + 1) * 4, :],
                                                                 in_=banks[b][:, 0:80].rearrange("p (k r) -> p k r", k=4)),
                  reads=[bk(b)], writes=["vecs"])
        NCW = DEPTH * CW
        P.add("dve", lambda e: e.memset(cwT[:, :, :], 0.0), writes=["cwT"])
        for half in range(2):
            b = rot()
            for kk in range(4):
                k = half * 4 + kk
                P.add("pe", lambda e, b=b, k=k, kk=kk: e.transpose(banks[b][:, kk * 128:kk * 128 + NCW], vst[1][0:NCW, k * 128:(k + 1) * 128], ident[0:NCW, 0:NCW]),
                      reads=["vst1", "ident"], writes=[bk(b)])
            P.add("dve", lambda e, b=b, half=half: e.tensor_copy(out=cwT[:, half * 4:(half + 1) * 4, 0:NCW],
                                                                 in_=banks[b][:, :].rearrange("p (k r) -> p k r", k=4)[:, :, 0:NCW]),
                  reads=[bk(b)], writes=["cwT"])

        def build_spatial(sample, dq="pool", layers=None):
            for l in (range(DEPTH) if layers is None else layers):
                stg = vst[l % 2]
                skey = "vst%d" % (l % 2)
                sview = stg[:, :].rearrange("p (h s) -> p h s", h=8)
                if not sample:
                    P.add(dq, lambda e, l=l, sview=sview: e.dma_start(out=sview, in_=w_s[l].rearrange("h t s -> t h s")),
                          writes=[skey], dma=skey)
                else:
                    P.add("dve", lambda e, stg=stg: e.memset(stg[:, :], 0.0), writes=[skey])
                    P.add(dq, lambda e, l=l, sview=sview: e.dma_start(out=sview[0:64, :, 0:64], in_=w_s[l][:, 0:64, 0:64].rearrange("h t s -> t h s")),
                          reads=[skey], writes=[skey], dma=skey)
                    P.add(dq, lambda e, l=l, sview=sview: e.dma_start(out=sview[64:128, :, 64:128], in_=w_s[l][:, 0:64, 0:64].rearrange("h t s -> t h s")),
                          reads=[skey], writes=[skey], dma=skey)
                for half in range(2):
                    b = rot()
                    for hh in range(4):
                        h = half * 4 + hh
                        P.add("pe", lambda e, b=b, h=h, hh=hh, sview=sview: e.transpose(banks[b][:, hh * 128:(hh + 1) * 128], sview[:, h, :], ident[:]),
                              reads=[skey, "ident"], writes=[bk(b)])
                    P.add("act", lambda e, b=b, l=l, half=half: e.activation(out=wsT[:, l, half * 4:(half + 1) * 4, :],
                                                                             in_=banks[b][:, :].rearrange("p (h t) -> p h t", h=4), func=AF.Copy),
                          reads=[bk(b)], writes=[("wsT", l)])
                if not sample:
                    P.add("dve", lambda e, l=l: e.memset(wsT[64:128, l, :, 0:64], 0.0), reads=[("wsT", l)], writes=[("wsT", l)])
                bt = yst
                btv = bt[:, :].rearrange("p (h t) -> p h t", h=8)
                if not sample:
                    P.add(dq, lambda e, l=l, btv=btv: e.dma_start(out=btv, in_=b_s[l].partition_broadcast(128)),
                          writes=["yst"], dma="yst")
                else:
                    for q in range(2):
                        P.add(dq, lambda e, l=l, btv=btv, q=q: e.dma_start(out=btv[:, :, q * 64:(q + 1) * 64], in_=b_s[l][:, 0:64].partition_broadcast(128)),
                              writes=["yst"], dma="yst")
                ti = tmp_next()
                hiv = tmps[ti][:, :].bitcast(BF16)[:, 0:1024]
                P.add("dve", lambda e, hiv=hiv, bt=bt: e.tensor_copy(out=hiv, in_=bt[:, :]), reads=["yst"], writes=[("tmp", ti)])
                tj = tmp_next()
                tk = tmp_next()
                lov = tmps[tj]
                lov2 = tmps[tk]
                P.add("dve", lambda e, hiv=hiv, bt=bt, lov=lov: e.tensor_tensor(out=lov[:, :], in0=bt[:, 0:512], in1=hiv[:, 0:512], op=ALU.subtract),
                      reads=["yst", ("tmp", ti)], writes=[("tmp", tj)])
                P.add("dve", lambda e, hiv=hiv, bt=bt, lov2=lov2: e.tensor_tensor(out=lov2[:, :], in0=bt[:, 512:1024], in1=hiv[:, 512:1024], op=ALU.subtract),
                      reads=["yst", ("tmp", ti)], writes=[("tmp", tk)])
                Bv = Bs[:, l, :, :].rearrange("p h t -> p (h t)")
                P.add("dve", lambda e, Bv=Bv, hiv=hiv: e.tensor_scalar(out=Bv, in0=hiv, scalar1=ident[:, 0:1], scalar2=None, op0=ALU.mult),
                      reads=[("tmp", ti), "ident"], writes=[("Bs", l)])
                P.add("dve", lambda e, Bv=Bv, lov=lov: e.scalar_tensor_tensor(out=Bv[:, 0:512], in0=lov[:, :], scalar=ident[:, 1:2], in1=Bv[:, 0:512], op0=ALU.mult, op1=ALU.add),
                      reads=[("tmp", tj), ("Bs", l), "ident"], writes=[("Bs", l)])
                P.add("dve", lambda e, Bv=Bv, lov2=lov2: e.scalar_tensor_tensor(out=Bv[:, 512:1024], in0=lov2[:, :], scalar=ident[:, 1:2], in1=Bv[:, 512:1024], op0=ALU.mult, op1=ALU.add),
                      reads=[("tmp", tk), ("Bs", l), "ident"], writes=[("Bs", l)])

        build_spatial(False, "act", layers=[0])

        P.add("dve", lambda e: e.memset(aT[:, :, :], 0.0), writes=["aTh"] + [("aT", j) for j in range(8)])
        P.add("dve", lambda e: e.reduce_sum(out=m4[:, :], in_=ident[:, :].rearrange("p (i c) -> p i c", i=4), axis=mybir.AxisListType.X),
              reads=["ident"], writes=["m4"])
        P.add("dve", lambda e: e.reduce_sum(out=irep[:, :], in_=ident[:, :].rearrange("p (i c) -> p c i", i=4), axis=mybir.AxisListType.X),
              reads=["ident"], writes=["irep"])
        cb_keys = [("cbuf", j) for j in range(8)]
        Rflat = cbuf[:, :, :].rearrange("p a b -> p (a b)")
        Rpad = Rflat.rearrange("p (q g k) -> p q g k", q=4, g=32)
        P.add("dve", lambda e: e.memset(Rflat, 0.0), writes=cb_keys)
        cw_flat = cwT[:, :, :].rearrange("p j x -> p (j x)")
        for q in range(4):
            ti = tmp_next()
            selv = tmps[ti][:, 0:128]
            P.add("dve", lambda e, selv=selv, q=q: e.tensor_copy(out=selv.rearrange("p (i c) -> p i c", i=4),
                                                                 in_=ident[:, 32 * q:32 * q + 32].unsqueeze(1).broadcast_to([128, 4, 32])),
                  reads=["ident"], writes=[("tmp", ti)])
            for hh in range(2):
                b_ = rot()
                P.add("pe", lambda e, b_=b_, selv=selv, hh=hh: e.matmul(banks[b_][:, 0:512], lhsT=selv, rhs=cw_flat[:, hh * 512:(hh + 1) * 512], start=True, stop=True),
                      reads=[("tmp", ti), "cwT"], writes=[bk(b_)])
                P.add("act", lambda e, b_=b_, q=q, hh=hh: e.activation(
                    out=Rpad[:, q, hh * 16:(hh + 1) * 16, 0:CW].rearrange("p (j l) k -> p j l k", j=4),
                    in_=banks[b_][:, 0:512].rearrange("p (j x) -> p j x", j=4)[:, :, 0:NCW].rearrange("p j (l k) -> p j l k", l=DEPTH), func=AF.Copy),
                      reads=[bk(b_)], writes=cb_keys)
        Vflat = cwT[:, :, :].rearrange("p j x -> p (j x)")
        V2 = Vflat.rearrange("p (g m) -> p g m", m=8)
        R4 = Rflat.rearrange("p (g m i) -> p g m i", m=8, i=4)
        P.add("dve", lambda e: e.tensor_scalar(out=V2, in0=R4[:, :, :, 0], scalar1=m4[:, 0:1], scalar2=None, op0=ALU.mult),
              reads=cb_keys + ["m4"], writes=["cwT"])
        for ii in range(1, 4):
            P.add("dve", lambda e, ii=ii: e.scalar_tensor_tensor(out=V2, in0=R4[:, :, :, ii], scalar=m4[:, ii:ii + 1], in1=V2, op0=ALU.mult, op1=ALU.add),
                  reads=cb_keys + ["m4", "cwT"], writes=["cwT"])
        V5 = Vflat.rearrange("p (q j l m) -> p q j l m", q=4, j=8, l=DEPTH)
        mT_flat = mT[:, :, :].rearrange("p a b -> p (a b)")
        nstg = 0
        for l in range(DEPTH):
            for jh in range(2):
                hq = nstg % 2
                nstg += 1
                stg = mT_flat[:, hq * 4096:(hq + 1) * 4096]
                stg5 = stg.rearrange("p (j m q c) -> p j m q c", j=4, m=8, q=4)
                keys = [("mT", jj) for jj in range(hq * 8, hq * 8 + 8)]
                for jj in range(4):
                    j = jh * 4 + jj
                    P.add("pool", lambda e, stg5=stg5, jj=jj, j=j, l=l: e.tensor_tensor(
                        out=stg5[:, jj, :, :, :],
                        in0=V5[:, :, j, l, :].rearrange("p q m -> p m q").unsqueeze(3).broadcast_to([128, 8, 4, 32]),
                        in1=irep[:, :].unsqueeze(1).unsqueeze(1).broadcast_to([128, 8, 4, 32]), op=ALU.mult),
                          reads=["cwT", "irep"], writes=keys)
                P.add("act", lambda e, stg=stg, l=l, jh=jh: e.dma_start(out=wp_bf[l, jh], in_=stg), reads=keys, writes=[("s_wp", l)], dma=("s_wp", l))

        def emit_pass_loads(l, si):
            T = si.T
            nt = T // 128
            if si.kind == "p":
                psrc = pp[l, si.b, si.t0:si.t0 + T, :].rearrange("(i p) e -> p i e", p=128)
                P.add("sp", lambda e, psrc=psrc: e.dma_start(out=pst[:, 0:nt, :], in_=psrc), writes=["pst"], dma="pst")
            else:
                P.add("sp", lambda e: e.dma_start(out=pst[:, 0, :], in_=psm[l].rearrange("b t e -> (b t) e")), writes=["pst"], dma="pst")
            P.add("sp", lambda e: e.dma_start(out=lnv[:, 0, :], in_=ln_v_g[l].partition_broadcast(128)), writes=["lnv"], dma="lnv")
            P.add("sp", lambda e: e.dma_start(out=lnv[:, 1, :], in_=ln_v_b[l].partition_broadcast(128)), writes=["lnv"], dma="lnv")

        def emit_x_load(si, i):
            q = i % 2
            if si.kind == "p":
                src = xp[si.b, si.t0 + i * 128:si.t0 + (i + 1) * 128, :]
            else:
                src = xsm.rearrange("b t d -> (b t) d")
            P.add("sp", lambda e, i=i, src=src: e.dma_start(out=vst[i][:, :], in_=src), writes=["vst%d" % i], dma="vst%d" % i)

        def rsqrt_chain(src_ap, dst_ap, T, scale, rkeys, wkey):
            t0 = tmp_next()
            P.add("act", lambda e, t0=t0: e.activation(out=tmps[t0][:, 0:T], in_=src_ap, func=AF.Ln, bias=epst[:, 0:1], scale=scale),
                  reads=list(rkeys) + ["epst"], writes=[("tmp", t0)])
            P.add("act", lambda e, t0=t0: e.activation(out=dst_ap, in_=tmps[t0][:, 0:T], func=AF.Exp, scale=-0.5),
                  reads=[("tmp", t0)], writes=[wkey])

        def preload_lnexp():
            P.add("act", lambda e: e.activation(out=dummy[:, 0:1], in_=epst[:, 0:1], func=AF.Ln), reads=["epst"], writes=["dummy"])

        def emit_pass(l, si, nxt, first_st=False):
            T = si.T
            nt = T // 128
            inv_d = 1.0 / D
            winl = win_bf[l].rearrange("(k p) e -> p k e", p=128)

            def win_chunk(c0):
                return load_chunk(winl[:, :, c0:c0 + 512], 8, ("s_win", l, c0 // D))

            if si.kind == "p":
                if si.seq_start:
                    P.add("dve", lambda e: e.memset(aT[:, :, 0:HIST], 0.0), writes=["aTh"])
                else:
                    P.add("act", lambda e: e.activation(out=aT[:, :, 0:HIST], in_=tails[:, l, :, :], func=AF.Copy),
                          reads=[("tails", l)], writes=["aTh"])
            else:
                for q, (oc, ln, ac) in enumerate(si.segs):
                    P.add("pool", lambda e, q=q: e.dma_start(out=yst[0:HIST, :], in_=sc[l, q]), writes=["yst"], dma="yst")
                    for half in range(2):
                        b = rot()
                        for kk in range(4):
                            k = half * 4 + kk
                            P.add("pe", lambda e, b=b, k=k, kk=kk: e.transpose(banks[b][:, kk * HIST:(kk + 1) * HIST], yst[0:HIST, k * 128:(k + 1) * 128], ident[0:HIST, 0:HIST]),
                                  reads=["yst", "ident"], writes=[bk(b)])
                        P.add("act", lambda e, b=b, half=half, ac=ac: e.activation(out=aT[:, half * 4:(half + 1) * 4, ac:ac + HIST],
                                                                                   in_=banks[b][:, 0:4 * HIST].rearrange("p (k r) -> p k r", k=4), func=AF.Copy),
                              reads=[bk(b)], writes=["aTh"])

            preload_lnexp()
            for k in range(8):
                q = k % 2
                P.add("act", lambda e, k=k, q=q: e.activation(out=sqt[q][:, 0:T], in_=hT[:, k, 0:T], func=AF.Square),
                      reads=[("hT", k)], writes=[("sqt", q)])
                P.add("pe", lambda e, k=k, q=q: e.matmul(banks[6][:, 0:T], lhsT=ones_bf[:], rhs=sqt[q][:, 0:T], start=(k == 0), stop=(k == 7)),
                      reads=[("sqt", q), "ones_bf"], writes=[bk(6)])
            rsqrt_chain(banks[6][:, 0:T], rstdb[:, 0:T], T, inv_d, [bk(6)], "rstdb")
            for kp in range(2):
                b = rot()
                for i in range(nt):
                    P.add("pe", lambda e, b=b, i=i, kp=kp: e.transpose(banks[b][:, i * 128:(i + 1) * 128], pst[:, i, kp * 128:(kp + 1) * 128], ident[:]),
                          reads=["pst", "ident"], writes=[bk(b)])
                P.add("act", lambda e, b=b, kp=kp: e.activation(out=pT[:, kp, 0:T], in_=banks[b][:, 0:T], func=AF.Copy),
                      reads=[bk(b)], writes=[("pT", kp)])

            vch = [win_chunk(D), win_chunk(D + 512)]
            groups = [(i, half) for i in range(nt) for half in range(2)]
            for g0 in range(0, len(groups), 4):
                grp = groups[g0:g0 + 4]
                bs_ = [rot() for _ in grp]
                for k in range(8):
                    if g0 == 0:
                        P.add("dve", lambda e, k=k: e.scalar_tensor_tensor(out=xnT[:, k, 0:T], in0=hT[:, k, 0:T], scalar=vecs[:, k, l:l + 1], in1=rstdb[:, 0:T],
                                                                           op0=ALU.mult, op1=ALU.mult),
                              reads=[("hT", k), "vecs", "rstdb"], writes=[("xnT", k)])
                    for (i, half), b in zip(grp, bs_):
                        s, view = vch[half]
                        P.add("pe", lambda e, b=b, k=k, i=i, view=view: e.matmul(banks[b][:, :], lhsT=xnT[:, k, i * 128:(i + 1) * 128], rhs=view[:, k, :],
                                                                                start=(k == 0), stop=(k == 7)),
                              reads=[("xnT", k), ("ring", s)], writes=[bk(b)])
                for (i, half), b in zip(grp, bs_):
                    P.add("act", lambda e, b=b, i=i, half=half: e.activation(out=vst[i][:, half * 512:(half + 1) * 512], in_=banks[b][:, :], func=AF.Gelu_apprx_tanh),
                          reads=[bk(b)], writes=["vst%d" % i])
                for i in sorted(set(i for i, _ in grp)):
                    vkey = "vst%d" % i
                    for half in range(2):
                        P.add("dve", lambda e, i=i, half=half: e.bn_stats(out=bnst[i][:, half, :], in_=vst[i][:, half * 512:(half + 1) * 512]),
                              reads=[vkey], writes=[("bnst", i)])
                    P.add("dve", lambda e, i=i: e.bn_aggr(out=mv4[:, i, :], in_=bnst[i][:, :, :].rearrange("p a b -> p (a b)")),
                          reads=[("bnst", i)], writes=["mv4"])
            P.add("act", lambda e: e.activation(out=sd4[:, 0:nt], in_=mv4[:, 0:nt, 1], func=AF.Ln, bias=epst[:, 0:1], scale=1.0),
                  reads=["mv4", "epst"], writes=["sd4"])
            P.add("act", lambda e: e.activation(out=rs4[:, 0:nt], in_=sd4[:, 0:nt], func=AF.Exp, scale=-0.5), reads=["sd4"], writes=["rs4"])
            deferred = []
            for i in range(nt):
              def _norm(i=i):
                  vkey = "vst%d" % i
                  P.add("dve", lambda e, i=i: e.tensor_scalar(out=vst[i][:, :], in0=vst[i][:, :], scalar1=mv4[:, i, 0:1], scalar2=rs4[:, i:i + 1],
                                                              op0=ALU.subtract, op1=ALU.mult),
                        reads=[vkey, "mv4", "rs4"], writes=[vkey])
                  P.add("dve", lambda e, i=i: e.tensor_tensor(out=vst[i][:, :], in0=vst[i][:, :], in1=lnv[:, 0, :], op=ALU.mult),
                        reads=[vkey, "lnv"], writes=[vkey])
                  if si.kind == "p":
                      P.add("dve", lambda e, i=i: e.tensor_tensor(out=vhat[:, i, :], in0=vst[i][:, :], in1=lnv[:, 1, :], op=ALU.add),
                            reads=[vkey, "lnv"], writes=[("vhat", i)])
                  else:
                      P.add("dve", lambda e, i=i: e.tensor_tensor(out=vst[i][:, :], in0=vst[i][:, :], in1=lnv[:, 1, :], op=ALU.add),
                            reads=[vkey, "lnv"], writes=[vkey])
                      P.add("pool", lambda e, i=i: e.dma_start(out=vs_o[l].rearrange("b t c -> (b t) c"), in_=vst[i][:, :]),
                            reads=[vkey], dma=("o_vs", i))
                      out_keys.append(("o_vs", i))
                      P.add("act", lambda e, i=i: e.activation(out=vhat[:, i, :], in_=vst[i][:, :], func=AF.Copy),
                            reads=[vkey], writes=[("vhat", i)])
              deferred.append(_norm)

            for j in range(8):
                jj = j % 4
                if jj == 0:
                    lch = win_chunk(3 * D + (j // 4) * 512)
                    gch = win_chunk(4 * D + (j // 4) * 512)
                bl, bg = rot(), rot()
                for (b, (s, view)) in ((bl, lch), (bg, gch)):
                    for k in range(8):
                        P.add("pe", lambda e, b=b, k=k, view=view, jj=jj: e.matmul(banks[b][:, 0:T], lhsT=view[:, k, jj * 128:(jj + 1) * 128], rhs=xnT[:, k, 0:T],
                                                                                 start=(k == 0), stop=(k == 7)),
                              reads=[("xnT", k), ("ring", s)], writes=[bk(b)])
                ta = tmp_next()
                P.add("act", lambda e, bg=bg, ta=ta: e.activation(out=tmps[ta][:, 0:T], in_=banks[bg][:, 0:T], func=AF.Sigmoid),
                      reads=[bk(bg)], writes=[("tmp", ta)])
                for q, (oc, ln, ac) in enumerate(si.segs):
                    P.add("dve", lambda e, bl=bl, ta=ta, j=j, oc=oc, ln=ln, ac=ac: e.tensor_tensor(out=aT[:, j, ac + HIST:ac + HIST + ln], in0=banks[bl][:, oc:oc + ln],
                                                                                                  in1=tmps[ta][:, oc:oc + ln], op=ALU.mult),
                          reads=[bk(bl), ("tmp", ta)], writes=[("aT", j)])
                    if si.final:
                        P.add("dve", lambda e, bl=bl, ta=ta, j=j, oc=oc, ln=ln, q=q: e.tensor_tensor(out=afin[:, j, q, :], in0=banks[bl][:, oc + ln - HIST:oc + ln],
                                                                                                    in1=tmps[ta][:, oc + ln - HIST:oc + ln], op=ALU.mult),
                              reads=[bk(bl), ("tmp", ta)], writes=["afin"])
                if j % 2 == 1 and deferred:
                    deferred.pop(0)()
            while deferred:
                deferred.pop(0)()
            if si.kind == "p" and not si.final:
                P.add("act", lambda e: e.activation(out=tails[:, l, :, :], in_=aT[:, :, T:T + HIST], func=AF.Copy),
                      reads=[("aT", j) for j in range(8)], writes=[("tails", l)])

            for jp in range(4):
                js = (2 * jp, 2 * jp + 1)
                if jp % 2 == 0:
                    uch = win_chunk((jp // 2) * 512)
                    zch = win_chunk(2 * D + (jp // 2) * 512)
                bu = [rot(), rot()]
                bz = [rot(), rot()]
                bm = [rot(), rot()]
                for (bb, (s, view)) in ((bu, uch), (bz, zch)):
                    for idx, j in enumerate(js):
                        jj = j % 4
                        b = bb[idx]
                        for k in range(8):
                            P.add("pe", lambda e, b=b, k=k, view=view, jj=jj: e.matmul(banks[b][:, 0:T], lhsT=view[:, k, jj * 128:(jj + 1) * 128], rhs=xnT[:, k, 0:T],
                                                                                     start=(k == 0), stop=(k == 7)),
                                  reads=[("xnT", k), ("ring", s)], writes=[bk(b)])
                for idx, j in enumerate(js):
                    b = bm[idx]
                    for i in range(nt):
                        P.add("pe", lambda e, b=b, i=i, j=j: e.matmul(banks[b][:, i * 128:(i + 1) * 128], lhsT=vhat[:, i, j * 128:(j + 1) * 128], rhs=wsT[:, l, j, :],
                                                                      start=True, stop=False),
                              reads=[("vhat", i), ("wsT", l)], writes=[bk(b)])
                        P.add("pe", lambda e, b=b, i=i, j=j: e.matmul(banks[b][:, i * 128:(i + 1) * 128], lhsT=ones2[:], rhs=Bs[:, l, j, :],
                                                                      start=False, stop=True),
                              reads=["ones2", ("Bs", l)], writes=[bk(b)])
                tas = [tmp_next(), tmp_next()]
                tbs = [tmp_next(), tmp_next()]
                for idx in range(2):
                    P.add("act", lambda e, b=bu[idx], ta=tas[idx]: e.activation(out=tmps[ta][:, 0:T], in_=banks[b][:, 0:T], func=AF.Gelu_apprx_tanh),
                          reads=[bk(bu[idx])], writes=[("tmp", tas[idx])])
                for idx in range(2):
                    P.add("act", lambda e, b=bz[idx], tb=tbs[idx]: e.activation(out=tmps[tb][:, 0:T], in_=banks[b][:, 0:T], func=AF.Silu),
                          reads=[bk(bz[idx])], writes=[("tmp", tbs[idx])])
                for idx, j in enumerate(js):
                    ta, tb = tas[idx], tbs[idx]
                    P.add("dve", lambda e, ta=ta, tb=tb: e.tensor_tensor(out=tmps[ta][:, 0:T], in0=tmps[ta][:, 0:T], in1=tmps[tb][:, 0:T], op=ALU.mult),
                          reads=[("tmp", ta), ("tmp", tb)], writes=[("tmp", ta)])
                    P.add("dve", lambda e, b=bm[idx], ta=ta, j=j: e.tensor_tensor(out=mT[:, j, 0:T], in0=banks[b][:, 0:T], in1=tmps[ta][:, 0:T], op=ALU.mult),
                          reads=[bk(bm[idx]), ("tmp", ta)], writes=[("mT", j)])

            WA = max(ac + 28 + ln for (oc, ln, ac) in si.segs)
            ones32 = ones_bf[:, 0:32]

            def conv_stats_mm(j):
                dq = j % 2
                for q in range(4):
                    P.add("pe", lambda e, dq=dq, j=j, q=q: e.matmul(banks[6][32 * q:32 * q + 32, 0:T], lhsT=ones32, rhs=cbt[dq][:, 0:T], start=(j == 0), stop=(j == 7),
                                                                   tile_position=(0, 32 * q)),
                          reads=[("cbt", dq), "ones_bf"], writes=[bk(6)])
                for q in range(4):
                    P.add("pe", lambda e, dq=dq, j=j, q=q: e.matmul(banks[7][32 * q:32 * q + 32, 0:T], lhsT=ones32, rhs=sqt[dq][:, 0:T], start=(j == 0), stop=(j == 7),
                                                                   tile_position=(0, 32 * q)),
                          reads=[("sqt", dq), "ones_bf"], writes=[bk(7)])

            def emit_sel(j):
                sl = j % 2
                W0 = min(WA, 512)
                W1 = WA - W0
                bqs = [rot() for _ in range(4)]
                for q in range(4):
                    for i in range(4):
                        P.add("pe", lambda e, b=bqs[q], q=q, i=i, j=j, W0=W0: e.matmul(banks[b][32 * i:32 * i + 32, 0:W0], lhsT=identb[:, 32 * q:32 * q + 32],
                                                                                      rhs=aT[:, j, i:i + W0], start=True, stop=True, tile_position=(0, 32 * i)),
                              reads=["identb", ("aT", j), "aTh"], writes=[bk(bqs[q])])
                bsm = None
                if W1 > 0:
                    bsm = rot()
                    for q in range(4):
                        for i in range(4):
                            P.add("pe", lambda e, b=bsm, q=q, i=i, j=j, W1=W1: e.matmul(banks[b][32 * i:32 * i + 32, W1 * q:W1 * q + W1], lhsT=identb[:, 32 * q:32 * q + 32],
                                                                                       rhs=aT[:, j, 512 + i:512 + i + W1], start=True, stop=True, tile_position=(0, 32 * i)),
                                  reads=["identb", ("aT", j), "aTh"], writes=[bk(bsm)])
                for q in range(4):
                    if q < 2:
                        P.add("act", lambda e, b=bqs[q], q=q, sl=sl, W0=W0: e.activation(out=A4[sl][:, q, 0:W0], in_=banks[b][:, 0:W0], func=AF.Copy),
                              reads=[bk(bqs[q])], writes=[("A4", sl)])
                    else:
                        P.add("dve", lambda e, b=bqs[q], q=q, sl=sl, W0=W0: e.tensor_copy(out=A4[sl][:, q, 0:W0], in_=banks[b][:, 0:W0]),
                              reads=[bk(bqs[q])], writes=[("A4", sl)])
                if W1 > 0:
                    P.add("dve", lambda e, b=bsm, sl=sl, W1=W1: e.tensor_copy(out=A4[sl][:, :, 512:512 + W1], in_=banks[b][:, 0:4 * W1].rearrange("p (q w) -> p q w", q=4)),
                          reads=[bk(bsm)], writes=[("A4", sl)])

            preload_lnexp()
            emit_sel(0)
            for j in range(8):
                dq = j % 2
                sl = j % 2
                jj = j % 4
                if jj == 0:
                    wps, wpv = load_chunk(wp_bf[l, j // 4], "wp", ("s_wp", l))
                if j + 1 < 8:
                    emit_sel(j + 1)
                bc = rot()
                for (oc, ln, ac) in si.segs:
                    for m in range(8):
                        for q in range(4):
                            P.add("pe", lambda e, bc=bc, wpv=wpv, jj=jj, m=m, q=q, oc=oc, ln=ln, ac=ac, sl=sl: e.matmul(
                                banks[bc][32 * q:32 * q + 32, oc:oc + ln], lhsT=wpv[:, jj, m, q, :], rhs=A4[sl][:, q, ac + 4 * m:ac + 4 * m + ln],
                                start=(m == 0), stop=(m == 7), tile_position=(0, 32 * q)),
                                  reads=[("ring", wps), ("A4", sl)], writes=[bk(bc)])
                if j >= 1:
                    conv_stats_mm(j - 1)
                cbias = vecs[:, j, 4 + l:5 + l]
                P.add("act", lambda e, bc=bc, j=j, cbias=cbias: e.activation(out=cbuf[:, j, 0:T], in_=banks[bc][:, 0:T], func=AF.Identity, bias=cbias, scale=1.0),
                      reads=[bk(bc), "vecs"], writes=[("cbuf", j)])
                P.add("act", lambda e, bc=bc, dq=dq, cbias=cbias: e.activation(out=sqt[dq][:, 0:T], in_=banks[bc][:, 0:T], func=AF.Square, bias=cbias, scale=1.0),
                      reads=[bk(bc), "vecs"], writes=[("sqt", dq)])
                P.add("dve", lambda e, dq=dq, j=j: e.tensor_copy(out=cbt[dq][:, 0:T], in_=cbuf[:, j, 0:T]), reads=[("cbuf", j)], writes=[("cbt", dq)])
            conv_stats_mm(7)
            tc, td = tmp_next(), tmp_next()
            P.add("act", lambda e: e.activation(out=meanb[:, 0:T], in_=banks[6][:, 0:T], func=AF.Copy, scale=inv_d), reads=[bk(6)], writes=["meanb"])
            P.add("dve", lambda e, tc=tc: e.tensor_tensor(out=tmps[tc][:, 0:T], in0=meanb[:, 0:T], in1=meanb[:, 0:T], op=ALU.mult),
                  reads=["meanb"], writes=[("tmp", tc)])
            P.add("dve", lambda e, tc=tc, td=td: e.scalar_tensor_tensor(out=tmps[td][:, 0:T], in0=banks[7][:, 0:T], scalar=inv_d, in1=tmps[tc][:, 0:T],
                                                                        op0=ALU.mult, op1=ALU.subtract),
                  reads=[bk(7), ("tmp", tc)], writes=[("tmp", td)])
            rsqrt_chain(tmps[td][:, 0:T], rstdb[:, 0:T], T, 1.0, [("tmp", td)], "rstdb")

            if first_st and l + 1 < DEPTH:
                build_spatial(False, "pool", layers=[l + 1])
            if nxt is not None:
                nl, nsi = nxt
                emit_pass_loads(nl, nsi)
                if nl == 0 and nsi.kind != "s":
                    for i in range(nsi.T // 128):
                        emit_x_load(nsi, i)

            pend = None
            for j in range(8):
                jj = j % 4
                if jj == 0:
                    zbch = win_chunk(5 * D + (j // 4) * 512)
                bz = rot()
                s, view = zbch
                for k in range(8):
                    P.add("pe", lambda e, bz=bz, k=k, view=view, jj=jj: e.matmul(banks[bz][:, 0:T], lhsT=view[:, k, jj * 128:(jj + 1) * 128], rhs=xnT[:, k, 0:T],
                                                                               start=(k == 0), stop=(k == 7)),
                          reads=[("xnT", k), ("ring", s)], writes=[bk(bz)])
                tb, t1, ty = tmp_next(), tmp_next(), tmp_next()
                P.add("act", lambda e, bz=bz, tb=tb: e.activation(out=tmps[tb][:, 0:T], in_=banks[bz][:, 0:T], func=AF.Silu),
                      reads=[bk(bz)], writes=[("tmp", tb)])
                P.add("dve", lambda e, j=j, t1=t1: e.tensor_tensor(out=tmps[t1][:, 0:T], in0=cbuf[:, j, 0:T], in1=meanb[:, 0:T], op=ALU.subtract),
                      reads=[("cbuf", j), "meanb"], writes=[("tmp", t1)])
                P.add("dve", lambda e, t1=t1: e.tensor_tensor(out=tmps[t1][:, 0:T], in0=tmps[t1][:, 0:T], in1=rstdb[:, 0:T], op=ALU.mult),
                      reads=[("tmp", t1), "rstdb"], writes=[("tmp", t1)])
                P.add("act", lambda e, j=j, t1=t1, ty=ty: e.activation(out=tmps[ty][:, 0:T], in_=tmps[t1][:, 0:T], func=AF.Silu,
                                                                       bias=vecs[:, j, 12 + l:13 + l], scale=vecs[:, j, 8 + l:9 + l]),
                      reads=[("tmp", t1), "vecs"], writes=[("tmp", ty)])
                if pend is not None:
                    pj, pty, ptb = pend
                    P.add("dve", lambda e, pj=pj, pty=pty, ptb=ptb: e.tensor_tensor(out=mT[:, 8 + pj, 0:T], in0=tmps[pty][:, 0:T], in1=tmps[ptb][:, 0:T], op=ALU.mult),
                          reads=[("tmp", pty), ("tmp", ptb)], writes=[("mT", 8 + pj)])
                pend = (j, ty, tb)
            pj, pty, ptb = pend
            P.add("dve", lambda e, pj=pj, pty=pty, ptb=ptb: e.tensor_tensor(out=mT[:, 8 + pj, 0:T], in0=tmps[pty][:, 0:T], in1=tmps[ptb][:, 0:T], op=ALU.mult),
                  reads=[("tmp", pty), ("tmp", ptb)], writes=[("mT", 8 + pj)])

            woutl = wout_bf[l].rearrange("(k p) d -> p k d", p=128)
            pending_stats = []
            preload_lnexp()

            def flush_stats():
                for (j, dq) in pending_stats:
                    P.add("pe", lambda e, dq=dq, j=j: e.matmul(banks[6][:, 0:T], lhsT=ones_bf[:], rhs=sqt[dq][:, 0:T], start=(j == 0), stop=(j == 7)),
                          reads=[("sqt", dq), "ones_bf"], writes=[bk(6)])
                del pending_stats[:]

            for dh in range(2):
                woch = [load_chunk(woutl[:, kh * 8:(kh + 1) * 8, dh * 512:(dh + 1) * 512], 8, ("s_wout", l, kh)) for kh in range(2)]
                bo = [rot() for _ in range(4)]
                for k in range(16):
                    s, view = woch[k // 8]
                    for jd in range(4):
                        P.add("pe", lambda e, b=bo[jd], k=k, view=view, jd=jd: e.matmul(banks[b][:, 0:T], lhsT=view[:, k % 8, jd * 128:(jd + 1) * 128], rhs=mT[:, k, 0:T],
                                                                                      start=(k == 0), stop=(k == 15)),
                              reads=[("mT", k), ("ring", s)], writes=[bk(bo[jd])])
                for jd in range(4):
                    j = dh * 4 + jd
                    dq = j % 2
                    if len(pending_stats) >= 2:
                        jj0, dq0 = pending_stats.pop(0)
                        P.add("pe", lambda e, dq=dq0, j=jj0: e.matmul(banks[6][:, 0:T], lhsT=ones_bf[:], rhs=sqt[dq][:, 0:T], start=(j == 0), stop=(j == 7)),
                              reads=[("sqt", dq0), "ones_bf"], writes=[bk(6)])
                    P.add("act", lambda e, b=bo[jd], j=j: e.activation(out=cbuf[:, j, 0:T], in_=banks[b][:, 0:T], func=AF.Copy),
                          reads=[bk(bo[jd])], writes=[("cbuf", j)])
                    P.add("act", lambda e, b=bo[jd], dq=dq: e.activation(out=sqt[dq][:, 0:T], in_=banks[b][:, 0:T], func=AF.Square),
                          reads=[bk(bo[jd])], writes=[("sqt", dq)])
                    pending_stats.append((j, dq))
            flush_stats()
            rsqrt_chain(banks[6][:, 0:T], rstdb[:, 0:T], T, inv_d, [bk(6)], "rstdb")
            for j in range(8):
                t1 = tmp_next()
                P.add("dve", lambda e, j=j, t1=t1: e.tensor_tensor(out=tmps[t1][:, 0:T], in0=cbuf[:, j, 0:T], in1=rstdb[:, 0:T], op=ALU.mult),
                      reads=[("cbuf", j), "rstdb"], writes=[("tmp", t1)])
                P.add("dve", lambda e, j=j, t1=t1: e.scalar_tensor_tensor(out=hT[:, j, 0:T], in0=tmps[t1][:, 0:T], scalar=vecs[:, j, 16 + l:17 + l], in1=hT[:, j, 0:T],
                                                                          op0=ALU.mult, op1=ALU.add),
                      reads=[("tmp", t1), "vecs", ("hT", j)], writes=[("hT", j)])
                P.add("act", lambda e, j=j: e.activation(out=xnT[:, j, 0:T], in_=hT[:, j, 0:T], func=AF.Copy),
                      reads=[("hT", j)], writes=[("xnT", j)])

            wpel = wpe_bf[l].rearrange("(k p) d -> p k d", p=128)
            wpgl = wpg_bf[l].rearrange("(k p) d -> p k d", p=128)
            pes, peview = load_chunk(wpel[:, :, :], 2, ("s_wpe", l))
            for jq in range(2):
                s, view = load_chunk(wpgl[:, :, jq * 512:(jq + 1) * 512], 8, ("s_wpg", l))
                bg = [rot() for _ in range(4)]
                for k in range(8):
                    for jd in range(4):
                        P.add("pe", lambda e, b=bg[jd], k=k, view=view, jd=jd: e.matmul(banks[b][:, 0:T], lhsT=view[:, k, jd * 128:(jd + 1) * 128], rhs=xnT[:, k, 0:T],
                                                                                      start=(k == 0), stop=(k == 7)),
                              reads=[("xnT", k), ("ring", s)], writes=[bk(bg[jd])])
                for jd in range(4):
                    j = jq * 4 + jd
                    bp = rot()
                    for kp in range(2):
                        P.add("pe", lambda e, bp=bp, kp=kp, j=j: e.matmul(banks[bp][:, 0:T], lhsT=peview[:, kp, j * 128:(j + 1) * 128], rhs=pT[:, kp, 0:T],
                                                                          start=(kp == 0), stop=(kp == 1)),
                              reads=[("pT", kp), ("ring", pes)], writes=[bk(bp)])
                    ta = tmp_next()
                    P.add("act", lambda e, b=bg[jd], ta=ta: e.activation(out=tmps[ta][:, 0:T], in_=banks[b][:, 0:T], func=AF.Sigmoid),
                          reads=[bk(bg[jd])], writes=[("tmp", ta)])
                    P.add("dve", lambda e, bp=bp, ta=ta: e.tensor_tensor(out=tmps[ta][:, 0:T], in0=banks[bp][:, 0:T], in1=tmps[ta][:, 0:T], op=ALU.mult),
                          reads=[bk(bp), ("tmp", ta)], writes=[("tmp", ta)])
                    P.add("dve", lambda e, j=j, ta=ta: e.tensor_tensor(out=hT[:, j, 0:T], in0=hT[:, j, 0:T], in1=tmps[ta][:, 0:T], op=ALU.add),
                          reads=[("hT", j), ("tmp", ta)], writes=[("hT", j)])

            if si.final:
                for q, (oc, ln, ac) in enumerate(si.segs):
                    for half in range(2):
                        b = rot()
                        for kk in range(4):
                            j = half * 4 + kk
                            P.add("pe", lambda e, b=b, j=j, kk=kk, q=q: e.transpose(banks[b][0:HIST, kk * 128:(kk + 1) * 128], afin[:, j, q, :], ident[:]),
                                  reads=["afin", "ident"], writes=[bk(b)])
                        P.add("act", lambda e, b=b, half=half: e.activation(out=yst[0:HIST, half * 512:(half + 1) * 512], in_=banks[b][0:HIST, :], func=AF.Copy),
                              reads=[bk(b)], writes=["yst"])
                    if si.kind == "p":
                        dst = ncp[l, si.b]
                    else:
                        dst = ncs[l, q]
                    P.add("pool", lambda e, dst=dst: e.dma_start(out=dst, in_=yst[0:HIST, :]), reads=["yst"], dma="o_cst")
                    out_keys.append("o_cst")

        st_list = []
        for b in range(2):
            for s in range(SEQ // ST_T):
                st_list.append(STInfo("p", b, s * ST_T, ST_T, [(0, ST_T, 0)], s == 0, s == SEQ // ST_T - 1))
        st_list.append(STInfo("s", 0, 0, 128, [(0, 64, 0), (64, 64, HIST + 64)], True, True))

        for idx_st, si in enumerate(st_list):
            T = si.T
            nt = T // 128
            if si.kind == "s":
                build_spatial(True)
            if idx_st == 0:
                emit_pass_loads(0, si)
            for i in range(nt):
                q = i
                xkey = "vst%d" % i
                if idx_st == 0 or si.kind == "s":
                    emit_x_load(si, i)
                for half in range(2):
                    b = rot()
                    for kk in range(4):
                        k = half * 4 + kk
                        P.add("pe", lambda e, b=b, k=k, kk=kk, q=q: e.transpose(banks[b][:, kk * 128:(kk + 1) * 128], vst[q][:, k * 128:(k + 1) * 128], ident[:]),
                              reads=[xkey, "ident"], writes=[bk(b)])
                    eng = "act" if half == 0 else "dve"
                    if eng == "act":
                        P.add("act", lambda e, b=b, half=half, i=i: e.activation(out=hT[:, half * 4:(half + 1) * 4, i * 128:(i + 1) * 128],
                                                                                 in_=banks[b][:, :].rearrange("p (k t) -> p k t", k=4), func=AF.Copy),
                              reads=[bk(b)], writes=[("hT", half * 4 + kk) for kk in range(4)])
                    else:
                        P.add("dve", lambda e, b=b, half=half, i=i: e.tensor_copy(out=hT[:, half * 4:(half + 1) * 4, i * 128:(i + 1) * 128],
                                                                                  in_=banks[b][:, :].rearrange("p (k t) -> p k t", k=4)),
                              reads=[bk(b)], writes=[("hT", half * 4 + kk) for kk in range(4)])
            for l in range(DEPTH):
                if l < DEPTH - 1:
                    nxt = (l + 1, si)
                elif idx_st + 1 < len(st_list):
                    nxt = (0, st_list[idx_st + 1])
                else:
                    nxt = None
                emit_pass(l, si, nxt, first_st=(idx_st == 0))
            for i in range(nt):
                yb = yst2[i % 2]
                ykey = "yst" if i % 2 == 0 else "yst_b"
                for half in range(2):
                    b = rot()
                    for kk in range(4):
                        k = half * 4 + kk
                        P.add("pe", lambda e, b=b, k=k, kk=kk, i=i: e.transpose(banks[b][:, kk * 128:(kk + 1) * 128], hT[:, k, i * 128:(i + 1) * 128], ident[:]),
                              reads=[("hT", k), "ident"], writes=[bk(b)])
                    if half == 0:
                        P.add("act", lambda e, b=b, half=half, yb=yb: e.activation(out=yb[:, half * 512:(half + 1) * 512], in_=banks[b][:, :], func=AF.Copy),
                              reads=[bk(b)], writes=[ykey])
                    else:
                        P.add("dve", lambda e, b=b, half=half, yb=yb: e.tensor_copy(out=yb[:, half * 512:(half + 1) * 512], in_=banks[b][:, :]),
                              reads=[bk(b)], writes=[ykey])
                if si.kind == "p":
                    dst = y_p[si.b, si.t0 + i * 128:si.t0 + (i + 1) * 128, :]
                else:
                    dst = y_s.rearrange("b t d -> (b t) d")
                P.add("pool", lambda e, dst=dst, yb=yb: e.dma_start(out=dst, in_=yb[:, :]), reads=[ykey], dma="o_y%d" % (i % 2))
                out_keys.append("o_y%d" % (i % 2))

        P.emit(final_wait_bufs=sorted(set(out_keys), key=str))
    return nc


_CACHE = {}


def kernel(x_prompt, x_sample, state_conv, p_prompt, p_sample, g_pre, w_in, ln_v_g, ln_v_b, w_s, b_s,
           conv_w, conv_b, ln_c_g, ln_c_b, w_out, g_post, w_pe, w_pg):
    f = lambda a: np.ascontiguousarray(np.asarray(a, dtype=np.float32))
    x_prompt, x_sample, state_conv, p_prompt, p_sample = map(f, (x_prompt, x_sample, state_conv, p_prompt, p_sample))
    shared = {"g_pre": f(g_pre), "w_in": f(w_in), "ln_v_g": f(ln_v_g), "ln_v_b": f(ln_v_b), "w_s": f(w_s), "b_s": f(b_s),
              "conv_w": f(conv_w), "conv_b": f(conv_b), "ln_c_g": f(ln_c_g), "ln_c_b": f(ln_c_b), "w_out": f(w_out),
              "g_post": f(g_post), "w_pe": f(w_pe), "w_pg": f(w_pg)}
    if "nc" not in _CACHE:
        _CACHE["nc"] = build_program()
    nc = _CACHE["nc"]
    in_maps = []
    for c in range(NCORES):
        sl = slice(2 * c, 2 * c + 2)
        m = dict(shared)
        m["xp"] = np.ascontiguousarray(x_prompt[sl])
        m["xsm"] = np.ascontiguousarray(x_sample[sl])
        m["sc"] = np.ascontiguousarray(state_conv[:, sl])
        m["pp"] = np.ascontiguousarray(p_prompt[:, sl])
        m["psm"] = np.ascontiguousarray(p_sample[:, sl])
        in_maps.append(m)
    res = run_bass_kernel_spmd(nc, in_maps, core_ids=list(range(NCORES)))
    rs = res.results
    y_prompt = np.concatenate([np.asarray(r["y_p"], dtype=np.float32) for r in rs], axis=0)
    y_sample = np.concatenate([np.asarray(r["y_s"], dtype=np.float32) for r in rs], axis=0)
    new_conv_prompt = np.concatenate([np.asarray(r["ncp"], dtype=np.float32) for r in rs], axis=1)
    new_conv_sample = np.concatenate([np.asarray(r["ncs"], dtype=np.float32) for r in rs], axis=1)
    new_gmlp_v_sample = np.concatenate([np.asarray(r["vs_o"], dtype=np.float32) for r in rs], axis=1)
    return (y_prompt, y_sample, new_conv_prompt, new_conv_sample, new_gmlp_v_sample)
```

```python
import contextlib
import numpy as np
import concourse.bass as bass
import concourse.mybir as mybir
from concourse.bass_utils import run_bass_kernel_spmd

F32 = mybir.dt.float32
BF16 = mybir.dt.bfloat16
AF = mybir.ActivationFunctionType
ALU = mybir.AluOpType

NCORES = 8
D = 1024
DEPTH = 4
SEQ = 2048
DEC = 64
CW = 31
HIST = CW - 1
DPLE = 256
EPS = 1e-6
ST_T = 512
RING = 5

ENGS = ("pe", "act", "dve", "pool", "sp")


class Op:
    __slots__ = ("eng", "fn", "reads", "writes", "dma", "deps", "marked", "cnt", "idx", "dmaval")

    def __init__(self, eng, fn, reads, writes, dma):
        self.eng, self.fn, self.reads, self.writes, self.dma = eng, fn, reads, writes, dma
        self.deps = []
        self.marked = False
        self.cnt = 0
        self.dmaval = 0


class Prog:
    def __init__(self, nc):
        self.nc = nc
        self.ops = []

    def add(self, eng, fn, reads=(), writes=(), dma=None):
        o = Op(eng, fn, tuple(reads), tuple(writes), dma)
        o.idx = len(self.ops)
        self.ops.append(o)
        return o

    def analyze(self):
        last_w = {}
        readers = {}
        ops = self.ops
        for o in ops:
            deps = set()
            for b in o.reads:
                w = last_w.get(b)
                if w is not None:
                    deps.add(w)
            for b in o.writes:
                w = last_w.get(b)
                if w is not None:
                    deps.add(w)
                for r in readers.get(b, ()):
                    deps.add(r)
            deps.discard(o.idx)
            best = {}
            for d in deps:
                p = ops[d]
                if p.dma is not None:
                    o.deps.append(d)
                    continue
                if o.dma is None and p.eng == o.eng:
                    if p.eng == "pe":
                        continue
                    if not any(b in p.writes for b in o.reads):
                        continue
                if d > best.get(p.eng, -1):
                    best[p.eng] = d
            for d in best.values():
                o.deps.append(d)
                ops[d].marked = True
            for b in o.reads:
                readers.setdefault(b, []).append(o.idx)
            for b in o.writes:
                last_w[b] = o.idx
                readers[b] = []

    def emit(self, final_wait_bufs=()):
        nc = self.nc
        self.analyze()
        cnt = {e: 0 for e in ENGS}
        dmacnt = {}
        for o in self.ops:
            if o.dma is None:
                if o.marked:
                    cnt[o.eng] += 1
                o.cnt = cnt[o.eng]
            else:
                dmacnt[o.dma] = dmacnt.get(o.dma, 0) + 16
                o.dmaval = dmacnt[o.dma]
        ops = self.ops
        with contextlib.ExitStack() as st:
            esem = {e: st.enter_context(nc.semaphore("s_" + e)) for e in ENGS}
            dsem = {b: st.enter_context(nc.semaphore("d_%d" % i)) for i, b in enumerate(dmacnt)}
            block = st.enter_context(nc.Block())

            def mk(ename):
                def body(eng):
                    waited = {}
                    for o in ops:
                        if o.eng != ename:
                            continue
                        need = {}
                        for d in o.deps:
                            p = ops[d]
                            if p.dma is None:
                                s, v = esem[p.eng], p.cnt
                            else:
                                s, v = dsem[p.dma], p.dmaval
                            k = id(s)
                            if v > need.get(k, (None, 0))[1]:
                                need[k] = (s, v)
                        for k, (s, v) in need.items():
                            if waited.get(k, 0) >= v:
                                continue
                            eng.wait_ge(s, v)
                            waited[k] = v
                        ins = o.fn(eng)
                        if o.dma is not None:
                            ins.then_inc(dsem[o.dma], 16)
                        elif o.marked:
                            ins.then_inc(esem[ename], 1)
                    if ename == "sp":
                        for b in final_wait_bufs:
                            eng.wait_ge(dsem[b], dmacnt[b])
                return body

            block.tensor(mk("pe"))
            block.scalar(mk("act"))
            block.vector(mk("dve"))
            block.gpsimd(mk("pool"))
            block.sync(mk("sp"))


class STInfo:
    def __init__(self, kind, b, t0, T, segs, seq_start, final):
        self.kind, self.b, self.t0, self.T, self.segs = kind, b, t0, T, segs
        self.seq_start, self.final = seq_start, final


def build_program():
    nc = bass.Bass("TRN2", target_bir_lowering=False)

    def din(name, shape):
        return nc.dram_tensor(name, list(shape), F32, kind="ExternalInput").ap()

    def dout(name, shape):
        return nc.dram_tensor(name, list(shape), F32, kind="ExternalOutput").ap()

    xp = din("xp", [2, SEQ, D])
    xsm = din("xsm", [2, DEC, D])
    sc = din("sc", [DEPTH, 2, HIST, D])
    pp = din("pp", [DEPTH, 2, SEQ, DPLE])
    psm = din("psm", [DEPTH, 2, DEC, DPLE])
    g_pre = din("g_pre", [DEPTH, D])
    w_in = din("w_in", [DEPTH, D, 6 * D])
    ln_v_g = din("ln_v_g", [DEPTH, D])
    ln_v_b = din("ln_v_b", [DEPTH, D])
    w_s = din("w_s", [DEPTH, 8, 128, 128])
    b_s = din("b_s", [DEPTH, 8, 128])
    conv_w = din("conv_w", [DEPTH, CW, D])
    conv_b = din("conv_b", [DEPTH, D])
    ln_c_g = din("ln_c_g", [DEPTH, D])
    ln_c_b = din("ln_c_b", [DEPTH, D])
    w_out = din("w_out", [DEPTH, 2 * D, D])
    g_post = din("g_post", [DEPTH, D])
    w_pe = din("w_pe", [DEPTH, DPLE, D])
    w_pg = din("w_pg", [DEPTH, D, D])

    y_p = dout("y_p", [2, SEQ, D])
    y_s = dout("y_s", [2, DEC, D])
    ncp = dout("ncp", [DEPTH, 2, HIST, D])
    ncs = dout("ncs", [DEPTH, 2, HIST, D])
    vs_o = dout("vs_o", [DEPTH, 2, DEC, D])

    win_bf = nc.dram_tensor("win_bf", [DEPTH, D, 6 * D], BF16, kind="Internal").ap()
    wout_bf = nc.dram_tensor("wout_bf", [DEPTH, 2 * D, D], BF16, kind="Internal").ap()
    wpg_bf = nc.dram_tensor("wpg_bf", [DEPTH, D, D], BF16, kind="Internal").ap()
    wpe_bf = nc.dram_tensor("wpe_bf", [DEPTH, DPLE, D], BF16, kind="Internal").ap()
    wp_bf = nc.dram_tensor("wp_bf", [DEPTH, 2, 128, 4096], BF16, kind="Internal").ap()

    with contextlib.ExitStack() as es:
        def sb(name, shape, dt):
            return es.enter_context(nc.sbuf_tensor(name, list(shape), dt))

        ident = sb("ident", [128, 128], F32)
        identb = sb("identb", [128, 128], BF16)
        ones_bf = sb("ones_bf", [128, 128], BF16)
        ones2 = sb("ones2", [128, 128], BF16)
        epst = sb("epst", [128, 1], F32)
        dummy = sb("dummy_ln", [128, 2], F32)
        vecs = sb("vecs", [128, 8, 20], F32)
        cwT = sb("cwT", [128, 8, 128], F32)
        wsT = sb("wsT", [128, DEPTH, 8, 128], BF16)
        Bs = sb("Bs", [128, DEPTH, 8, 128], BF16)
        ring = [sb("ring%d" % i, [128, 4096], BF16) for i in range(RING)]
        hT = sb("hT", [128, 8, ST_T], F32)
        yst = sb("yst", [128, D], F32)
        yst2 = [yst, sb("yst_b", [128, D], F32)]
        xnT = sb("xnT", [128, 8, ST_T], BF16)
        sqt = [sb("sqt%d" % i, [128, ST_T], BF16) for i in range(2)]
        vst = [sb("vst%d" % i, [128, D], F32) for i in range(4)]
        vhat = sb("vhat", [128, 4, D], BF16)
        lnv = sb("lnv", [128, 2, D], F32)
        bnst = [sb("bnst%d" % i, [128, 2, 6], F32) for i in range(4)]
        mv4 = sb("mv4", [128, 4, 2], F32)
        sd4 = sb("sd4", [128, 4], F32)
        rs4 = sb("rs4", [128, 4], F32)
        NTMP = 6
        tmps = [sb("tmp%d" % i, [128, ST_T], F32) for i in range(NTMP)]
        rstdb = sb("rstdb", [128, ST_T], F32)
        meanb = sb("meanb", [128, ST_T], F32)
        ATW = HIST + ST_T + 1
        aT = sb("aT", [128, 8, ATW], BF16)
        tails = sb("tails", [128, DEPTH, 8, HIST], BF16)
        afin = sb("afin", [128, 8, 2, HIST], F32)
        mT = sb("mT", [128, 16, ST_T], BF16)
        cbuf = sb("cbuf", [128, 8, ST_T], F32)
        cbt = [sb("cbt%d" % i, [128, ST_T], BF16) for i in range(2)]
        pst = sb("pst", [128, 4, DPLE], F32)
        pT = sb("pT", [128, 2, ST_T], BF16)
        A4 = [sb("A4_%d" % i, [128, 4, 540], BF16) for i in range(2)]
        m4 = sb("m4", [128, 4], F32)
        irep = sb("irep", [128, 32], F32)
        banks = [es.enter_context(nc.psum_tensor("bank%d" % i, [128, 512], F32)) for i in range(8)]

        P = Prog(nc)
        state = {"rot": 0, "tmp": 0, "chunk": 0}
        out_keys = []

        def rot():
            b = state["rot"] % 6
            state["rot"] += 1
            return b

        def tmp_next():
            i = state["tmp"] % NTMP
            state["tmp"] += 1
            return i

        def bk(b):
            return ("bank", b)

        def ring_view(s, k):
            if k == 8:
                return ring[s][:, :].rearrange("p (k e) -> p k e", k=8)
            if k == "wp":
                return ring[s][:, :].rearrange("p (j m q c) -> p j m q c", j=4, m=8, q=4)
            return ring[s][:, 0:2048].rearrange("p (k e) -> p k e", k=2)

        def load_chunk(src_ap, k, key):
            s = state["chunk"] % RING
            state["chunk"] += 1
            view = ring_view(s, k)
            dst = ring[s][:, :] if k == "wp" else view
            P.add("sp", lambda e, dst=dst, src=src_ap: e.dma_start(out=dst, in_=src),
                  reads=[key], writes=[("ring", s)], dma=("ring", s))
            return s, view

        P.add("dve", lambda e: e.memset(ident[:], 0.0), writes=["ident"])
        P.add("pool", lambda e: e.affine_select(out=ident[:], in_=ident[:], pattern=[[-1, 128]], compare_op=ALU.not_equal,
                                                fill=1.0, base=0, channel_multiplier=1), reads=["ident"], writes=["ident"])
        P.add("dve", lambda e: e.tensor_copy(out=identb[:], in_=ident[:]), reads=["ident"], writes=["identb"])
        P.add("dve", lambda e: e.memset(ones_bf[:], 1.0), writes=["ones_bf"])
        P.add("dve", lambda e: e.memset(ones2[:], 0.0), writes=["ones2"])
        P.add("dve", lambda e: e.memset(ones2[0:2, :], 1.0), writes=["ones2"])
        P.add("dve", lambda e: e.memset(epst[:], EPS), writes=["epst"])

        cast_hist = []
        CAST_DEPTH = 2

        def cast_dma2(out_ap, in_ap, key):
            rd = [cast_hist[-CAST_DEPTH]] if len(cast_hist) >= CAST_DEPTH else []
            P.add("pool", lambda e: e.dma_start(out=out_ap, in_=in_ap), reads=rd, writes=[key], dma=key)
            cast_hist.append(key)

        def emit_casts(l):
            for g in (1, 3, 4, 0, 2, 5):
                cast_dma2(win_bf[l][:, g * D:(g + 1) * D], w_in[l][:, g * D:(g + 1) * D], ("s_win", l, g))
            for hh in range(2):
                cast_dma2(wout_bf[l][hh * D:(hh + 1) * D, :], w_out[l][hh * D:(hh + 1) * D, :], ("s_wout", l, hh))
            cast_dma2(wpg_bf[l], w_pg[l], ("s_wpg", l))
            cast_dma2(wpe_bf[l], w_pe[l], ("s_wpe", l))

        cb_keys0 = [("cbuf", j) for j in range(8)]
        mt_keys0 = [("mT", j) for j in range(16)]
        ws_all = cbuf[:, :, :].rearrange("p a b -> p (a b)").rearrange("p (l h s) -> p l h s", l=DEPTH, h=8)
        bs_all = mT[:, :, :].rearrange("p a b -> p (a b)").bitcast(F32)
        P.add("act", lambda e: e.dma_start(out=ws_all.rearrange("p l h s -> p (l h) s"), in_=w_s.rearrange("l h t s -> t (l h) s")),
              writes=cb_keys0, dma="pre_ws")
        P.add("act", lambda e: e.dma_start(out=bs_all.rearrange("p (l h t) -> p l h t", l=DEPTH, h=8), in_=b_s.partition_broadcast(128)),
              writes=mt_keys0, dma="pre_bs")
        for l_ in range(DEPTH):
            emit_casts(l_)

        for r, src in enumerate([g_pre, conv_b, ln_c_g, ln_c_b, g_post]):
            P.add("act", lambda e, r=r, src=src: e.dma_start(out=vst[0][4 * r:4 * r + 4, :], in_=src),
                  writes=["vst0"], dma="vst0")
        P.add("act", lambda e: e.dma_start(out=vst[1][0:DEPTH * CW, :], in_=conv_w.rearrange("l k c -> (l k) c")),
              writes=["vst1"], dma="vst1")
        for half in range(2):
            b = rot()
            for kk in range(4):
                k = half * 4 + kk
                P.add("pe", lambda e, b=b, k=k, kk=kk: e.transpose(banks[b][:, kk * 20:(kk + 1) * 20], vst[0][0:20, k * 128:(k + 1) * 128], ident[0:20, 0:20]),
                      reads=["vst0", "ident"], writes=[bk(b)])
            P.add("dve", lambda e, b=b, half=half: e.tensor_copy(out=vecs[:, half * 4:(half + 1) * 4, :],
                                                                 in_=banks[b][:, 0:80].rearrange("p (k r) -> p k r", k=4)),
                  reads=[bk(b)], writes=["vecs"])
        NCW = DEPTH * CW
        P.add("dve", lambda e: e.memset(cwT[:, :, :], 0.0), writes=["cwT"])
        for half in range(2):
            b = rot()
            for kk in range(4):
                k = half * 4 + kk
                P.add("pe", lambda e, b=b, k=k, kk=kk: e.transpose(banks[b][:, kk * 128:kk * 128 + NCW], vst[1][0:NCW, k * 128:(k + 1) * 128], ident[0:NCW, 0:NCW]),
                      reads=["vst1", "ident"], writes=[bk(b)])
            P.add("dve", lambda e, b=b, half=half: e.tensor_copy(out=cwT[:, half * 4:(half + 1) * 4, 0:NCW],
                                                                 in_=banks[b][:, :].rearrange("p (k r) -> p k r", k=4)[:, :, 0:NCW]),
                  reads=[bk(b)], writes=["cwT"])

        def build_spatial(sample, dq="pool", pre=None):
            for l in range(DEPTH):
                stg = vst[l % 2]
                skey = "vst%d" % (l % 2)
                sview = stg[:, :].rearrange("p (h s) -> p h s", h=8)
                skeys = [skey]
                if pre is not None:
                    sview, skeys = pre["ws"](l)
                elif not sample:
                    P.add(dq, lambda e, l=l, sview=sview: e.dma_start(out=sview, in_=w_s[l].rearrange("h t s -> t h s")),
                          writes=[skey], dma=skey)
                else:
                    P.add("dve", lambda e, stg=stg: e.memset(stg[:, :], 0.0), writes=[skey])
                    P.add(dq, lambda e, l=l, sview=sview: e.dma_start(out=sview[0:64, :, 0:64], in_=w_s[l][:, 0:64, 0:64].rearrange("h t s -> t h s")),
                          reads=[skey], writes=[skey], dma=skey)
                    P.add(dq, lambda e, l=l, sview=sview: e.dma_start(out=sview[64:128, :, 64:128], in_=w_s[l][:, 0:64, 0:64].rearrange("h t s -> t h s")),
                          reads=[skey], writes=[skey], dma=skey)
                for half in range(2):
                    b = rot()
                    for hh in range(4):
                        h = half * 4 + hh
                        P.add("pe", lambda e, b=b, h=h, hh=hh, sview=sview: e.transpose(banks[b][:, hh * 128:(hh + 1) * 128], sview[:, h, :], ident[:]),
                              reads=skeys + ["ident"], writes=[bk(b)])
                    P.add("act", lambda e, b=b, l=l, half=half: e.activation(out=wsT[:, l, half * 4:(half + 1) * 4, :],
                                                                             in_=banks[b][:, :].rearrange("p (h t) -> p h t", h=4), func=AF.Copy),
                          reads=[bk(b)], writes=[("wsT", l)])
                if not sample:
                    P.add("dve", lambda e, l=l: e.memset(wsT[64:128, l, :, 0:64], 0.0), reads=[("wsT", l)], writes=[("wsT", l)])
                bt = yst
                btf = bt[:, :]
                btkeys = ["yst"]
                btv = bt[:, :].rearrange("p (h t) -> p h t", h=8)
                if pre is not None:
                    btf, btkeys = pre["bs"](l)
                elif not sample:
                    P.add(dq, lambda e, l=l, btv=btv: e.dma_start(out=btv, in_=b_s[l].partition_broadcast(128)),
                          writes=["yst"], dma="yst")
                else:
                    for q in range(2):
                        P.add(dq, lambda e, l=l, btv=btv, q=q: e.dma_start(out=btv[:, :, q * 64:(q + 1) * 64], in_=b_s[l][:, 0:64].partition_broadcast(128)),
                              writes=["yst"], dma="yst")
                ti = tmp_next()
                hiv = tmps[ti][:, :].bitcast(BF16)[:, 0:1024]
                P.add("dve", lambda e, hiv=hiv, btf=btf: e.tensor_copy(out=hiv, in_=btf), reads=btkeys, writes=[("tmp", ti)])
                tj = tmp_next()
                tk = tmp_next()
                lov = tmps[tj]
                lov2 = tmps[tk]
                P.add("dve", lambda e, hiv=hiv, btf=btf, lov=lov: e.tensor_tensor(out=lov[:, :], in0=btf[:, 0:512], in1=hiv[:, 0:512], op=ALU.subtract),
                      reads=btkeys + [("tmp", ti)], writes=[("tmp", tj)])
                P.add("dve", lambda e, hiv=hiv, btf=btf, lov2=lov2: e.tensor_tensor(out=lov2[:, :], in0=btf[:, 512:1024], in1=hiv[:, 512:1024], op=ALU.subtract),
                      reads=btkeys + [("tmp", ti)], writes=[("tmp", tk)])
                Bv = Bs[:, l, :, :].rearrange("p h t -> p (h t)")
                P.add("dve", lambda e, Bv=Bv, hiv=hiv: e.tensor_scalar(out=Bv, in0=hiv, scalar1=ident[:, 0:1], scalar2=None, op0=ALU.mult),
                      reads=[("tmp", ti), "ident"], writes=[("Bs", l)])
                P.add("dve", lambda e, Bv=Bv, lov=lov: e.scalar_tensor_tensor(out=Bv[:, 0:512], in0=lov[:, :], scalar=ident[:, 1:2], in1=Bv[:, 0:512], op0=ALU.mult, op1=ALU.add),
                      reads=[("tmp", tj), ("Bs", l), "ident"], writes=[("Bs", l)])
                P.add("dve", lambda e, Bv=Bv, lov2=lov2: e.scalar_tensor_tensor(out=Bv[:, 512:1024], in0=lov2[:, :], scalar=ident[:, 1:2], in1=Bv[:, 512:1024], op0=ALU.mult, op1=ALU.add),
                      reads=[("tmp", tk), ("Bs", l), "ident"], writes=[("Bs", l)])

        build_spatial(False, "act", pre={"ws": lambda l: (ws_all[:, l, :, :], cb_keys0), "bs": lambda l: (bs_all[:, l * 1024:(l + 1) * 1024], mt_keys0)})

        P.add("dve", lambda e: e.memset(aT[:, :, :], 0.0), writes=["aTh"] + [("aT", j) for j in range(8)])
        P.add("dve", lambda e: e.reduce_sum(out=m4[:, :], in_=ident[:, :].rearrange("p (i c) -> p i c", i=4), axis=mybir.AxisListType.X),
              reads=["ident"], writes=["m4"])
        P.add("dve", lambda e: e.reduce_sum(out=irep[:, :], in_=ident[:, :].rearrange("p (i c) -> p c i", i=4), axis=mybir.AxisListType.X),
              reads=["ident"], writes=["irep"])
        cb_keys = [("cbuf", j) for j in range(8)]
        Rflat = cbuf[:, :, :].rearrange("p a b -> p (a b)")
        Rpad = Rflat.rearrange("p (q g k) -> p q g k", q=4, g=32)
        P.add("dve", lambda e: e.memset(Rflat, 0.0), writes=cb_keys)
        cw_flat = cwT[:, :, :].rearrange("p j x -> p (j x)")
        for q in range(4):
            ti = tmp_next()
            selv = tmps[ti][:, 0:128]
            P.add("dve", lambda e, selv=selv, q=q: e.tensor_copy(out=selv.rearrange("p (i c) -> p i c", i=4),
                                                                 in_=ident[:, 32 * q:32 * q + 32].unsqueeze(1).broadcast_to([128, 4, 32])),
                  reads=["ident"], writes=[("tmp", ti)])
            for hh in range(2):
                b_ = rot()
                P.add("pe", lambda e, b_=b_, selv=selv, hh=hh: e.matmul(banks[b_][:, 0:512], lhsT=selv, rhs=cw_flat[:, hh * 512:(hh + 1) * 512], start=True, stop=True),
                      reads=[("tmp", ti), "cwT"], writes=[bk(b_)])
                P.add("act", lambda e, b_=b_, q=q, hh=hh: e.activation(
                    out=Rpad[:, q, hh * 16:(hh + 1) * 16, 0:CW].rearrange("p (j l) k -> p j l k", j=4),
                    in_=banks[b_][:, 0:512].rearrange("p (j x) -> p j x", j=4)[:, :, 0:NCW].rearrange("p j (l k) -> p j l k", l=DEPTH), func=AF.Copy),
                      reads=[bk(b_)], writes=cb_keys)
        Vflat = cwT[:, :, :].rearrange("p j x -> p (j x)")
        V2 = Vflat.rearrange("p (g m) -> p g m", m=8)
        R4 = Rflat.rearrange("p (g m i) -> p g m i", m=8, i=4)
        P.add("dve", lambda e: e.tensor_scalar(out=V2, in0=R4[:, :, :, 0], scalar1=m4[:, 0:1], scalar2=None, op0=ALU.mult),
              reads=cb_keys + ["m4"], writes=["cwT"])
        for ii in range(1, 4):
            P.add("dve", lambda e, ii=ii: e.scalar_tensor_tensor(out=V2, in0=R4[:, :, :, ii], scalar=m4[:, ii:ii + 1], in1=V2, op0=ALU.mult, op1=ALU.add),
                  reads=cb_keys + ["m4", "cwT"], writes=["cwT"])
        V5 = Vflat.rearrange("p (q j l m) -> p q j l m", q=4, j=8, l=DEPTH)
        mT_flat = mT[:, :, :].rearrange("p a b -> p (a b)")
        nstg = 0
        for l in range(DEPTH):
            for jh in range(2):
                hq = nstg % 2
                nstg += 1
                stg = mT_flat[:, hq * 4096:(hq + 1) * 4096]
                stg5 = stg.rearrange("p (j m q c) -> p j m q c", j=4, m=8, q=4)
                keys = [("mT", jj) for jj in range(hq * 8, hq * 8 + 8)]
                for jj in range(4):
                    j = jh * 4 + jj
                    P.add("pool", lambda e, stg5=stg5, jj=jj, j=j, l=l: e.tensor_tensor(
                        out=stg5[:, jj, :, :, :],
                        in0=V5[:, :, j, l, :].rearrange("p q m -> p m q").unsqueeze(3).broadcast_to([128, 8, 4, 32]),
                        in1=irep[:, :].unsqueeze(1).unsqueeze(1).broadcast_to([128, 8, 4, 32]), op=ALU.mult),
                          reads=["cwT", "irep"], writes=keys)
                P.add("act", lambda e, stg=stg, l=l, jh=jh: e.dma_start(out=wp_bf[l, jh], in_=stg), reads=keys, writes=[("s_wp", l)], dma=("s_wp", l))

        def emit_pass_loads(l, si):
            T = si.T
            nt = T // 128
            if si.kind == "p":
                psrc = pp[l, si.b, si.t0:si.t0 + T, :].rearrange("(i p) e -> p i e", p=128)
                P.add("sp", lambda e, psrc=psrc: e.dma_start(out=pst[:, 0:nt, :], in_=psrc), writes=["pst"], dma="pst")
            else:
                P.add("sp", lambda e: e.dma_start(out=pst[:, 0, :], in_=psm[l].rearrange("b t e -> (b t) e")), writes=["pst"], dma="pst")
            P.add("sp", lambda e: e.dma_start(out=lnv[:, 0, :], in_=ln_v_g[l].partition_broadcast(128)), writes=["lnv"], dma="lnv")
            P.add("sp", lambda e: e.dma_start(out=lnv[:, 1, :], in_=ln_v_b[l].partition_broadcast(128)), writes=["lnv"], dma="lnv")

        def emit_x_load(si, i):
            q = i % 2
            if si.kind == "p":
                src = xp[si.b, si.t0 + i * 128:si.t0 + (i + 1) * 128, :]
            else:
                src = xsm.rearrange("b t d -> (b t) d")
            P.add("sp", lambda e, i=i, src=src: e.dma_start(out=vst[i][:, :], in_=src), writes=["vst%d" % i], dma="vst%d" % i)

        def rsqrt_chain(src_ap, dst_ap, T, scale, rkeys, wkey):
            t0 = tmp_next()
            P.add("act", lambda e, t0=t0: e.activation(out=tmps[t0][:, 0:T], in_=src_ap, func=AF.Ln, bias=epst[:, 0:1], scale=scale),
                  reads=list(rkeys) + ["epst"], writes=[("tmp", t0)])
            P.add("act", lambda e, t0=t0: e.activation(out=dst_ap, in_=tmps[t0][:, 0:T], func=AF.Exp, scale=-0.5),
                  reads=[("tmp", t0)], writes=[wkey])

        def preload_lnexp():
            P.add("act", lambda e: e.activation(out=dummy[:, 0:1], in_=epst[:, 0:1], func=AF.Ln), reads=["epst"], writes=["dummy"])

        def emit_pass(l, si, nxt):
            T = si.T
            nt = T // 128
            inv_d = 1.0 / D
            winl = win_bf[l].rearrange("(k p) e -> p k e", p=128)

            def win_chunk(c0):
                return load_chunk(winl[:, :, c0:c0 + 512], 8, ("s_win", l, c0 // D))

            if si.kind == "p":
                if si.seq_start:
                    P.add("dve", lambda e: e.memset(aT[:, :, 0:HIST], 0.0), writes=["aTh"])
                else:
                    P.add("act", lambda e: e.activation(out=aT[:, :, 0:HIST], in_=tails[:, l, :, :], func=AF.Copy),
                          reads=[("tails", l)], writes=["aTh"])
            else:
                for q, (oc, ln, ac) in enumerate(si.segs):
                    P.add("pool", lambda e, q=q: e.dma_start(out=yst[0:HIST, :], in_=sc[l, q]), writes=["yst"], dma="yst")
                    for half in range(2):
                        b = rot()
                        for kk in range(4):
                            k = half * 4 + kk
                            P.add("pe", lambda e, b=b, k=k, kk=kk: e.transpose(banks[b][:, kk * HIST:(kk + 1) * HIST], yst[0:HIST, k * 128:(k + 1) * 128], ident[0:HIST, 0:HIST]),
                                  reads=["yst", "ident"], writes=[bk(b)])
                        P.add("act", lambda e, b=b, half=half, ac=ac: e.activation(out=aT[:, half * 4:(half + 1) * 4, ac:ac + HIST],
                                                                                   in_=banks[b][:, 0:4 * HIST].rearrange("p (k r) -> p k r", k=4), func=AF.Copy),
                              reads=[bk(b)], writes=["aTh"])

            preload_lnexp()
            for k in range(8):
                q = k % 2
                P.add("act", lambda e, k=k, q=q: e.activation(out=sqt[q][:, 0:T], in_=hT[:, k, 0:T], func=AF.Square),
                      reads=[("hT", k)], writes=[("sqt", q)])
                P.add("pe", lambda e, k=k, q=q: e.matmul(banks[6][:, 0:T], lhsT=ones_bf[:], rhs=sqt[q][:, 0:T], start=(k == 0), stop=(k == 7)),
                      reads=[("sqt", q), "ones_bf"], writes=[bk(6)])
            rsqrt_chain(banks[6][:, 0:T], rstdb[:, 0:T], T, inv_d, [bk(6)], "rstdb")
            for kp in range(2):
                b = rot()
                for i in range(nt):
                    P.add("pe", lambda e, b=b, i=i, kp=kp: e.transpose(banks[b][:, i * 128:(i + 1) * 128], pst[:, i, kp * 128:(kp + 1) * 128], ident[:]),
                          reads=["pst", "ident"], writes=[bk(b)])
                P.add("act", lambda e, b=b, kp=kp: e.activation(out=pT[:, kp, 0:T], in_=banks[b][:, 0:T], func=AF.Copy),
                      reads=[bk(b)], writes=[("pT", kp)])

            vch = [win_chunk(D), win_chunk(D + 512)]
            groups = [(i, half) for i in range(nt) for half in range(2)]
            for g0 in range(0, len(groups), 4):
                grp = groups[g0:g0 + 4]
                bs_ = [rot() for _ in grp]
                for k in range(8):
                    if g0 == 0:
                        P.add("dve", lambda e, k=k: e.scalar_tensor_tensor(out=xnT[:, k, 0:T], in0=hT[:, k, 0:T], scalar=vecs[:, k, l:l + 1], in1=rstdb[:, 0:T],
                                                                           op0=ALU.mult, op1=ALU.mult),
                              reads=[("hT", k), "vecs", "rstdb"], writes=[("xnT", k)])
                    for (i, half), b in zip(grp, bs_):
                        s, view = vch[half]
                        P.add("pe", lambda e, b=b, k=k, i=i, view=view: e.matmul(banks[b][:, :], lhsT=xnT[:, k, i * 128:(i + 1) * 128], rhs=view[:, k, :],
                                                                                start=(k == 0), stop=(k == 7)),
                              reads=[("xnT", k), ("ring", s)], writes=[bk(b)])
                for (i, half), b in zip(grp, bs_):
                    P.add("act", lambda e, b=b, i=i, half=half: e.activation(out=vst[i][:, half * 512:(half + 1) * 512], in_=banks[b][:, :], func=AF.Gelu_apprx_tanh),
                          reads=[bk(b)], writes=["vst%d" % i])
                for i in sorted(set(i for i, _ in grp)):
                    vkey = "vst%d" % i
                    for half in range(2):
                        P.add("dve", lambda e, i=i, half=half: e.bn_stats(out=bnst[i][:, half, :], in_=vst[i][:, half * 512:(half + 1) * 512]),
                              reads=[vkey], writes=[("bnst", i)])
                    P.add("dve", lambda e, i=i: e.bn_aggr(out=mv4[:, i, :], in_=bnst[i][:, :, :].rearrange("p a b -> p (a b)")),
                          reads=[("bnst", i)], writes=["mv4"])
            P.add("act", lambda e: e.activation(out=sd4[:, 0:nt], in_=mv4[:, 0:nt, 1], func=AF.Ln, bias=epst[:, 0:1], scale=1.0),
                  reads=["mv4", "epst"], writes=["sd4"])
            P.add("act", lambda e: e.activation(out=rs4[:, 0:nt], in_=sd4[:, 0:nt], func=AF.Exp, scale=-0.5), reads=["sd4"], writes=["rs4"])
            deferred = []
            for i in range(nt):
              def _norm(i=i):
                  vkey = "vst%d" % i
                  P.add("dve", lambda e, i=i: e.tensor_scalar(out=vst[i][:, :], in0=vst[i][:, :], scalar1=mv4[:, i, 0:1], scalar2=rs4[:, i:i + 1],
                                                              op0=ALU.subtract, op1=ALU.mult),
                        reads=[vkey, "mv4", "rs4"], writes=[vkey])
                  P.add("dve", lambda e, i=i: e.tensor_tensor(out=vst[i][:, :], in0=vst[i][:, :], in1=lnv[:, 0, :], op=ALU.mult),
                        reads=[vkey, "lnv"], writes=[vkey])
                  if si.kind == "p":
                      P.add("dve", lambda e, i=i: e.tensor_tensor(out=vhat[:, i, :], in0=vst[i][:, :], in1=lnv[:, 1, :], op=ALU.add),
                            reads=[vkey, "lnv"], writes=[("vhat", i)])
                  else:
                      P.add("dve", lambda e, i=i: e.tensor_tensor(out=vst[i][:, :], in0=vst[i][:, :], in1=lnv[:, 1, :], op=ALU.add),
                            reads=[vkey, "lnv"], writes=[vkey])
                      P.add("pool", lambda e, i=i: e.dma_start(out=vs_o[l].rearrange("b t c -> (b t) c"), in_=vst[i][:, :]),
                            reads=[vkey], dma=("o_vs", i))
                      out_keys.append(("o_vs", i))
                      P.add("act", lambda e, i=i: e.activation(out=vhat[:, i, :], in_=vst[i][:, :], func=AF.Copy),
                            reads=[vkey], writes=[("vhat", i)])
              deferred.append(_norm)

            for j in range(8):
                jj = j % 4
                if jj == 0:
                    lch = win_chunk(3 * D + (j // 4) * 512)
                    gch = win_chunk(4 * D + (j // 4) * 512)
                bl, bg = rot(), rot()
                for (b, (s, view)) in ((bl, lch), (bg, gch)):
                    for k in range(8):
                        P.add("pe", lambda e, b=b, k=k, view=view, jj=jj: e.matmul(banks[b][:, 0:T], lhsT=view[:, k, jj * 128:(jj + 1) * 128], rhs=xnT[:, k, 0:T],
                                                                                 start=(k == 0), stop=(k == 7)),
                              reads=[("xnT", k), ("ring", s)], writes=[bk(b)])
                ta = tmp_next()
                P.add("act", lambda e, bg=bg, ta=ta: e.activation(out=tmps[ta][:, 0:T], in_=banks[bg][:, 0:T], func=AF.Sigmoid),
                      reads=[bk(bg)], writes=[("tmp", ta)])
                for q, (oc, ln, ac) in enumerate(si.segs):
                    P.add("dve", lambda e, bl=bl, ta=ta, j=j, oc=oc, ln=ln, ac=ac: e.tensor_tensor(out=aT[:, j, ac + HIST:ac + HIST + ln], in0=banks[bl][:, oc:oc + ln],
                                                                                                  in1=tmps[ta][:, oc:oc + ln], op=ALU.mult),
                          reads=[bk(bl), ("tmp", ta)], writes=[("aT", j)])
                    if si.final:
                        P.add("dve", lambda e, bl=bl, ta=ta, j=j, oc=oc, ln=ln, q=q: e.tensor_tensor(out=afin[:, j, q, :], in0=banks[bl][:, oc + ln - HIST:oc + ln],
                                                                                                    in1=tmps[ta][:, oc + ln - HIST:oc + ln], op=ALU.mult),
                              reads=[bk(bl), ("tmp", ta)], writes=["afin"])
                if j % 2 == 1 and deferred:
                    deferred.pop(0)()
            while deferred:
                deferred.pop(0)()
            if si.kind == "p" and not si.final:
                P.add("act", lambda e: e.activation(out=tails[:, l, :, :], in_=aT[:, :, T:T + HIST], func=AF.Copy),
                      reads=[("aT", j) for j in range(8)], writes=[("tails", l)])

            for jp in range(4):
                js = (2 * jp, 2 * jp + 1)
                if jp % 2 == 0:
                    uch = win_chunk((jp // 2) * 512)
                    zch = win_chunk(2 * D + (jp // 2) * 512)
                bu = [rot(), rot()]
                bz = [rot(), rot()]
                bm = [rot(), rot()]
                for (bb, (s, view)) in ((bu, uch), (bz, zch)):
                    for idx, j in enumerate(js):
                        jj = j % 4
                        b = bb[idx]
                        for k in range(8):
                            P.add("pe", lambda e, b=b, k=k, view=view, jj=jj: e.matmul(banks[b][:, 0:T], lhsT=view[:, k, jj * 128:(jj + 1) * 128], rhs=xnT[:, k, 0:T],
                                                                                     start=(k == 0), stop=(k == 7)),
                                  reads=[("xnT", k), ("ring", s)], writes=[bk(b)])
                for idx, j in enumerate(js):
                    b = bm[idx]
                    for i in range(nt):
                        P.add("pe", lambda e, b=b, i=i, j=j: e.matmul(banks[b][:, i * 128:(i + 1) * 128], lhsT=vhat[:, i, j * 128:(j + 1) * 128], rhs=wsT[:, l, j, :],
                                                                      start=True, stop=False),
                              reads=[("vhat", i), ("wsT", l)], writes=[bk(b)])
                        P.add("pe", lambda e, b=b, i=i, j=j: e.matmul(banks[b][:, i * 128:(i + 1) * 128], lhsT=ones2[:], rhs=Bs[:, l, j, :],
                                                                      start=False, stop=True),
                              reads=["ones2", ("Bs", l)], writes=[bk(b)])
                tas = [tmp_next(), tmp_next()]
                tbs = [tmp_next(), tmp_next()]
                for idx in range(2):
                    P.add("act", lambda e, b=bu[idx], ta=tas[idx]: e.activation(out=tmps[ta][:, 0:T], in_=banks[b][:, 0:T], func=AF.Gelu_apprx_tanh),
                          reads=[bk(bu[idx])], writes=[("tmp", tas[idx])])
                for idx in range(2):
                    P.add("act", lambda e, b=bz[idx], tb=tbs[idx]: e.activation(out=tmps[tb][:, 0:T], in_=banks[b][:, 0:T], func=AF.Silu),
                          reads=[bk(bz[idx])], writes=[("tmp", tbs[idx])])
                for idx, j in enumerate(js):
                    ta, tb = tas[idx], tbs[idx]
                    P.add("dve", lambda e, ta=ta, tb=tb: e.tensor_tensor(out=tmps[ta][:, 0:T], in0=tmps[ta][:, 0:T], in1=tmps[tb][:, 0:T], op=ALU.mult),
                          reads=[("tmp", ta), ("tmp", tb)], writes=[("tmp", ta)])
                    P.add("dve", lambda e, b=bm[idx], ta=ta, j=j: e.tensor_tensor(out=mT[:, j, 0:T], in0=banks[b][:, 0:T], in1=tmps[ta][:, 0:T], op=ALU.mult),
                          reads=[bk(bm[idx]), ("tmp", ta)], writes=[("mT", j)])

            WA = max(ac + 28 + ln for (oc, ln, ac) in si.segs)
            ones32 = ones_bf[:, 0:32]

            def conv_stats_mm(j):
                dq = j % 2
                for q in range(4):
                    P.add("pe", lambda e, dq=dq, j=j, q=q: e.matmul(banks[6][32 * q:32 * q + 32, 0:T], lhsT=ones32, rhs=cbt[dq][:, 0:T], start=(j == 0), stop=(j == 7),
                                                                   tile_position=(0, 32 * q)),
                          reads=[("cbt", dq), "ones_bf"], writes=[bk(6)])
                for q in range(4):
                    P.add("pe", lambda e, dq=dq, j=j, q=q: e.matmul(banks[7][32 * q:32 * q + 32, 0:T], lhsT=ones32, rhs=sqt[dq][:, 0:T], start=(j == 0), stop=(j == 7),
                                                                   tile_position=(0, 32 * q)),
                          reads=[("sqt", dq), "ones_bf"], writes=[bk(7)])

            def emit_sel(j):
                sl = j % 2
                W0 = min(WA, 512)
                W1 = WA - W0
                bqs = [rot() for _ in range(4)]
                for q in range(4):
                    for i in range(4):
                        P.add("pe", lambda e, b=bqs[q], q=q, i=i, j=j, W0=W0: e.matmul(banks[b][32 * i:32 * i + 32, 0:W0], lhsT=identb[:, 32 * q:32 * q + 32],
                                                                                      rhs=aT[:, j, i:i + W0], start=True, stop=True, tile_position=(0, 32 * i)),
                              reads=["identb", ("aT", j), "aTh"], writes=[bk(bqs[q])])
                bsm = None
                if W1 > 0:
                    bsm = rot()
                    for q in range(4):
                        for i in range(4):
                            P.add("pe", lambda e, b=bsm, q=q, i=i, j=j, W1=W1: e.matmul(banks[b][32 * i:32 * i + 32, W1 * q:W1 * q + W1], lhsT=identb[:, 32 * q:32 * q + 32],
                                                                                       rhs=aT[:, j, 512 + i:512 + i + W1], start=True, stop=True, tile_position=(0, 32 * i)),
                                  reads=["identb", ("aT", j), "aTh"], writes=[bk(bsm)])
                for q in range(4):
                    if q < 2:
                        P.add("act", lambda e, b=bqs[q], q=q, sl=sl, W0=W0: e.activation(out=A4[sl][:, q, 0:W0], in_=banks[b][:, 0:W0], func=AF.Copy),
                              reads=[bk(bqs[q])], writes=[("A4", sl)])
                    else:
                        P.add("dve", lambda e, b=bqs[q], q=q, sl=sl, W0=W0: e.tensor_copy(out=A4[sl][:, q, 0:W0], in_=banks[b][:, 0:W0]),
                              reads=[bk(bqs[q])], writes=[("A4", sl)])
                if W1 > 0:
                    P.add("dve", lambda e, b=bsm, sl=sl, W1=W1: e.tensor_copy(out=A4[sl][:, :, 512:512 + W1], in_=banks[b][:, 0:4 * W1].rearrange("p (q w) -> p q w", q=4)),
                          reads=[bk(bsm)], writes=[("A4", sl)])

            preload_lnexp()
            emit_sel(0)
            for j in range(8):
                dq = j % 2
                sl = j % 2
                jj = j % 4
                if jj == 0:
                    wps, wpv = load_chunk(wp_bf[l, j // 4], "wp", ("s_wp", l))
                if j + 1 < 8:
                    emit_sel(j + 1)
                bc = rot()
                for (oc, ln, ac) in si.segs:
                    for m in range(8):
                        for q in range(4):
                            P.add("pe", lambda e, bc=bc, wpv=wpv, jj=jj, m=m, q=q, oc=oc, ln=ln, ac=ac, sl=sl: e.matmul(
                                banks[bc][32 * q:32 * q + 32, oc:oc + ln], lhsT=wpv[:, jj, m, q, :], rhs=A4[sl][:, q, ac + 4 * m:ac + 4 * m + ln],
                                start=(m == 0), stop=(m == 7), tile_position=(0, 32 * q)),
                                  reads=[("ring", wps), ("A4", sl)], writes=[bk(bc)])
                if j >= 1:
                    conv_stats_mm(j - 1)
                cbias = vecs[:, j, 4 + l:5 + l]
                P.add("act", lambda e, bc=bc, j=j, cbias=cbias: e.activation(out=cbuf[:, j, 0:T], in_=banks[bc][:, 0:T], func=AF.Identity, bias=cbias, scale=1.0),
                      reads=[bk(bc), "vecs"], writes=[("cbuf", j)])
                P.add("act", lambda e, bc=bc, dq=dq, cbias=cbias: e.activation(out=sqt[dq][:, 0:T], in_=banks[bc][:, 0:T], func=AF.Square, bias=cbias, scale=1.0),
                      reads=[bk(bc), "vecs"], writes=[("sqt", dq)])
                P.add("dve", lambda e, dq=dq, j=j: e.tensor_copy(out=cbt[dq][:, 0:T], in_=cbuf[:, j, 0:T]), reads=[("cbuf", j)], writes=[("cbt", dq)])
            conv_stats_mm(7)
            tc, td = tmp_next(), tmp_next()
            P.add("act", lambda e: e.activation(out=meanb[:, 0:T], in_=banks[6][:, 0:T], func=AF.Copy, scale=inv_d), reads=[bk(6)], writes=["meanb"])
            P.add("dve", lambda e, tc=tc: e.tensor_tensor(out=tmps[tc][:, 0:T], in0=meanb[:, 0:T], in1=meanb[:, 0:T], op=ALU.mult),
                  reads=["meanb"], writes=[("tmp", tc)])
            P.add("dve", lambda e, tc=tc, td=td: e.scalar_tensor_tensor(out=tmps[td][:, 0:T], in0=banks[7][:, 0:T], scalar=inv_d, in1=tmps[tc][:, 0:T],
                                                                        op0=ALU.mult, op1=ALU.subtract),
                  reads=[bk(7), ("tmp", tc)], writes=[("tmp", td)])
            rsqrt_chain(tmps[td][:, 0:T], rstdb[:, 0:T], T, 1.0, [("tmp", td)], "rstdb")

            if nxt is not None:
                nl, nsi = nxt
                emit_pass_loads(nl, nsi)
                if nl == 0 and nsi.kind != "s":
                    for i in range(nsi.T // 128):
                        emit_x_load(nsi, i)

            pend = None
            for j in range(8):
                jj = j % 4
                if jj == 0:
                    zbch = win_chunk(5 * D + (j // 4) * 512)
                bz = rot()
                s, view = zbch
                for k in range(8):
                    P.add("pe", lambda e, bz=bz, k=k, view=view, jj=jj: e.matmul(banks[bz][:, 0:T], lhsT=view[:, k, jj * 128:(jj + 1) * 128], rhs=xnT[:, k, 0:T],
                                                                               start=(k == 0), stop=(k == 7)),
                          reads=[("xnT", k), ("ring", s)], writes=[bk(bz)])
                tb, t1, ty = tmp_next(), tmp_next(), tmp_next()
                P.add("act", lambda e, bz=bz, tb=tb: e.activation(out=tmps[tb][:, 0:T], in_=banks[bz][:, 0:T], func=AF.Silu),
                      reads=[bk(bz)], writes=[("tmp", tb)])
                P.add("dve", lambda e, j=j, t1=t1: e.tensor_tensor(out=tmps[t1][:, 0:T], in0=cbuf[:, j, 0:T], in1=meanb[:, 0:T], op=ALU.subtract),
                      reads=[("cbuf", j), "meanb"], writes=[("tmp", t1)])
                P.add("dve", lambda e, t1=t1: e.tensor_tensor(out=tmps[t1][:, 0:T], in0=tmps[t1][:, 0:T], in1=rstdb[:, 0:T], op=ALU.mult),
                      reads=[("tmp", t1), "rstdb"], writes=[("tmp", t1)])
                P.add("act", lambda e, j=j, t1=t1, ty=ty: e.activation(out=tmps[ty][:, 0:T], in_=tmps[t1][:, 0:T], func=AF.Silu,
                                                                       bias=vecs[:, j, 12 + l:13 + l], scale=vecs[:, j, 8 + l:9 + l]),
                      reads=[("tmp", t1), "vecs"], writes=[("tmp", ty)])
                if pend is not None:
                    pj, pty, ptb = pend
                    P.add("dve", lambda e, pj=pj, pty=pty, ptb=ptb: e.tensor_tensor(out=mT[:, 8 + pj, 0:T], in0=tmps[pty][:, 0:T], in1=tmps[ptb][:, 0:T], op=ALU.mult),
                          reads=[("tmp", pty), ("tmp", ptb)], writes=[("mT", 8 + pj)])
                pend = (j, ty, tb)
            pj, pty, ptb = pend
            P.add("dve", lambda e, pj=pj, pty=pty, ptb=ptb: e.tensor_tensor(out=mT[:, 8 + pj, 0:T], in0=tmps[pty][:, 0:T], in1=tmps[ptb][:, 0:T], op=ALU.mult),
                  reads=[("tmp", pty), ("tmp", ptb)], writes=[("mT", 8 + pj)])

            woutl = wout_bf[l].rearrange("(k p) d -> p k d", p=128)
            pending_stats = []
            preload_lnexp()

            def flush_stats():
                for (j, dq) in pending_stats:
                    P.add("pe", lambda e, dq=dq, j=j: e.matmul(banks[6][:, 0:T], lhsT=ones_bf[:], rhs=sqt[dq][:, 0:T], start=(j == 0), stop=(j == 7)),
                          reads=[("sqt", dq), "ones_bf"], writes=[bk(6)])
                del pending_stats[:]

            for dh in range(2):
                woch = [load_chunk(woutl[:, kh * 8:(kh + 1) * 8, dh * 512:(dh + 1) * 512], 8, ("s_wout", l, kh)) for kh in range(2)]
                bo = [rot() for _ in range(4)]
                for k in range(16):
                    s, view = woch[k // 8]
                    for jd in range(4):
                        P.add("pe", lambda e, b=bo[jd], k=k, view=view, jd=jd: e.matmul(banks[b][:, 0:T], lhsT=view[:, k % 8, jd * 128:(jd + 1) * 128], rhs=mT[:, k, 0:T],
                                                                                      start=(k == 0), stop=(k == 15)),
                              reads=[("mT", k), ("ring", s)], writes=[bk(bo[jd])])
                for jd in range(4):
                    j = dh * 4 + jd
                    dq = j % 2
                    if len(pending_stats) >= 2:
                        jj0, dq0 = pending_stats.pop(0)
                        P.add("pe", lambda e, dq=dq0, j=jj0: e.matmul(banks[6][:, 0:T], lhsT=ones_bf[:], rhs=sqt[dq][:, 0:T], start=(j == 0), stop=(j == 7)),
                              reads=[("sqt", dq0), "ones_bf"], writes=[bk(6)])
                    P.add("act", lambda e, b=bo[jd], j=j: e.activation(out=cbuf[:, j, 0:T], in_=banks[b][:, 0:T], func=AF.Copy),
                          reads=[bk(bo[jd])], writes=[("cbuf", j)])
                    P.add("act", lambda e, b=bo[jd], dq=dq: e.activation(out=sqt[dq][:, 0:T], in_=banks[b][:, 0:T], func=AF.Square),
                          reads=[bk(bo[jd])], writes=[("sqt", dq)])
                    pending_stats.append((j, dq))
            flush_stats()
            rsqrt_chain(banks[6][:, 0:T], rstdb[:, 0:T], T, inv_d, [bk(6)], "rstdb")
            for j in range(8):
                t1 = tmp_next()
                P.add("dve", lambda e, j=j, t1=t1: e.tensor_tensor(out=tmps[t1][:, 0:T], in0=cbuf[:, j, 0:T], in1=rstdb[:, 0:T], op=ALU.mult),
                      reads=[("cbuf", j), "rstdb"], writes=[("tmp", t1)])
                P.add("dve", lambda e, j=j, t1=t1: e.scalar_tensor_tensor(out=hT[:, j, 0:T], in0=tmps[t1][:, 0:T], scalar=vecs[:, j, 16 + l:17 + l], in1=hT[:, j, 0:T],
                                                                          op0=ALU.mult, op1=ALU.add),
                      reads=[("tmp", t1), "vecs", ("hT", j)], writes=[("hT", j)])
                P.add("act", lambda e, j=j: e.activation(out=xnT[:, j, 0:T], in_=hT[:, j, 0:T], func=AF.Copy),
                      reads=[("hT", j)], writes=[("xnT", j)])

            wpel = wpe_bf[l].rearrange("(k p) d -> p k d", p=128)
            wpgl = wpg_bf[l].rearrange("(k p) d -> p k d", p=128)
            pes, peview = load_chunk(wpel[:, :, :], 2, ("s_wpe", l))
            for jq in range(2):
                s, view = load_chunk(wpgl[:, :, jq * 512:(jq + 1) * 512], 8, ("s_wpg", l))
                bg = [rot() for _ in range(4)]
                for k in range(8):
                    for jd in range(4):
                        P.add("pe", lambda e, b=bg[jd], k=k, view=view, jd=jd: e.matmul(banks[b][:, 0:T], lhsT=view[:, k, jd * 128:(jd + 1) * 128], rhs=xnT[:, k, 0:T],
                                                                                      start=(k == 0), stop=(k == 7)),
                              reads=[("xnT", k), ("ring", s)], writes=[bk(bg[jd])])
                for jd in range(4):
                    j = jq * 4 + jd
                    bp = rot()
                    for kp in range(2):
                        P.add("pe", lambda e, bp=bp, kp=kp, j=j: e.matmul(banks[bp][:, 0:T], lhsT=peview[:, kp, j * 128:(j + 1) * 128], rhs=pT[:, kp, 0:T],
                                                                          start=(kp == 0), stop=(kp == 1)),
                              reads=[("pT", kp), ("ring", pes)], writes=[bk(bp)])
                    ta = tmp_next()
                    P.add("act", lambda e, b=bg[jd], ta=ta: e.activation(out=tmps[ta][:, 0:T], in_=banks[b][:, 0:T], func=AF.Sigmoid),
                          reads=[bk(bg[jd])], writes=[("tmp", ta)])
                    P.add("dve", lambda e, bp=bp, ta=ta: e.tensor_tensor(out=tmps[ta][:, 0:T], in0=banks[bp][:, 0:T], in1=tmps[ta][:, 0:T], op=ALU.mult),
                          reads=[bk(bp), ("tmp", ta)], writes=[("tmp", ta)])
                    P.add("dve", lambda e, j=j, ta=ta: e.tensor_tensor(out=hT[:, j, 0:T], in0=hT[:, j, 0:T], in1=tmps[ta][:, 0:T], op=ALU.add),
                          reads=[("hT", j), ("tmp", ta)], writes=[("hT", j)])

            if si.final:
                for q, (oc, ln, ac) in enumerate(si.segs):
                    for half in range(2):
                        b = rot()
                        for kk in range(4):
                            j = half * 4 + kk
                            P.add("pe", lambda e, b=b, j=j, kk=kk, q=q: e.transpose(banks[b][0:HIST, kk * 128:(kk + 1) * 128], afin[:, j, q, :], ident[:]),
                                  reads=["afin", "ident"], writes=[bk(b)])
                        P.add("act", lambda e, b=b, half=half: e.activation(out=yst[0:HIST, half * 512:(half + 1) * 512], in_=banks[b][0:HIST, :], func=AF.Copy),
                              reads=[bk(b)], writes=["yst"])
                    if si.kind == "p":
                        dst = ncp[l, si.b]
                    else:
                        dst = ncs[l, q]
                    P.add("pool", lambda e, dst=dst: e.dma_start(out=dst, in_=yst[0:HIST, :]), reads=["yst"], dma="o_cst")
                    out_keys.append("o_cst")

        st_list = []
        for b in range(2):
            for s in range(SEQ // ST_T):
                st_list.append(STInfo("p", b, s * ST_T, ST_T, [(0, ST_T, 0)], s == 0, s == SEQ // ST_T - 1))
        st_list.append(STInfo("s", 0, 0, 128, [(0, 64, 0), (64, 64, HIST + 64)], True, True))

        for idx_st, si in enumerate(st_list):
            T = si.T
            nt = T // 128
            if si.kind == "s":
                build_spatial(True)
            if idx_st == 0:
                emit_pass_loads(0, si)
            for i in range(nt):
                q = i
                xkey = "vst%d" % i
                if idx_st == 0 or si.kind == "s":
                    emit_x_load(si, i)
                for half in range(2):
                    b = rot()
                    for kk in range(4):
                        k = half * 4 + kk
                        P.add("pe", lambda e, b=b, k=k, kk=kk, q=q: e.transpose(banks[b][:, kk * 128:(kk + 1) * 128], vst[q][:, k * 128:(k + 1) * 128], ident[:]),
                              reads=[xkey, "ident"], writes=[bk(b)])
                    eng = "act" if half == 0 else "dve"
                    if eng == "act":
                        P.add("act", lambda e, b=b, half=half, i=i: e.activation(out=hT[:, half * 4:(half + 1) * 4, i * 128:(i + 1) * 128],
                                                                                 in_=banks[b][:, :].rearrange("p (k t) -> p k t", k=4), func=AF.Copy),
                              reads=[bk(b)], writes=[("hT", half * 4 + kk) for kk in range(4)])
                    else:
                        P.add("dve", lambda e, b=b, half=half, i=i: e.tensor_copy(out=hT[:, half * 4:(half + 1) * 4, i * 128:(i + 1) * 128],
                                                                                  in_=banks[b][:, :].rearrange("p (k t) -> p k t", k=4)),
                              reads=[bk(b)], writes=[("hT", half * 4 + kk) for kk in range(4)])
            for l in range(DEPTH):
                if l < DEPTH - 1:
                    nxt = (l + 1, si)
                elif idx_st + 1 < len(st_list):
                    nxt = (0, st_list[idx_st + 1])
                else:
                    nxt = None
                emit_pass(l, si, nxt)
            for i in range(nt):
                yb = yst2[i % 2]
                ykey = "yst" if i % 2 == 0 else "yst_b"
                for half in range(2):
                    b = rot()
                    for kk in range(4):
                        k = half * 4 + kk
                        P.add("pe", lambda e, b=b, k=k, kk=kk, i=i: e.transpose(banks[b][:, kk * 128:(kk + 1) * 128], hT[:, k, i * 128:(i + 1) * 128], ident[:]),
                              reads=[("hT", k), "ident"], writes=[bk(b)])
                    if half == 0:
                        P.add("act", lambda e, b=b, half=half, yb=yb: e.activation(out=yb[:, half * 512:(half + 1) * 512], in_=banks[b][:, :], func=AF.Copy),
                              reads=[bk(b)], writes=[ykey])
                    else:
                        P.add("dve", lambda e, b=b, half=half, yb=yb: e.tensor_copy(out=yb[:, half * 512:(half + 1) * 512], in_=banks[b][:, :]),
                              reads=[bk(b)], writes=[ykey])
                if si.kind == "p":
                    dst = y_p[si.b, si.t0 + i * 128:si.t0 + (i + 1) * 128, :]
                else:
                    dst = y_s.rearrange("b t d -> (b t) d")
                P.add("pool", lambda e, dst=dst, yb=yb: e.dma_start(out=dst, in_=yb[:, :]), reads=[ykey], dma="o_y%d" % (i % 2))
                out_keys.append("o_y%d" % (i % 2))

        P.emit(final_wait_bufs=sorted(set(out_keys), key=str))
    return nc


_CACHE = {}


def kernel(x_prompt, x_sample, state_conv, p_prompt, p_sample, g_pre, w_in, ln_v_g, ln_v_b, w_s, b_s,
           conv_w, conv_b, ln_c_g, ln_c_b, w_out, g_post, w_pe, w_pg):
    f = lambda a: np.ascontiguousarray(np.asarray(a, dtype=np.float32))
    x_prompt, x_sample, state_conv, p_prompt, p_sample = map(f, (x_prompt, x_sample, state_conv, p_prompt, p_sample))
    shared = {"g_pre": f(g_pre), "w_in": f(w_in), "ln_v_g": f(ln_v_g), "ln_v_b": f(ln_v_b), "w_s": f(w_s), "b_s": f(b_s),
              "conv_w": f(conv_w), "conv_b": f(conv_b), "ln_c_g": f(ln_c_g), "ln_c_b": f(ln_c_b), "w_out": f(w_out),
              "g_post": f(g_post), "w_pe": f(w_pe), "w_pg": f(w_pg)}
    if "nc" not in _CACHE:
        _CACHE["nc"] = build_program()
    nc = _CACHE["nc"]
    in_maps = []
    for c in range(NCORES):
        sl = slice(2 * c, 2 * c + 2)
        m = dict(shared)
        m["xp"] = np.ascontiguousarray(x_prompt[sl])
        m["xsm"] = np.ascontiguousarray(x_sample[sl])
        m["sc"] = np.ascontiguousarray(state_conv[:, sl])
        m["pp"] = np.ascontiguousarray(p_prompt[:, sl])
        m["psm"] = np.ascontiguousarray(p_sample[:, sl])
        in_maps.append(m)
    res = run_bass_kernel_spmd(nc, in_maps, core_ids=list(range(NCORES)))
    rs = res.results
    y_prompt = np.concatenate([np.asarray(r["y_p"], dtype=np.float32) for r in rs], axis=0)
    y_sample = np.concatenate([np.asarray(r["y_s"], dtype=np.float32) for r in rs], axis=0)
    new_conv_prompt = np.concatenate([np.asarray(r["ncp"], dtype=np.float32) for r in rs], axis=1)
    new_conv_sample = np.concatenate([np.asarray(r["ncs"], dtype=np.float32) for r in rs], axis=1)
    new_gmlp_v_sample = np.concatenate([np.asarray(r["vs_o"], dtype=np.float32) for r in rs], axis=1)
    return (y_prompt, y_sample, new_conv_prompt, new_conv_sample, new_gmlp_v_sample)
```

```python
import contextlib
import numpy as np
import concourse.bass as bass
import concourse.mybir as mybir
from concourse.bass_utils import run_bass_kernel_spmd

F32 = mybir.dt.float32
BF16 = mybir.dt.bfloat16
AF = mybir.ActivationFunctionType
ALU = mybir.AluOpType

NCORES = 8
D = 1024
DEPTH = 4
SEQ = 2048
DEC = 64
CW = 31
HIST = CW - 1
DPLE = 256
EPS = 1e-6
ST_T = 512
RING = 5

ENGS = ("pe", "act", "dve", "pool", "sp")


class Op:
    __slots__ = ("eng", "fn", "reads", "writes", "dma", "deps", "marked", "cnt", "idx", "dmaval")

    def __init__(self, eng, fn, reads, writes, dma):
        self.eng, self.fn, self.reads, self.writes, self.dma = eng, fn, reads, writes, dma
        self.deps = []
        self.marked = False
        self.cnt = 0
        self.dmaval = 0


class Prog:
    def __init__(self, nc):
        self.nc = nc
        self.ops = []

    def add(self, eng, fn, reads=(), writes=(), dma=None):
        o = Op(eng, fn, tuple(reads), tuple(writes), dma)
        o.idx = len(self.ops)
        self.ops.append(o)
        return o

    def analyze(self):
        last_w = {}
        readers = {}
        ops = self.ops
        for o in ops:
            deps = set()
            for b in o.reads:
                w = last_w.get(b)
                if w is not None:
                    deps.add(w)
            for b in o.writes:
                w = last_w.get(b)
                if w is not None:
                    deps.add(w)
                for r in readers.get(b, ()):
                    deps.add(r)
            deps.discard(o.idx)
            best = {}
            for d in deps:
                p = ops[d]
                if p.dma is not None:
                    o.deps.append(d)
                    continue
                if o.dma is None and p.eng == o.eng:
                    if p.eng == "pe":
                        continue
                    if not any(b in p.writes for b in o.reads):
                        continue
                if d > best.get(p.eng, -1):
                    best[p.eng] = d
            for d in best.values():
                o.deps.append(d)
                ops[d].marked = True
            for b in o.reads:
                readers.setdefault(b, []).append(o.idx)
            for b in o.writes:
                last_w[b] = o.idx
                readers[b] = []

    def emit(self, final_wait_bufs=()):
        nc = self.nc
        self.analyze()
        cnt = {e: 0 for e in ENGS}
        dmacnt = {}
        for o in self.ops:
            if o.dma is None:
                if o.marked:
                    cnt[o.eng] += 1
                o.cnt = cnt[o.eng]
            else:
                dmacnt[o.dma] = dmacnt.get(o.dma, 0) + 16
                o.dmaval = dmacnt[o.dma]
        ops = self.ops
        with contextlib.ExitStack() as st:
            esem = {e: st.enter_context(nc.semaphore("s_" + e)) for e in ENGS}
            dsem = {b: st.enter_context(nc.semaphore("d_%d" % i)) for i, b in enumerate(dmacnt)}
            block = st.enter_context(nc.Block())

            def mk(ename):
                def body(eng):
                    waited = {}
                    for o in ops:
                        if o.eng != ename:
                            continue
                        need = {}
                        for d in o.deps:
                            p = ops[d]
                            if p.dma is None:
                                s, v = esem[p.eng], p.cnt
                            else:
                                s, v = dsem[p.dma], p.dmaval
                            k = id(s)
                            if v > need.get(k, (None, 0))[1]:
                                need[k] = (s, v)
                        for k, (s, v) in need.items():
                            if waited.get(k, 0) >= v:
                                continue
                            eng.wait_ge(s, v)
                            waited[k] = v
                        ins = o.fn(eng)
                        if o.dma is not None:
                            ins.then_inc(dsem[o.dma], 16)
                        elif o.marked:
                            ins.then_inc(esem[ename], 1)
                    if ename == "sp":
                        for b in final_wait_bufs:
                            eng.wait_ge(dsem[b], dmacnt[b])
                return body

            block.tensor(mk("pe"))
            block.scalar(mk("act"))
            block.vector(mk("dve"))
            block.gpsimd(mk("pool"))
            block.sync(mk("sp"))


class STInfo:
    def __init__(self, kind, b, t0, T, segs, seq_start, final):
        self.kind, self.b, self.t0, self.T, self.segs = kind, b, t0, T, segs
        self.seq_start, self.final = seq_start, final


def build_program():
    nc = bass.Bass("TRN2", target_bir_lowering=False)

    def din(name, shape):
        return nc.dram_tensor(name, list(shape), F32, kind="ExternalInput").ap()

    def dout(name, shape):
        return nc.dram_tensor(name, list(shape), F32, kind="ExternalOutput").ap()

    xp = din("xp", [2, SEQ, D])
    xsm = din("xsm", [2, DEC, D])
    sc = din("sc", [DEPTH, 2, HIST, D])
    pp = din("pp", [DEPTH, 2, SEQ, DPLE])
    psm = din("psm", [DEPTH, 2, DEC, DPLE])
    g_pre = din("g_pre", [DEPTH, D])
    w_in = din("w_in", [DEPTH, D, 6 * D])
    ln_v_g = din("ln_v_g", [DEPTH, D])
    ln_v_b = din("ln_v_b", [DEPTH, D])
    w_s = din("w_s", [DEPTH, 8, 128, 128])
    b_s = din("b_s", [DEPTH, 8, 128])
    conv_w = din("conv_w", [DEPTH, CW, D])
    conv_b = din("conv_b", [DEPTH, D])
    ln_c_g = din("ln_c_g", [DEPTH, D])
    ln_c_b = din("ln_c_b", [DEPTH, D])
    w_out = din("w_out", [DEPTH, 2 * D, D])
    g_post = din("g_post", [DEPTH, D])
    w_pe = din("w_pe", [DEPTH, DPLE, D])
    w_pg = din("w_pg", [DEPTH, D, D])

    y_p = dout("y_p", [2, SEQ, D])
    y_s = dout("y_s", [2, DEC, D])
    ncp = dout("ncp", [DEPTH, 2, HIST, D])
    ncs = dout("ncs", [DEPTH, 2, HIST, D])
    vs_o = dout("vs_o", [DEPTH, 2, DEC, D])

    win_bf = nc.dram_tensor("win_bf", [DEPTH, D, 6 * D], BF16, kind="Internal").ap()
    wout_bf = nc.dram_tensor("wout_bf", [DEPTH, 2 * D, D], BF16, kind="Internal").ap()
    wpg_bf = nc.dram_tensor("wpg_bf", [DEPTH, D, D], BF16, kind="Internal").ap()
    wpe_bf = nc.dram_tensor("wpe_bf", [DEPTH, DPLE, D], BF16, kind="Internal").ap()
    wp_bf = nc.dram_tensor("wp_bf", [DEPTH, 2, 128, 4096], BF16, kind="Internal").ap()

    with contextlib.ExitStack() as es:
        def sb(name, shape, dt):
            return es.enter_context(nc.sbuf_tensor(name, list(shape), dt))

        ident = sb("ident", [128, 128], F32)
        identb = sb("identb", [128, 128], BF16)
        ones_bf = sb("ones_bf", [128, 128], BF16)
        ones2 = sb("ones2", [128, 128], BF16)
        epst = sb("epst", [128, 1], F32)
        dummy = sb("dummy_ln", [128, 2], F32)
        vecs = sb("vecs", [128, 8, 20], F32)
        cwT = sb("cwT", [128, 8, 128], F32)
        wsT = sb("wsT", [128, DEPTH, 8, 128], BF16)
        Bs = sb("Bs", [128, DEPTH, 8, 128], BF16)
        ring = [sb("ring%d" % i, [128, 4096], BF16) for i in range(RING)]
        hT = sb("hT", [128, 8, ST_T], F32)
        yst = sb("yst", [128, D], F32)
        yst2 = [yst, sb("yst_b", [128, D], F32)]
        xnT = sb("xnT", [128, 8, ST_T], BF16)
        sqt = [sb("sqt%d" % i, [128, ST_T], BF16) for i in range(2)]
        vst = [sb("vst%d" % i, [128, D], F32) for i in range(4)]
        vhat = sb("vhat", [128, 4, D], BF16)
        lnv = sb("lnv", [128, 2, D], F32)
        bnst = [sb("bnst%d" % i, [128, 2, 6], F32) for i in range(4)]
        mv4 = sb("mv4", [128, 4, 2], F32)
        sd4 = sb("sd4", [128, 4], F32)
        rs4 = sb("rs4", [128, 4], F32)
        NTMP = 6
        tmps = [sb("tmp%d" % i, [128, ST_T], F32) for i in range(NTMP)]
        rstdb = sb("rstdb", [128, ST_T], F32)
        meanb = sb("meanb", [128, ST_T], F32)
        ATW = HIST + ST_T + 1
        aT = sb("aT", [128, 8, ATW], BF16)
        tails = sb("tails", [128, DEPTH, 8, HIST], BF16)
        afin = sb("afin", [128, 8, 2, HIST], F32)
        mT = sb("mT", [128, 16, ST_T], BF16)
        cbuf = sb("cbuf", [128, 8, ST_T], F32)
        cbt = [sb("cbt%d" % i, [128, ST_T], BF16) for i in range(2)]
        pst = sb("pst", [128, 4, DPLE], F32)
        pT = sb("pT", [128, 2, ST_T], BF16)
        A4 = [sb("A4_%d" % i, [128, 4, 540], BF16) for i in range(2)]
        m4 = sb("m4", [128, 4], F32)
        irep = sb("irep", [128, 32], F32)
        banks = [es.enter_context(nc.psum_tensor("bank%d" % i, [128, 512], F32)) for i in range(8)]

        P = Prog(nc)
        state = {"rot": 0, "tmp": 0, "chunk": 0}
        out_keys = []

        def rot():
            b = state["rot"] % 6
            state["rot"] += 1
            return b

        def tmp_next():
            i = state["tmp"] % NTMP
            state["tmp"] += 1
            return i

        def bk(b):
            return ("bank", b)

        def ring_view(s, k):
            if k == 8:
                return ring[s][:, :].rearrange("p (k e) -> p k e", k=8)
            if k == "wp":
                return ring[s][:, :].rearrange("p (j m q c) -> p j m q c", j=4, m=8, q=4)
            return ring[s][:, 0:2048].rearrange("p (k e) -> p k e", k=2)

        def load_chunk(src_ap, k, key):
            s = state["chunk"] % RING
            state["chunk"] += 1
            view = ring_view(s, k)
            dst = ring[s][:, :] if k == "wp" else view
            P.add("sp", lambda e, dst=dst, src=src_ap: e.dma_start(out=dst, in_=src),
                  reads=[key], writes=[("ring", s)], dma=("ring", s))
            return s, view

        P.add("dve", lambda e: e.memset(ident[:], 0.0), writes=["ident"])
        P.add("pool", lambda e: e.affine_select(out=ident[:], in_=ident[:], pattern=[[-1, 128]], compare_op=ALU.not_equal,
                                                fill=1.0, base=0, channel_multiplier=1), reads=["ident"], writes=["ident"])
        P.add("dve", lambda e: e.tensor_copy(out=identb[:], in_=ident[:]), reads=["ident"], writes=["identb"])
        P.add("dve", lambda e: e.memset(ones_bf[:], 1.0), writes=["ones_bf"])
        P.add("dve", lambda e: e.memset(ones2[:], 0.0), writes=["ones2"])
        P.add("dve", lambda e: e.memset(ones2[0:2, :], 1.0), writes=["ones2"])
        P.add("dve", lambda e: e.memset(epst[:], EPS), writes=["epst"])

        cast_hist = []
        CAST_DEPTH = 6

        def cast_dma2(out_ap, in_ap, key):
            rd = [cast_hist[-CAST_DEPTH]] if len(cast_hist) >= CAST_DEPTH else []
            P.add("pool", lambda e: e.dma_start(out=out_ap, in_=in_ap), reads=rd, writes=[key], dma=key)
            cast_hist.append(key)

        def emit_casts(l):
            for g in (1, 3, 4, 0, 2, 5):
                cast_dma2(win_bf[l][:, g * D:(g + 1) * D], w_in[l][:, g * D:(g + 1) * D], ("s_win", l, g))
            for hh in range(2):
                cast_dma2(wout_bf[l][hh * D:(hh + 1) * D, :], w_out[l][hh * D:(hh + 1) * D, :], ("s_wout", l, hh))
            cast_dma2(wpg_bf[l], w_pg[l], ("s_wpg", l))
            cast_dma2(wpe_bf[l], w_pe[l], ("s_wpe", l))

        cb_keys0 = [("cbuf", j) for j in range(8)]
        mt_keys0 = [("mT", j) for j in range(16)]
        ws_all = cbuf[:, :, :].rearrange("p a b -> p (a b)").rearrange("p (l h s) -> p l h s", l=DEPTH, h=8)
        bs_all = mT[:, :, :].rearrange("p a b -> p (a b)").bitcast(F32)
        P.add("act", lambda e: e.dma_start(out=ws_all.rearrange("p l h s -> p (l h) s"), in_=w_s.rearrange("l h t s -> t (l h) s")),
              writes=cb_keys0, dma="pre_ws")
        P.add("act", lambda e: e.dma_start(out=bs_all.rearrange("p (l h t) -> p l h t", l=DEPTH, h=8), in_=b_s.partition_broadcast(128)),
              writes=mt_keys0, dma="pre_bs")
        for l_ in range(DEPTH):
            emit_casts(l_)

        for r, src in enumerate([g_pre, conv_b, ln_c_g, ln_c_b, g_post]):
            P.add("act", lambda e, r=r, src=src: e.dma_start(out=vst[0][4 * r:4 * r + 4, :], in_=src),
                  writes=["vst0"], dma="vst0")
        P.add("act", lambda e: e.dma_start(out=vst[1][0:DEPTH * CW, :], in_=conv_w.rearrange("l k c -> (l k) c")),
              writes=["vst1"], dma="vst1")
        for half in range(2):
            b = rot()
            for kk in range(4):
                k = half * 4 + kk
                P.add("pe", lambda e, b=b, k=k, kk=kk: e.transpose(banks[b][:, kk * 20:(kk + 1) * 20], vst[0][0:20, k * 128:(k + 1) * 128], ident[0:20, 0:20]),
                      reads=["vst0", "ident"], writes=[bk(b)])
            P.add("dve", lambda e, b=b, half=half: e.tensor_copy(out=vecs[:, half * 4:(half + 1) * 4, :],
                                                                 in_=banks[b][:, 0:80].rearrange("p (k r) -> p k r", k=4)),
                  reads=[bk(b)], writes=["vecs"])
        NCW = DEPTH * CW
        P.add("dve", lambda e: e.memset(cwT[:, :, :], 0.0), writes=["cwT"])
        for half in range(2):
            b = rot()
            for kk in range(4):
                k = half * 4 + kk
                P.add("pe", lambda e, b=b, k=k, kk=kk: e.transpose(banks[b][:, kk * 128:kk * 128 + NCW], vst[1][0:NCW, k * 128:(k + 1) * 128], ident[0:NCW, 0:NCW]),
                      reads=["vst1", "ident"], writes=[bk(b)])
            P.add("dve", lambda e, b=b, half=half: e.tensor_copy(out=cwT[:, half * 4:(half + 1) * 4, 0:NCW],
                                                                 in_=banks[b][:, :].rearrange("p (k r) -> p k r", k=4)[:, :, 0:NCW]),
                  reads=[bk(b)], writes=["cwT"])

        def build_spatial(sample, dq="pool", pre=None):
            for l in range(DEPTH):
                stg = vst[l % 2]
                skey = "vst%d" % (l % 2)
                sview = stg[:, :].rearrange("p (h s) -> p h s", h=8)
                skeys = [skey]
                if pre is not None:
                    sview, skeys = pre["ws"](l)
                elif not sample:
                    P.add(dq, lambda e, l=l, sview=sview: e.dma_start(out=sview, in_=w_s[l].rearrange("h t s -> t h s")),
                          writes=[skey], dma=skey)
                else:
                    P.add("dve", lambda e, stg=stg: e.memset(stg[:, :], 0.0), writes=[skey])
                    P.add(dq, lambda e, l=l, sview=sview: e.dma_start(out=sview[0:64, :, 0:64], in_=w_s[l][:, 0:64, 0:64].rearrange("h t s -> t h s")),
                          reads=[skey], writes=[skey], dma=skey)
                    P.add(dq, lambda e, l=l, sview=sview: e.dma_start(out=sview[64:128, :, 64:128], in_=w_s[l][:, 0:64, 0:64].rearrange("h t s -> t h s")),
                          reads=[skey], writes=[skey], dma=skey)
                for half in range(2):
                    b = rot()
                    for hh in range(4):
                        h = half * 4 + hh
                        P.add("pe", lambda e, b=b, h=h, hh=hh, sview=sview: e.transpose(banks[b][:, hh * 128:(hh + 1) * 128], sview[:, h, :], ident[:]),
                              reads=skeys + ["ident"], writes=[bk(b)])
                    P.add("act", lambda e, b=b, l=l, half=half: e.activation(out=wsT[:, l, half * 4:(half + 1) * 4, :],
                                                                             in_=banks[b][:, :].rearrange("p (h t) -> p h t", h=4), func=AF.Copy),
                          reads=[bk(b)], writes=[("wsT", l)])
                if not sample:
                    P.add("dve", lambda e, l=l: e.memset(wsT[64:128, l, :, 0:64], 0.0), reads=[("wsT", l)], writes=[("wsT", l)])
                bt = yst
                btf = bt[:, :]
                btkeys = ["yst"]
                btv = bt[:, :].rearrange("p (h t) -> p h t", h=8)
                if pre is not None:
                    btf, btkeys = pre["bs"](l)
                elif not sample:
                    P.add(dq, lambda e, l=l, btv=btv: e.dma_start(out=btv, in_=b_s[l].partition_broadcast(128)),
                          writes=["yst"], dma="yst")
                else:
                    for q in range(2):
                        P.add(dq, lambda e, l=l, btv=btv, q=q: e.dma_start(out=btv[:, :, q * 64:(q + 1) * 64], in_=b_s[l][:, 0:64].partition_broadcast(128)),
                              writes=["yst"], dma="yst")
                ti = tmp_next()
                hiv = tmps[ti][:, :].bitcast(BF16)[:, 0:1024]
                P.add("dve", lambda e, hiv=hiv, btf=btf: e.tensor_copy(out=hiv, in_=btf), reads=btkeys, writes=[("tmp", ti)])
                tj = tmp_next()
                tk = tmp_next()
                lov = tmps[tj]
                lov2 = tmps[tk]
                P.add("dve", lambda e, hiv=hiv, btf=btf, lov=lov: e.tensor_tensor(out=lov[:, :], in0=btf[:, 0:512], in1=hiv[:, 0:512], op=ALU.subtract),
                      reads=btkeys + [("tmp", ti)], writes=[("tmp", tj)])
                P.add("dve", lambda e, hiv=hiv, btf=btf, lov2=lov2: e.tensor_tensor(out=lov2[:, :], in0=btf[:, 512:1024], in1=hiv[:, 512:1024], op=ALU.subtract),
                      reads=btkeys + [("tmp", ti)], writes=[("tmp", tk)])
                Bv = Bs[:, l, :, :].rearrange("p h t -> p (h t)")
                P.add("dve", lambda e, Bv=Bv, hiv=hiv: e.tensor_scalar(out=Bv, in0=hiv, scalar1=ident[:, 0:1], scalar2=None, op0=ALU.mult),
                      reads=[("tmp", ti), "ident"], writes=[("Bs", l)])
                P.add("dve", lambda e, Bv=Bv, lov=lov: e.scalar_tensor_tensor(out=Bv[:, 0:512], in0=lov[:, :], scalar=ident[:, 1:2], in1=Bv[:, 0:512], op0=ALU.mult, op1=ALU.add),
                      reads=[("tmp", tj), ("Bs", l), "ident"], writes=[("Bs", l)])
                P.add("dve", lambda e, Bv=Bv, lov2=lov2: e.scalar_tensor_tensor(out=Bv[:, 512:1024], in0=lov2[:, :], scalar=ident[:, 1:2], in1=Bv[:, 512:1024], op0=ALU.mult, op1=ALU.add),
                      reads=[("tmp", tk), ("Bs", l), "ident"], writes=[("Bs", l)])

        build_spatial(False, "act", pre={"ws": lambda l: (ws_all[:, l, :, :], cb_keys0), "bs": lambda l: (bs_all[:, l * 1024:(l + 1) * 1024], mt_keys0)})

        P.add("dve", lambda e: e.memset(aT[:, :, :], 0.0), writes=["aTh"] + [("aT", j) for j in range(8)])
        P.add("dve", lambda e: e.reduce_sum(out=m4[:, :], in_=ident[:, :].rearrange("p (i c) -> p i c", i=4), axis=mybir.AxisListType.X),
              reads=["ident"], writes=["m4"])
        P.add("dve", lambda e: e.reduce_sum(out=irep[:, :], in_=ident[:, :].rearrange("p (i c) -> p c i", i=4), axis=mybir.AxisListType.X),
              reads=["ident"], writes=["irep"])
        cb_keys = [("cbuf", j) for j in range(8)]
        Rflat = cbuf[:, :, :].rearrange("p a b -> p (a b)")
        Rpad = Rflat.rearrange("p (q g k) -> p q g k", q=4, g=32)
        P.add("dve", lambda e: e.memset(Rflat, 0.0), writes=cb_keys)
        cw_flat = cwT[:, :, :].rearrange("p j x -> p (j x)")
        for q in range(4):
            ti = tmp_next()
            selv = tmps[ti][:, 0:128]
            P.add("dve", lambda e, selv=selv, q=q: e.tensor_copy(out=selv.rearrange("p (i c) -> p i c", i=4),
                                                                 in_=ident[:, 32 * q:32 * q + 32].unsqueeze(1).broadcast_to([128, 4, 32])),
                  reads=["ident"], writes=[("tmp", ti)])
            for hh in range(2):
                b_ = rot()
                P.add("pe", lambda e, b_=b_, selv=selv, hh=hh: e.matmul(banks[b_][:, 0:512], lhsT=selv, rhs=cw_flat[:, hh * 512:(hh + 1) * 512], start=True, stop=True),
                      reads=[("tmp", ti), "cwT"], writes=[bk(b_)])
                P.add("act", lambda e, b_=b_, q=q, hh=hh: e.activation(
                    out=Rpad[:, q, hh * 16:(hh + 1) * 16, 0:CW].rearrange("p (j l) k -> p j l k", j=4),
                    in_=banks[b_][:, 0:512].rearrange("p (j x) -> p j x", j=4)[:, :, 0:NCW].rearrange("p j (l k) -> p j l k", l=DEPTH), func=AF.Copy),
                      reads=[bk(b_)], writes=cb_keys)
        Vflat = cwT[:, :, :].rearrange("p j x -> p (j x)")
        V2 = Vflat.rearrange("p (g m) -> p g m", m=8)
        R4 = Rflat.rearrange("p (g m i) -> p g m i", m=8, i=4)
        P.add("dve", lambda e: e.tensor_scalar(out=V2, in0=R4[:, :, :, 0], scalar1=m4[:, 0:1], scalar2=None, op0=ALU.mult),
              reads=cb_keys + ["m4"], writes=["cwT"])
        for ii in range(1, 4):
            P.add("dve", lambda e, ii=ii: e.scalar_tensor_tensor(out=V2, in0=R4[:, :, :, ii], scalar=m4[:, ii:ii + 1], in1=V2, op0=ALU.mult, op1=ALU.add),
                  reads=cb_keys + ["m4", "cwT"], writes=["cwT"])
        V5 = Vflat.rearrange("p (q j l m) -> p q j l m", q=4, j=8, l=DEPTH)
        mT_flat = mT[:, :, :].rearrange("p a b -> p (a b)")
        nstg = 0
        for l in range(DEPTH):
            for jh in range(2):
                hq = nstg % 2
                nstg += 1
                stg = mT_flat[:, hq * 4096:(hq + 1) * 4096]
                stg5 = stg.rearrange("p (j m q c) -> p j m q c", j=4, m=8, q=4)
                keys = [("mT", jj) for jj in range(hq * 8, hq * 8 + 8)]
                for jj in range(4):
                    j = jh * 4 + jj
                    P.add("pool", lambda e, stg5=stg5, jj=jj, j=j, l=l: e.tensor_tensor(
                        out=stg5[:, jj, :, :, :],
                        in0=V5[:, :, j, l, :].rearrange("p q m -> p m q").unsqueeze(3).broadcast_to([128, 8, 4, 32]),
                        in1=irep[:, :].unsqueeze(1).unsqueeze(1).broadcast_to([128, 8, 4, 32]), op=ALU.mult),
                          reads=["cwT", "irep"], writes=keys)
                P.add("act", lambda e, stg=stg, l=l, jh=jh: e.dma_start(out=wp_bf[l, jh], in_=stg), reads=keys, writes=[("s_wp", l)], dma=("s_wp", l))

        def emit_pass_loads(l, si):
            T = si.T
            nt = T // 128
            if si.kind == "p":
                psrc = pp[l, si.b, si.t0:si.t0 + T, :].rearrange("(i p) e -> p i e", p=128)
                P.add("sp", lambda e, psrc=psrc: e.dma_start(out=pst[:, 0:nt, :], in_=psrc), writes=["pst"], dma="pst")
            else:
                P.add("sp", lambda e: e.dma_start(out=pst[:, 0, :], in_=psm[l].rearrange("b t e -> (b t) e")), writes=["pst"], dma="pst")
            P.add("sp", lambda e: e.dma_start(out=lnv[:, 0, :], in_=ln_v_g[l].partition_broadcast(128)), writes=["lnv"], dma="lnv")
            P.add("sp", lambda e: e.dma_start(out=lnv[:, 1, :], in_=ln_v_b[l].partition_broadcast(128)), writes=["lnv"], dma="lnv")

        def emit_x_load(si, i):
            q = i % 2
            if si.kind == "p":
                src = xp[si.b, si.t0 + i * 128:si.t0 + (i + 1) * 128, :]
            else:
                src = xsm.rearrange("b t d -> (b t) d")
            P.add("sp", lambda e, i=i, src=src: e.dma_start(out=vst[i][:, :], in_=src), writes=["vst%d" % i], dma="vst%d" % i)

        def rsqrt_chain(src_ap, dst_ap, T, scale, rkeys, wkey):
            t0 = tmp_next()
            P.add("act", lambda e, t0=t0: e.activation(out=tmps[t0][:, 0:T], in_=src_ap, func=AF.Ln, bias=epst[:, 0:1], scale=scale),
                  reads=list(rkeys) + ["epst"], writes=[("tmp", t0)])
            P.add("act", lambda e, t0=t0: e.activation(out=dst_ap, in_=tmps[t0][:, 0:T], func=AF.Exp, scale=-0.5),
                  reads=[("tmp", t0)], writes=[wkey])

        def preload_lnexp():
            P.add("act", lambda e: e.activation(out=dummy[:, 0:1], in_=epst[:, 0:1], func=AF.Ln), reads=["epst"], writes=["dummy"])

        def emit_pass(l, si, nxt):
            T = si.T
            nt = T // 128
            inv_d = 1.0 / D
            winl = win_bf[l].rearrange("(k p) e -> p k e", p=128)

            def win_chunk(c0):
                return load_chunk(winl[:, :, c0:c0 + 512], 8, ("s_win", l, c0 // D))

            if si.kind == "p":
                if si.seq_start:
                    P.add("dve", lambda e: e.memset(aT[:, :, 0:HIST], 0.0), writes=["aTh"])
                else:
                    P.add("act", lambda e: e.activation(out=aT[:, :, 0:HIST], in_=tails[:, l, :, :], func=AF.Copy),
                          reads=[("tails", l)], writes=["aTh"])
            else:
                for q, (oc, ln, ac) in enumerate(si.segs):
                    P.add("pool", lambda e, q=q: e.dma_start(out=yst[0:HIST, :], in_=sc[l, q]), writes=["yst"], dma="yst")
                    for half in range(2):
                        b = rot()
                        for kk in range(4):
                            k = half * 4 + kk
                            P.add("pe", lambda e, b=b, k=k, kk=kk: e.transpose(banks[b][:, kk * HIST:(kk + 1) * HIST], yst[0:HIST, k * 128:(k + 1) * 128], ident[0:HIST, 0:HIST]),
                                  reads=["yst", "ident"], writes=[bk(b)])
                        P.add("act", lambda e, b=b, half=half, ac=ac: e.activation(out=aT[:, half * 4:(half + 1) * 4, ac:ac + HIST],
                                                                                   in_=banks[b][:, 0:4 * HIST].rearrange("p (k r) -> p k r", k=4), func=AF.Copy),
                              reads=[bk(b)], writes=["aTh"])

            preload_lnexp()
            for k in range(8):
                q = k % 2
                P.add("act", lambda e, k=k, q=q: e.activation(out=sqt[q][:, 0:T], in_=hT[:, k, 0:T], func=AF.Square),
                      reads=[("hT", k)], writes=[("sqt", q)])
                P.add("pe", lambda e, k=k, q=q: e.matmul(banks[6][:, 0:T], lhsT=ones_bf[:], rhs=sqt[q][:, 0:T], start=(k == 0), stop=(k == 7)),
                      reads=[("sqt", q), "ones_bf"], writes=[bk(6)])
            rsqrt_chain(banks[6][:, 0:T], rstdb[:, 0:T], T, inv_d, [bk(6)], "rstdb")
            for kp in range(2):
                b = rot()
                for i in range(nt):
                    P.add("pe", lambda e, b=b, i=i, kp=kp: e.transpose(banks[b][:, i * 128:(i + 1) * 128], pst[:, i, kp * 128:(kp + 1) * 128], ident[:]),
                          reads=["pst", "ident"], writes=[bk(b)])
                P.add("act", lambda e, b=b, kp=kp: e.activation(out=pT[:, kp, 0:T], in_=banks[b][:, 0:T], func=AF.Copy),
                      reads=[bk(b)], writes=[("pT", kp)])

            vch = [win_chunk(D), win_chunk(D + 512)]
            groups = [(i, half) for i in range(nt) for half in range(2)]
            for g0 in range(0, len(groups), 4):
                grp = groups[g0:g0 + 4]
                bs_ = [rot() for _ in grp]
                for k in range(8):
                    if g0 == 0:
                        P.add("dve", lambda e, k=k: e.scalar_tensor_tensor(out=xnT[:, k, 0:T], in0=hT[:, k, 0:T], scalar=vecs[:, k, l:l + 1], in1=rstdb[:, 0:T],
                                                                           op0=ALU.mult, op1=ALU.mult),
                              reads=[("hT", k), "vecs", "rstdb"], writes=[("xnT", k)])
                    for (i, half), b in zip(grp, bs_):
                        s, view = vch[half]
                        P.add("pe", lambda e, b=b, k=k, i=i, view=view: e.matmul(banks[b][:, :], lhsT=xnT[:, k, i * 128:(i + 1) * 128], rhs=view[:, k, :],
                                                                                start=(k == 0), stop=(k == 7)),
                              reads=[("xnT", k), ("ring", s)], writes=[bk(b)])
                for (i, half), b in zip(grp, bs_):
                    P.add("act", lambda e, b=b, i=i, half=half: e.activation(out=vst[i][:, half * 512:(half + 1) * 512], in_=banks[b][:, :], func=AF.Gelu_apprx_tanh),
                          reads=[bk(b)], writes=["vst%d" % i])
                for i in sorted(set(i for i, _ in grp)):
                    vkey = "vst%d" % i
                    for half in range(2):
                        P.add("dve", lambda e, i=i, half=half: e.bn_stats(out=bnst[i][:, half, :], in_=vst[i][:, half * 512:(half + 1) * 512]),
                              reads=[vkey], writes=[("bnst", i)])
                    P.add("dve", lambda e, i=i: e.bn_aggr(out=mv4[:, i, :], in_=bnst[i][:, :, :].rearrange("p a b -> p (a b)")),
                          reads=[("bnst", i)], writes=["mv4"])
            P.add("act", lambda e: e.activation(out=sd4[:, 0:nt], in_=mv4[:, 0:nt, 1], func=AF.Ln, bias=epst[:, 0:1], scale=1.0),
                  reads=["mv4", "epst"], writes=["sd4"])
            P.add("act", lambda e: e.activation(out=rs4[:, 0:nt], in_=sd4[:, 0:nt], func=AF.Exp, scale=-0.5), reads=["sd4"], writes=["rs4"])
            deferred = []
            for i in range(nt):
              def _norm(i=i):
                  vkey = "vst%d" % i
                  P.add("dve", lambda e, i=i: e.tensor_scalar(out=vst[i][:, :], in0=vst[i][:, :], scalar1=mv4[:, i, 0:1], scalar2=rs4[:, i:i + 1],
                                                              op0=ALU.subtract, op1=ALU.mult),
                        reads=[vkey, "mv4", "rs4"], writes=[vkey])
                  P.add("dve", lambda e, i=i: e.tensor_tensor(out=vst[i][:, :], in0=vst[i][:, :], in1=lnv[:, 0, :], op=ALU.mult),
                        reads=[vkey, "lnv"], writes=[vkey])
                  if si.kind == "p":
                      P.add("dve", lambda e, i=i: e.tensor_tensor(out=vhat[:, i, :], in0=vst[i][:, :], in1=lnv[:, 1, :], op=ALU.add),
                            reads=[vkey, "lnv"], writes=[("vhat", i)])
                  else:
                      P.add("dve", lambda e, i=i: e.tensor_tensor(out=vst[i][:, :], in0=vst[i][:, :], in1=lnv[:, 1, :], op=ALU.add),
                            reads=[vkey, "lnv"], writes=[vkey])
                      P.add("pool", lambda e, i=i: e.dma_start(out=vs_o[l].rearrange("b t c -> (b t) c"), in_=vst[i][:, :]),
                            reads=[vkey], dma=("o_vs", i))
                      out_keys.append(("o_vs", i))
                      P.add("act", lambda e, i=i: e.activation(out=vhat[:, i, :], in_=vst[i][:, :], func=AF.Copy),
                            reads=[vkey], writes=[("vhat", i)])
              deferred.append(_norm)

            for j in range(8):
                jj = j % 4
                if jj == 0:
                    lch = win_chunk(3 * D + (j // 4) * 512)
                    gch = win_chunk(4 * D + (j // 4) * 512)
                bl, bg = rot(), rot()
                for (b, (s, view)) in ((bl, lch), (bg, gch)):
                    for k in range(8):
                        P.add("pe", lambda e, b=b, k=k, view=view, jj=jj: e.matmul(banks[b][:, 0:T], lhsT=view[:, k, jj * 128:(jj + 1) * 128], rhs=xnT[:, k, 0:T],
                                                                                 start=(k == 0), stop=(k == 7)),
                              reads=[("xnT", k), ("ring", s)], writes=[bk(b)])
                ta = tmp_next()
                P.add("act", lambda e, bg=bg, ta=ta: e.activation(out=tmps[ta][:, 0:T], in_=banks[bg][:, 0:T], func=AF.Sigmoid),
                      reads=[bk(bg)], writes=[("tmp", ta)])
                for q, (oc, ln, ac) in enumerate(si.segs):
                    P.add("dve", lambda e, bl=bl, ta=ta, j=j, oc=oc, ln=ln, ac=ac: e.tensor_tensor(out=aT[:, j, ac + HIST:ac + HIST + ln], in0=banks[bl][:, oc:oc + ln],
                                                                                                  in1=tmps[ta][:, oc:oc + ln], op=ALU.mult),
                          reads=[bk(bl), ("tmp", ta)], writes=[("aT", j)])
                    if si.final:
                        P.add("dve", lambda e, bl=bl, ta=ta, j=j, oc=oc, ln=ln, q=q: e.tensor_tensor(out=afin[:, j, q, :], in0=banks[bl][:, oc + ln - HIST:oc + ln],
                                                                                                    in1=tmps[ta][:, oc + ln - HIST:oc + ln], op=ALU.mult),
                              reads=[bk(bl), ("tmp", ta)], writes=["afin"])
                if j % 2 == 1 and deferred:
                    deferred.pop(0)()
            while deferred:
                deferred.pop(0)()
            if si.kind == "p" and not si.final:
                P.add("act", lambda e: e.activation(out=tails[:, l, :, :], in_=aT[:, :, T:T + HIST], func=AF.Copy),
                      reads=[("aT", j) for j in range(8)], writes=[("tails", l)])

            for jp in range(4):
                js = (2 * jp, 2 * jp + 1)
                if jp % 2 == 0:
                    uch = win_chunk((jp // 2) * 512)
                    zch = win_chunk(2 * D + (jp // 2) * 512)
                bu = [rot(), rot()]
                bz = [rot(), rot()]
                bm = [rot(), rot()]
                for (bb, (s, view)) in ((bu, uch), (bz, zch)):
                    for idx, j in enumerate(js):
                        jj = j % 4
                        b = bb[idx]
                        for k in range(8):
                            P.add("pe", lambda e, b=b, k=k, view=view, jj=jj: e.matmul(banks[b][:, 0:T], lhsT=view[:, k, jj * 128:(jj + 1) * 128], rhs=xnT[:, k, 0:T],
                                                                                     start=(k == 0), stop=(k == 7)),
                                  reads=[("xnT", k), ("ring", s)], writes=[bk(b)])
                for idx, j in enumerate(js):
                    b = bm[idx]
                    for i in range(nt):
                        P.add("pe", lambda e, b=b, i=i, j=j: e.matmul(banks[b][:, i * 128:(i + 1) * 128], lhsT=vhat[:, i, j * 128:(j + 1) * 128], rhs=wsT[:, l, j, :],
                                                                      start=True, stop=False),
                              reads=[("vhat", i), ("wsT", l)], writes=[bk(b)])
                        P.add("pe", lambda e, b=b, i=i, j=j: e.matmul(banks[b][:, i * 128:(i + 1) * 128], lhsT=ones2[:], rhs=Bs[:, l, j, :],
                                                                      start=False, stop=True),
                              reads=["ones2", ("Bs", l)], writes=[bk(b)])
                tas = [tmp_next(), tmp_next()]
                tbs = [tmp_next(), tmp_next()]
                for idx in range(2):
                    P.add("act", lambda e, b=bu[idx], ta=tas[idx]: e.activation(out=tmps[ta][:, 0:T], in_=banks[b][:, 0:T], func=AF.Gelu_apprx_tanh),
                          reads=[bk(bu[idx])], writes=[("tmp", tas[idx])])
                for idx in range(2):
                    P.add("act", lambda e, b=bz[idx], tb=tbs[idx]: e.activation(out=tmps[tb][:, 0:T], in_=banks[b][:, 0:T], func=AF.Silu),
                          reads=[bk(bz[idx])], writes=[("tmp", tbs[idx])])
                for idx, j in enumerate(js):
                    ta, tb = tas[idx], tbs[idx]
                    P.add("dve", lambda e, ta=ta, tb=tb: e.tensor_tensor(out=tmps[ta][:, 0:T], in0=tmps[ta][:, 0:T], in1=tmps[tb][:, 0:T], op=ALU.mult),
                          reads=[("tmp", ta), ("tmp", tb)], writes=[("tmp", ta)])
                    P.add("dve", lambda e, b=bm[idx], ta=ta, j=j: e.tensor_tensor(out=mT[:, j, 0:T], in0=banks[b][:, 0:T], in1=tmps[ta][:, 0:T], op=ALU.mult),
                          reads=[bk(bm[idx]), ("tmp", ta)], writes=[("mT", j)])

            WA = max(ac + 28 + ln for (oc, ln, ac) in si.segs)
            ones32 = ones_bf[:, 0:32]

            def conv_stats_mm(j):
                dq = j % 2
                for q in range(4):
                    P.add("pe", lambda e, dq=dq, j=j, q=q: e.matmul(banks[6][32 * q:32 * q + 32, 0:T], lhsT=ones32, rhs=cbt[dq][:, 0:T], start=(j == 0), stop=(j == 7),
                                                                   tile_position=(0, 32 * q)),
                          reads=[("cbt", dq), "ones_bf"], writes=[bk(6)])
                for q in range(4):
                    P.add("pe", lambda e, dq=dq, j=j, q=q: e.matmul(banks[7][32 * q:32 * q + 32, 0:T], lhsT=ones32, rhs=sqt[dq][:, 0:T], start=(j == 0), stop=(j == 7),
                                                                   tile_position=(0, 32 * q)),
                          reads=[("sqt", dq), "ones_bf"], writes=[bk(7)])

            def emit_sel(j):
                sl = j % 2
                W0 = min(WA, 512)
                W1 = WA - W0
                bqs = [rot() for _ in range(4)]
                for q in range(4):
                    for i in range(4):
                        P.add("pe", lambda e, b=bqs[q], q=q, i=i, j=j, W0=W0: e.matmul(banks[b][32 * i:32 * i + 32, 0:W0], lhsT=identb[:, 32 * q:32 * q + 32],
                                                                                      rhs=aT[:, j, i:i + W0], start=True, stop=True, tile_position=(0, 32 * i)),
                              reads=["identb", ("aT", j), "aTh"], writes=[bk(bqs[q])])
                bsm = None
                if W1 > 0:
                    bsm = rot()
                    for q in range(4):
                        for i in range(4):
                            P.add("pe", lambda e, b=bsm, q=q, i=i, j=j, W1=W1: e.matmul(banks[b][32 * i:32 * i + 32, W1 * q:W1 * q + W1], lhsT=identb[:, 32 * q:32 * q + 32],
                                                                                       rhs=aT[:, j, 512 + i:512 + i + W1], start=True, stop=True, tile_position=(0, 32 * i)),
                                  reads=["identb", ("aT", j), "aTh"], writes=[bk(bsm)])
                for q in range(4):
                    if q < 2:
                        P.add("act", lambda e, b=bqs[q], q=q, sl=sl, W0=W0: e.activation(out=A4[sl][:, q, 0:W0], in_=banks[b][:, 0:W0], func=AF.Copy),
                              reads=[bk(bqs[q])], writes=[("A4", sl)])
                    else:
                        P.add("dve", lambda e, b=bqs[q], q=q, sl=sl, W0=W0: e.tensor_copy(out=A4[sl][:, q, 0:W0], in_=banks[b][:, 0:W0]),
                              reads=[bk(bqs[q])], writes=[("A4", sl)])
                if W1 > 0:
                    P.add("dve", lambda e, b=bsm, sl=sl, W1=W1: e.tensor_copy(out=A4[sl][:, :, 512:512 + W1], in_=banks[b][:, 0:4 * W1].rearrange("p (q w) -> p q w", q=4)),
                          reads=[bk(bsm)], writes=[("A4", sl)])

            preload_lnexp()
            emit_sel(0)
            for j in range(8):
                dq = j % 2
                sl = j % 2
                jj = j % 4
                if jj == 0:
                    wps, wpv = load_chunk(wp_bf[l, j // 4], "wp", ("s_wp", l))
                if j + 1 < 8:
                    emit_sel(j + 1)
                bc = rot()
                for (oc, ln, ac) in si.segs:
                    for m in range(8):
                        for q in range(4):
                            P.add("pe", lambda e, bc=bc, wpv=wpv, jj=jj, m=m, q=q, oc=oc, ln=ln, ac=ac, sl=sl: e.matmul(
                                banks[bc][32 * q:32 * q + 32, oc:oc + ln], lhsT=wpv[:, jj, m, q, :], rhs=A4[sl][:, q, ac + 4 * m:ac + 4 * m + ln],
                                start=(m == 0), stop=(m == 7), tile_position=(0, 32 * q)),
                                  reads=[("ring", wps), ("A4", sl)], writes=[bk(bc)])
                if j >= 1:
                    conv_stats_mm(j - 1)
                cbias = vecs[:, j, 4 + l:5 + l]
                P.add("act", lambda e, bc=bc, j=j, cbias=cbias: e.activation(out=cbuf[:, j, 0:T], in_=banks[bc][:, 0:T], func=AF.Identity, bias=cbias, scale=1.0),
                      reads=[bk(bc), "vecs"], writes=[("cbuf", j)])
                P.add("act", lambda e, bc=bc, dq=dq, cbias=cbias: e.activation(out=sqt[dq][:, 0:T], in_=banks[bc][:, 0:T], func=AF.Square, bias=cbias, scale=1.0),
                      reads=[bk(bc), "vecs"], writes=[("sqt", dq)])
                P.add("dve", lambda e, dq=dq, j=j: e.tensor_copy(out=cbt[dq][:, 0:T], in_=cbuf[:, j, 0:T]), reads=[("cbuf", j)], writes=[("cbt", dq)])
            conv_stats_mm(7)
            tc, td = tmp_next(), tmp_next()
            P.add("act", lambda e: e.activation(out=meanb[:, 0:T], in_=banks[6][:, 0:T], func=AF.Copy, scale=inv_d), reads=[bk(6)], writes=["meanb"])
            P.add("dve", lambda e, tc=tc: e.tensor_tensor(out=tmps[tc][:, 0:T], in0=meanb[:, 0:T], in1=meanb[:, 0:T], op=ALU.mult),
                  reads=["meanb"], writes=[("tmp", tc)])
            P.add("dve", lambda e, tc=tc, td=td: e.scalar_tensor_tensor(out=tmps[td][:, 0:T], in0=banks[7][:, 0:T], scalar=inv_d, in1=tmps[tc][:, 0:T],
                                                                        op0=ALU.mult, op1=ALU.subtract),
                  reads=[bk(7), ("tmp", tc)], writes=[("tmp", td)])
            rsqrt_chain(tmps[td][:, 0:T], rstdb[:, 0:T], T, 1.0, [("tmp", td)], "rstdb")

            if nxt is not None:
                nl, nsi = nxt
                emit_pass_loads(nl, nsi)
                if nl == 0 and nsi.kind != "s":
                    for i in range(nsi.T // 128):
                        emit_x_load(nsi, i)

            pend = None
            for j in range(8):
                jj = j % 4
                if jj == 0:
                    zbch = win_chunk(5 * D + (j // 4) * 512)
                bz = rot()
                s, view = zbch
                for k in range(8):
                    P.add("pe", lambda e, bz=bz, k=k, view=view, jj=jj: e.matmul(banks[bz][:, 0:T], lhsT=view[:, k, jj * 128:(jj + 1) * 128], rhs=xnT[:, k, 0:T],
                                                                               start=(k == 0), stop=(k == 7)),
                          reads=[("xnT", k), ("ring", s)], writes=[bk(bz)])
                tb, t1, ty = tmp_next(), tmp_next(), tmp_next()
                P.add("act", lambda e, bz=bz, tb=tb: e.activation(out=tmps[tb][:, 0:T], in_=banks[bz][:, 0:T], func=AF.Silu),
                      reads=[bk(bz)], writes=[("tmp", tb)])
                P.add("dve", lambda e, j=j, t1=t1: e.tensor_tensor(out=tmps[t1][:, 0:T], in0=cbuf[:, j, 0:T], in1=meanb[:, 0:T], op=ALU.subtract),
                      reads=[("cbuf", j), "meanb"], writes=[("tmp", t1)])
                P.add("dve", lambda e, t1=t1: e.tensor_tensor(out=tmps[t1][:, 0:T], in0=tmps[t1][:, 0:T], in1=rstdb[:, 0:T], op=ALU.mult),
                      reads=[("tmp", t1), "rstdb"], writes=[("tmp", t1)])
                P.add("act", lambda e, j=j, t1=t1, ty=ty: e.activation(out=tmps[ty][:, 0:T], in_=tmps[t1][:, 0:T], func=AF.Silu,
                                                                       bias=vecs[:, j, 12 + l:13 + l], scale=vecs[:, j, 8 + l:9 + l]),
                      reads=[("tmp", t1), "vecs"], writes=[("tmp", ty)])
                if pend is not None:
                    pj, pty, ptb = pend
                    P.add("dve", lambda e, pj=pj, pty=pty, ptb=ptb: e.tensor_tensor(out=mT[:, 8 + pj, 0:T], in0=tmps[pty][:, 0:T], in1=tmps[ptb][:, 0:T], op=ALU.mult),
                          reads=[("tmp", pty), ("tmp", ptb)], writes=[("mT", 8 + pj)])
                pend = (j, ty, tb)
            pj, pty, ptb = pend
            P.add("dve", lambda e, pj=pj, pty=pty, ptb=ptb: e.tensor_tensor(out=mT[:, 8 + pj, 0:T], in0=tmps[pty][:, 0:T], in1=tmps[ptb][:, 0:T], op=ALU.mult),
                  reads=[("tmp", pty), ("tmp", ptb)], writes=[("mT", 8 + pj)])

            woutl = wout_bf[l].rearrange("(k p) d -> p k d", p=128)
            pending_stats = []
            preload_lnexp()

            def flush_stats():
                for (j, dq) in pending_stats:
                    P.add("pe", lambda e, dq=dq, j=j: e.matmul(banks[6][:, 0:T], lhsT=ones_bf[:], rhs=sqt[dq][:, 0:T], start=(j == 0), stop=(j == 7)),
                          reads=[("sqt", dq), "ones_bf"], writes=[bk(6)])
                del pending_stats[:]

            for dh in range(2):
                woch = [load_chunk(woutl[:, kh * 8:(kh + 1) * 8, dh * 512:(dh + 1) * 512], 8, ("s_wout", l, kh)) for kh in range(2)]
                bo = [rot() for _ in range(4)]
                for k in range(16):
                    s, view = woch[k // 8]
                    for jd in range(4):
                        P.add("pe", lambda e, b=bo[jd], k=k, view=view, jd=jd: e.matmul(banks[b][:, 0:T], lhsT=view[:, k % 8, jd * 128:(jd + 1) * 128], rhs=mT[:, k, 0:T],
                                                                                      start=(k == 0), stop=(k == 15)),
                              reads=[("mT", k), ("ring", s)], writes=[bk(bo[jd])])
                for jd in range(4):
                    j = dh * 4 + jd
                    dq = j % 2
                    if len(pending_stats) >= 2:
                        jj0, dq0 = pending_stats.pop(0)
                        P.add("pe", lambda e, dq=dq0, j=jj0: e.matmul(banks[6][:, 0:T], lhsT=ones_bf[:], rhs=sqt[dq][:, 0:T], start=(j == 0), stop=(j == 7)),
                              reads=[("sqt", dq0), "ones_bf"], writes=[bk(6)])
                    P.add("act", lambda e, b=bo[jd], j=j: e.activation(out=cbuf[:, j, 0:T], in_=banks[b][:, 0:T], func=AF.Copy),
                          reads=[bk(bo[jd])], writes=[("cbuf", j)])
                    P.add("act", lambda e, b=bo[jd], dq=dq: e.activation(out=sqt[dq][:, 0:T], in_=banks[b][:, 0:T], func=AF.Square),
                          reads=[bk(bo[jd])], writes=[("sqt", dq)])
                    pending_stats.append((j, dq))
            flush_stats()
            rsqrt_chain(banks[6][:, 0:T], rstdb[:, 0:T], T, inv_d, [bk(6)], "rstdb")
            for j in range(8):
                t1 = tmp_next()
                P.add("dve", lambda e, j=j, t1=t1: e.tensor_tensor(out=tmps[t1][:, 0:T], in0=cbuf[:, j, 0:T], in1=rstdb[:, 0:T], op=ALU.mult),
                      reads=[("cbuf", j), "rstdb"], writes=[("tmp", t1)])
                P.add("dve", lambda e, j=j, t1=t1: e.scalar_tensor_tensor(out=hT[:, j, 0:T], in0=tmps[t1][:, 0:T], scalar=vecs[:, j, 16 + l:17 + l], in1=hT[:, j, 0:T],
                                                                          op0=ALU.mult, op1=ALU.add),
                      reads=[("tmp", t1), "vecs", ("hT", j)], writes=[("hT", j)])
                P.add("act", lambda e, j=j: e.activation(out=xnT[:, j, 0:T], in_=hT[:, j, 0:T], func=AF.Copy),
                      reads=[("hT", j)], writes=[("xnT", j)])

            wpel = wpe_bf[l].rearrange("(k p) d -> p k d", p=128)
            wpgl = wpg_bf[l].rearrange("(k p) d -> p k d", p=128)
            pes, peview = load_chunk(wpel[:, :, :], 2, ("s_wpe", l))
            for jq in range(2):
                s, view = load_chunk(wpgl[:, :, jq * 512:(jq + 1) * 512], 8, ("s_wpg", l))
                bg = [rot() for _ in range(4)]
                for k in range(8):
                    for jd in range(4):
                        P.add("pe", lambda e, b=bg[jd], k=k, view=view, jd=jd: e.matmul(banks[b][:, 0:T], lhsT=view[:, k, jd * 128:(jd + 1) * 128], rhs=xnT[:, k, 0:T],
                                                                                      start=(k == 0), stop=(k == 7)),
                              reads=[("xnT", k), ("ring", s)], writes=[bk(bg[jd])])
                for jd in range(4):
                    j = jq * 4 + jd
                    bp = rot()
                    for kp in range(2):
                        P.add("pe", lambda e, bp=bp, kp=kp, j=j: e.matmul(banks[bp][:, 0:T], lhsT=peview[:, kp, j * 128:(j + 1) * 128], rhs=pT[:, kp, 0:T],
                                                                          start=(kp == 0), stop=(kp == 1)),
                              reads=[("pT", kp), ("ring", pes)], writes=[bk(bp)])
                    ta = tmp_next()
                    P.add("act", lambda e, b=bg[jd], ta=ta: e.activation(out=tmps[ta][:, 0:T], in_=banks[b][:, 0:T], func=AF.Sigmoid),
                          reads=[bk(bg[jd])], writes=[("tmp", ta)])
                    P.add("dve", lambda e, bp=bp, ta=ta: e.tensor_tensor(out=tmps[ta][:, 0:T], in0=banks[bp][:, 0:T], in1=tmps[ta][:, 0:T], op=ALU.mult),
                          reads=[bk(bp), ("tmp", ta)], writes=[("tmp", ta)])
                    P.add("dve", lambda e, j=j, ta=ta: e.tensor_tensor(out=hT[:, j, 0:T], in0=hT[:, j, 0:T], in1=tmps[ta][:, 0:T], op=ALU.add),
                          reads=[("hT", j), ("tmp", ta)], writes=[("hT", j)])

            if si.final:
                for q, (oc, ln, ac) in enumerate(si.segs):
                    for half in range(2):
                        b = rot()
                        for kk in range(4):
                            j = half * 4 + kk
                            P.add("pe", lambda e, b=b, j=j, kk=kk, q=q: e.transpose(banks[b][0:HIST, kk * 128:(kk + 1) * 128], afin[:, j, q, :], ident[:]),
                                  reads=["afin", "ident"], writes=[bk(b)])
                        P.add("act", lambda e, b=b, half=half: e.activation(out=yst[0:HIST, half * 512:(half + 1) * 512], in_=banks[b][0:HIST, :], func=AF.Copy),
                              reads=[bk(b)], writes=["yst"])
                    if si.kind == "p":
                        dst = ncp[l, si.b]
                    else:
                        dst = ncs[l, q]
                    P.add("pool", lambda e, dst=dst: e.dma_start(out=dst, in_=yst[0:HIST, :]), reads=["yst"], dma="o_cst")
                    out_keys.append("o_cst")

        st_list = []
        for b in range(2):
            for s in range(SEQ // ST_T):
                st_list.append(STInfo("p", b, s * ST_T, ST_T, [(0, ST_T, 0)], s == 0, s == SEQ // ST_T - 1))
        st_list.append(STInfo("s", 0, 0, 128, [(0, 64, 0), (64, 64, HIST + 64)], True, True))

        for idx_st, si in enumerate(st_list):
            T = si.T
            nt = T // 128
            if si.kind == "s":
                build_spatial(True)
            if idx_st == 0:
                emit_pass_loads(0, si)
            for i in range(nt):
                q = i
                xkey = "vst%d" % i
                if idx_st == 0 or si.kind == "s":
                    emit_x_load(si, i)
                for half in range(2):
                    b = rot()
                    for kk in range(4):
                        k = half * 4 + kk
                        P.add("pe", lambda e, b=b, k=k, kk=kk, q=q: e.transpose(banks[b][:, kk * 128:(kk + 1) * 128], vst[q][:, k * 128:(k + 1) * 128], ident[:]),
                              reads=[xkey, "ident"], writes=[bk(b)])
                    eng = "act" if half == 0 else "dve"
                    if eng == "act":
                        P.add("act", lambda e, b=b, half=half, i=i: e.activation(out=hT[:, half * 4:(half + 1) * 4, i * 128:(i + 1) * 128],
                                                                                 in_=banks[b][:, :].rearrange("p (k t) -> p k t", k=4), func=AF.Copy),
                              reads=[bk(b)], writes=[("hT", half * 4 + kk) for kk in range(4)])
                    else:
                        P.add("dve", lambda e, b=b, half=half, i=i: e.tensor_copy(out=hT[:, half * 4:(half + 1) * 4, i * 128:(i + 1) * 128],
                                                                                  in_=banks[b][:, :].rearrange("p (k t) -> p k t", k=4)),
                              reads=[bk(b)], writes=[("hT", half * 4 + kk) for kk in range(4)])
            for l in range(DEPTH):
                if l < DEPTH - 1:
                    nxt = (l + 1, si)
                elif idx_st + 1 < len(st_list):
                    nxt = (0, st_list[idx_st + 1])
                else:
                    nxt = None
                emit_pass(l, si, nxt)
            for i in range(nt):
                yb = yst2[i % 2]
                ykey = "yst" if i % 2 == 0 else "yst_b"
                for half in range(2):
                    b = rot()
                    for kk in range(4):
                        k = half * 4 + kk
                        P.add("pe", lambda e, b=b, k=k, kk=kk, i=i: e.transpose(banks[b][:, kk * 128:(kk + 1) * 128], hT[:, k, i * 128:(i + 1) * 128], ident[:]),
                              reads=[("hT", k), "ident"], writes=[bk(b)])
                    if half == 0:
                        P.add("act", lambda e, b=b, half=half, yb=yb: e.activation(out=yb[:, half * 512:(half + 1) * 512], in_=banks[b][:, :], func=AF.Copy),
                              reads=[bk(b)], writes=[ykey])
                    else:
                        P.add("dve", lambda e, b=b, half=half, yb=yb: e.tensor_copy(out=yb[:, half * 512:(half + 1) * 512], in_=banks[b][:, :]),
                              reads=[bk(b)], writes=[ykey])
                if si.kind == "p":
                    dst = y_p[si.b, si.t0 + i * 128:si.t0 + (i + 1) * 128, :]
                else:
                    dst = y_s.rearrange("b t d -> (b t) d")
                P.add("pool", lambda e, dst=dst, yb=yb: e.dma_start(out=dst, in_=yb[:, :]), reads=[ykey], dma="o_y%d" % (i % 2))
                out_keys.append("o_y%d" % (i % 2))

        P.emit(final_wait_bufs=sorted(set(out_keys), key=str))
    return nc


_CACHE = {}


def kernel(x_prompt, x_sample, state_conv, p_prompt, p_sample, g_pre, w_in, ln_v_g, ln_v_b, w_s, b_s,
           conv_w, conv_b, ln_c_g, ln_c_b, w_out, g_post, w_pe, w_pg):
    f = lambda a: np.ascontiguousarray(np.asarray(a, dtype=np.float32))
    x_prompt, x_sample, state_conv, p_prompt, p_sample = map(f, (x_prompt, x_sample, state_conv, p_prompt, p_sample))
    shared = {"g_pre": f(g_pre), "w_in": f(w_in), "ln_v_g": f(ln_v_g), "ln_v_b": f(ln_v_b), "w_s": f(w_s), "b_s": f(b_s),
              "conv_w": f(conv_w), "conv_b": f(conv_b), "ln_c_g": f(ln_c_g), "ln_c_b": f(ln_c_b), "w_out": f(w_out),
              "g_post": f(g_post), "w_pe": f(w_pe), "w_pg": f(w_pg)}
    if "nc" not in _CACHE:
        _CACHE["nc"] = build_program()
    nc = _CACHE["nc"]
    in_maps = []
    for c in range(NCORES):
        sl = slice(2 * c, 2 * c + 2)
        m = dict(shared)
        m["xp"] = np.ascontiguousarray(x_prompt[sl])
        m["xsm"] = np.ascontiguousarray(x_sample[sl])
        m["sc"] = np.ascontiguousarray(state_conv[:, sl])
        m["pp"] = np.ascontiguousarray(p_prompt[:, sl])
        m["psm"] = np.ascontiguousarray(p_sample[:, sl])
        in_maps.append(m)
    res = run_bass_kernel_spmd(nc, in_maps, core_ids=list(range(NCORES)))
    rs = res.results
    y_prompt = np.concatenate([np.asarray(r["y_p"], dtype=np.float32) for r in rs], axis=0)
    y_sample = np.concatenate([np.asarray(r["y_s"], dtype=np.float32) for r in rs], axis=0)
    new_conv_prompt = np.concatenate([np.asarray(r["ncp"], dtype=np.float32) for r in rs], axis=1)
    new_conv_sample = np.concatenate([np.asarray(r["ncs"], dtype=np.float32) for r in rs], axis=1)
    new_gmlp_v_sample = np.concatenate([np.asarray(r["vs_o"], dtype=np.float32) for r in rs], axis=1)
    return (y_prompt, y_sample, new_conv_prompt, new_conv_sample, new_gmlp_v_sample)
```

```python
import contextlib
import numpy as np
import concourse.bass as bass
import concourse.mybir as mybir
from concourse.bass_utils import run_bass_kernel_spmd

F32 = mybir.dt.float32
BF16 = mybir.dt.bfloat16
AF = mybir.ActivationFunctionType
ALU = mybir.AluOpType

NCORES = 8
D = 1024
DEPTH = 4
SEQ = 2048
DEC = 64
CW = 31
HIST = CW - 1
DPLE = 256
EPS = 1e-6
ST_T = 512
RING = 5

ENGS = ("pe", "act", "dve", "pool", "sp")


class Op:
    __slots__ = ("eng", "fn", "reads", "writes", "dma", "deps", "marked", "cnt", "idx", "dmaval")

    def __init__(self, eng, fn, reads, writes, dma):
        self.eng, self.fn, self.reads, self.writes, self.dma = eng, fn, reads, writes, dma
        self.deps = []
        self.marked = False
        self.cnt = 0
        self.dmaval = 0


class Prog:
    def __init__(self, nc):
        self.nc = nc
        self.ops = []

    def add(self, eng, fn, reads=(), writes=(), dma=None):
        o = Op(eng, fn, tuple(reads), tuple(writes), dma)
        o.idx = len(self.ops)
        self.ops.append(o)
        return o

    def analyze(self):
        last_w = {}
        readers = {}
        ops = self.ops
        for o in ops:
            deps = set()
            for b in o.reads:
                w = last_w.get(b)
                if w is not None:
                    deps.add(w)
            for b in o.writes:
                w = last_w.get(b)
                if w is not None:
                    deps.add(w)
                for r in readers.get(b, ()):
                    deps.add(r)
            deps.discard(o.idx)
            best = {}
            for d in deps:
                p = ops[d]
                if p.dma is not None:
                    o.deps.append(d)
                    continue
                if o.dma is None and p.eng == o.eng:
                    if p.eng == "pe":
                        continue
                    if not any(b in p.writes for b in o.reads):
                        continue
                if d > best.get(p.eng, -1):
                    best[p.eng] = d
            for d in best.values():
                o.deps.append(d)
                ops[d].marked = True
            for b in o.reads:
                readers.setdefault(b, []).append(o.idx)
            for b in o.writes:
                last_w[b] = o.idx
                readers[b] = []

    def emit(self, final_wait_bufs=()):
        nc = self.nc
        self.analyze()
        cnt = {e: 0 for e in ENGS}
        dmacnt = {}
        for o in self.ops:
            if o.dma is None:
                if o.marked:
                    cnt[o.eng] += 1
                o.cnt = cnt[o.eng]
            else:
                dmacnt[o.dma] = dmacnt.get(o.dma, 0) + 16
                o.dmaval = dmacnt[o.dma]
        ops = self.ops
        with contextlib.ExitStack() as st:
            esem = {e: st.enter_context(nc.semaphore("s_" + e)) for e in ENGS}
            dsem = {b: st.enter_context(nc.semaphore("d_%d" % i)) for i, b in enumerate(dmacnt)}
            block = st.enter_context(nc.Block())

            def mk(ename):
                def body(eng):
                    waited = {}
                    for o in ops:
                        if o.eng != ename:
                            continue
                        need = {}
                        for d in o.deps:
                            p = ops[d]
                            if p.dma is None:
                                s, v = esem[p.eng], p.cnt
                            else:
                                s, v = dsem[p.dma], p.dmaval
                            k = id(s)
                            if v > need.get(k, (None, 0))[1]:
                                need[k] = (s, v)
                        for k, (s, v) in need.items():
                            if waited.get(k, 0) >= v:
                                continue
                            eng.wait_ge(s, v)
                            waited[k] = v
                        ins = o.fn(eng)
                        if o.dma is not None:
                            ins.then_inc(dsem[o.dma], 16)
                        elif o.marked:
                            ins.then_inc(esem[ename], 1)
                    if ename == "sp":
                        for b in final_wait_bufs:
                            eng.wait_ge(dsem[b], dmacnt[b])
                return body

            block.tensor(mk("pe"))
            block.scalar(mk("act"))
            block.vector(mk("dve"))
            block.gpsimd(mk("pool"))
            block.sync(mk("sp"))


class STInfo:
    def __init__(self, kind, b, t0, T, segs, seq_start, final):
        self.kind, self.b, self.t0, self.T, self.segs = kind, b, t0, T, segs
        self.seq_start, self.final = seq_start, final


def build_program():
    nc = bass.Bass("TRN2", target_bir_lowering=False)

    def din(name, shape):
        return nc.dram_tensor(name, list(shape), F32, kind="ExternalInput").ap()

    def dout(name, shape):
        return nc.dram_tensor(name, list(shape), F32, kind="ExternalOutput").ap()

    xp = din("xp", [2, SEQ, D])
    xsm = din("xsm", [2, DEC, D])
    sc = din("sc", [DEPTH, 2, HIST, D])
    pp = din("pp", [DEPTH, 2, SEQ, DPLE])
    psm = din("psm", [DEPTH, 2, DEC, DPLE])
    g_pre = din("g_pre", [DEPTH, D])
    w_in = din("w_in", [DEPTH, D, 6 * D])
    ln_v_g = din("ln_v_g", [DEPTH, D])
    ln_v_b = din("ln_v_b", [DEPTH, D])
    w_s = din("w_s", [DEPTH, 8, 128, 128])
    b_s = din("b_s", [DEPTH, 8, 128])
    conv_w = din("conv_w", [DEPTH, CW, D])
    conv_b = din("conv_b", [DEPTH, D])
    ln_c_g = din("ln_c_g", [DEPTH, D])
    ln_c_b = din("ln_c_b", [DEPTH, D])
    w_out = din("w_out", [DEPTH, 2 * D, D])
    g_post = din("g_post", [DEPTH, D])
    w_pe = din("w_pe", [DEPTH, DPLE, D])
    w_pg = din("w_pg", [DEPTH, D, D])

    y_p = dout("y_p", [2, SEQ, D])
    y_s = dout("y_s", [2, DEC, D])
    ncp = dout("ncp", [DEPTH, 2, HIST, D])
    ncs = dout("ncs", [DEPTH, 2, HIST, D])
    vs_o = dout("vs_o", [DEPTH, 2, DEC, D])

    win_bf = nc.dram_tensor("win_bf", [DEPTH, D, 6 * D], BF16, kind="Internal").ap()
    wout_bf = nc.dram_tensor("wout_bf", [DEPTH, 2 * D, D], BF16, kind="Internal").ap()
    wpg_bf = nc.dram_tensor("wpg_bf", [DEPTH, D, D], BF16, kind="Internal").ap()
    wpe_bf = nc.dram_tensor("wpe_bf", [DEPTH, DPLE, D], BF16, kind="Internal").ap()
    wp_bf = nc.dram_tensor("wp_bf", [DEPTH, 2, 128, 4096], BF16, kind="Internal").ap()

    with contextlib.ExitStack() as es:
        def sb(name, shape, dt):
            return es.enter_context(nc.sbuf_tensor(name, list(shape), dt))

        ident = sb("ident", [128, 128], F32)
        identb = sb("identb", [128, 128], BF16)
        ones_bf = sb("ones_bf", [128, 128], BF16)
        ones2 = sb("ones2", [128, 128], BF16)
        epst = sb("epst", [128, 1], F32)
        dummy = sb("dummy_ln", [128, 2], F32)
        vecs = sb("vecs", [128, 8, 20], F32)
        cwT = sb("cwT", [128, 8, 128], F32)
        wsT = sb("wsT", [128, DEPTH, 8, 128], BF16)
        Bs = sb("Bs", [128, DEPTH, 8, 128], BF16)
        ring = [sb("ring%d" % i, [128, 4096], BF16) for i in range(RING)]
        hT = sb("hT", [128, 8, ST_T], F32)
        yst = sb("yst", [128, D], F32)
        yst2 = [yst, sb("yst_b", [128, D], F32)]
        xnT = sb("xnT", [128, 8, ST_T], BF16)
        sqt = [sb("sqt%d" % i, [128, ST_T], BF16) for i in range(2)]
        vst = [sb("vst%d" % i, [128, D], F32) for i in range(4)]
        vhat = sb("vhat", [128, 4, D], BF16)
        lnv = sb("lnv", [128, 2, D], F32)
        bnst = [sb("bnst%d" % i, [128, 2, 6], F32) for i in range(4)]
        mv4 = sb("mv4", [128, 4, 2], F32)
        sd4 = sb("sd4", [128, 4], F32)
        rs4 = sb("rs4", [128, 4], F32)
        NTMP = 6
        tmps = [sb("tmp%d" % i, [128, ST_T], F32) for i in range(NTMP)]
        rstdb = sb("rstdb", [128, ST_T], F32)
        meanb = sb("meanb", [128, ST_T], F32)
        ATW = HIST + ST_T + 1
        aT = sb("aT", [128, 8, ATW], BF16)
        tails = sb("tails", [128, DEPTH, 8, HIST], BF16)
        afin = sb("afin", [128, 8, 2, HIST], F32)
        mT = sb("mT", [128, 16, ST_T], BF16)
        cbuf = sb("cbuf", [128, 8, ST_T], F32)
        cbt = [sb("cbt%d" % i, [128, ST_T], BF16) for i in range(2)]
        pst = sb("pst", [128, 4, DPLE], F32)
        pT = sb("pT", [128, 2, ST_T], BF16)
        A4 = [sb("A4_%d" % i, [128, 4, 540], BF16) for i in range(2)]
        m4 = sb("m4", [128, 4], F32)
        irep = sb("irep", [128, 32], F32)
        banks = [es.enter_context(nc.psum_tensor("bank%d" % i, [128, 512], F32)) for i in range(8)]

        P = Prog(nc)
        state = {"rot": 0, "tmp": 0, "chunk": 0}
        out_keys = []

        def rot():
            b = state["rot"] % 6
            state["rot"] += 1
            return b

        def tmp_next():
            i = state["tmp"] % NTMP
            state["tmp"] += 1
            return i

        def bk(b):
            return ("bank", b)

        def ring_view(s, k):
            if k == 8:
                return ring[s][:, :].rearrange("p (k e) -> p k e", k=8)
            if k == "wp":
                return ring[s][:, :].rearrange("p (j m q c) -> p j m q c", j=4, m=8, q=4)
            return ring[s][:, 0:2048].rearrange("p (k e) -> p k e", k=2)

        def load_chunk(src_ap, k, key):
            s = state["chunk"] % RING
            state["chunk"] += 1
            view = ring_view(s, k)
            dst = ring[s][:, :] if k == "wp" else view
            P.add("sp", lambda e, dst=dst, src=src_ap: e.dma_start(out=dst, in_=src),
                  reads=[key], writes=[("ring", s)], dma=("ring", s))
            return s, view

        P.add("dve", lambda e: e.memset(ident[:], 0.0), writes=["ident"])
        P.add("pool", lambda e: e.affine_select(out=ident[:], in_=ident[:], pattern=[[-1, 128]], compare_op=ALU.not_equal,
                                                fill=1.0, base=0, channel_multiplier=1), reads=["ident"], writes=["ident"])
        P.add("dve", lambda e: e.tensor_copy(out=identb[:], in_=ident[:]), reads=["ident"], writes=["identb"])
        P.add("dve", lambda e: e.memset(ones_bf[:], 1.0), writes=["ones_bf"])
        P.add("dve", lambda e: e.memset(ones2[:], 0.0), writes=["ones2"])
        P.add("dve", lambda e: e.memset(ones2[0:2, :], 1.0), writes=["ones2"])
        P.add("dve", lambda e: e.memset(epst[:], EPS), writes=["epst"])

        def cast_dma(dst, src, key, rows_e):
            P.add("pool", lambda e: e.dma_start(out=dst.rearrange("r (a e) -> (r a) e", e=rows_e),
                                                in_=src.rearrange("r (a e) -> (r a) e", e=rows_e)),
                  writes=[key], dma=key)

        def emit_casts(l):
            for g in (1, 3, 4, 0, 2, 5):
                P.add("pool", lambda e, l=l, g=g: e.dma_start(out=win_bf[l][:, g * D:(g + 1) * D], in_=w_in[l][:, g * D:(g + 1) * D]),
                      writes=[("s_win", l, g)], dma=("s_win", l, g))
            cast_dma(wout_bf[l], w_out[l], ("s_wout", l), 1024)
            cast_dma(wpg_bf[l], w_pg[l], ("s_wpg", l), 1024)
            cast_dma(wpe_bf[l], w_pe[l], ("s_wpe", l), 1024)

        for l_ in range(DEPTH):
            emit_casts(l_)

        for r, src in enumerate([g_pre, conv_b, ln_c_g, ln_c_b, g_post]):
            P.add("act", lambda e, r=r, src=src: e.dma_start(out=vst[0][4 * r:4 * r + 4, :], in_=src),
                  writes=["vst0"], dma="vst0")
        P.add("act", lambda e: e.dma_start(out=vst[1][0:DEPTH * CW, :], in_=conv_w.rearrange("l k c -> (l k) c")),
              writes=["vst1"], dma="vst1")
        for half in range(2):
            b = rot()
            for kk in range(4):
                k = half * 4 + kk
                P.add("pe", lambda e, b=b, k=k, kk=kk: e.transpose(banks[b][:, kk * 20:(kk + 1) * 20], vst[0][0:20, k * 128:(k + 1) * 128], ident[0:20, 0:20]),
                      reads=["vst0", "ident"], writes=[bk(b)])
            P.add("dve", lambda e, b=b, half=half: e.tensor_copy(out=vecs[:, half * 4:(half + 1) * 4, :],
                                                                 in_=banks[b][:, 0:80].rearrange("p (k r) -> p k r", k=4)),
                  reads=[bk(b)], writes=["vecs"])
        NCW = DEPTH * CW
        P.add("dve", lambda e: e.memset(cwT[:, :, :], 0.0), writes=["cwT"])
        for half in range(2):
            b = rot()
            for kk in range(4):
                k = half * 4 + kk
                P.add("pe", lambda e, b=b, k=k, kk=kk: e.transpose(banks[b][:, kk * 128:kk * 128 + NCW], vst[1][0:NCW, k * 128:(k + 1) * 128], ident[0:NCW, 0:NCW]),
                      reads=["vst1", "ident"], writes=[bk(b)])
            P.add("dve", lambda e, b=b, half=half: e.tensor_copy(out=cwT[:, half * 4:(half + 1) * 4, 0:NCW],
                                                                 in_=banks[b][:, :].rearrange("p (k r) -> p k r", k=4)[:, :, 0:NCW]),
                  reads=[bk(b)], writes=["cwT"])

        def build_spatial(sample, dq="pool"):
            for l in range(DEPTH):
                stg = vst[l % 2]
                skey = "vst%d" % (l % 2)
                sview = stg[:, :].rearrange("p (h s) -> p h s", h=8)
                if not sample:
                    P.add(dq, lambda e, l=l, sview=sview: e.dma_start(out=sview, in_=w_s[l].rearrange("h t s -> t h s")),
                          writes=[skey], dma=skey)
                else:
                    P.add("dve", lambda e, stg=stg: e.memset(stg[:, :], 0.0), writes=[skey])
                    P.add(dq, lambda e, l=l, sview=sview: e.dma_start(out=sview[0:64, :, 0:64], in_=w_s[l][:, 0:64, 0:64].rearrange("h t s -> t h s")),
                          reads=[skey], writes=[skey], dma=skey)
                    P.add(dq, lambda e, l=l, sview=sview: e.dma_start(out=sview[64:128, :, 64:128], in_=w_s[l][:, 0:64, 0:64].rearrange("h t s -> t h s")),
                          reads=[skey], writes=[skey], dma=skey)
                for half in range(2):
                    b = rot()
                    for hh in range(4):
                        h = half * 4 + hh
                        P.add("pe", lambda e, b=b, h=h, hh=hh, sview=sview: e.transpose(banks[b][:, hh * 128:(hh + 1) * 128], sview[:, h, :], ident[:]),
                              reads=[skey, "ident"], writes=[bk(b)])
                    P.add("act", lambda e, b=b, l=l, half=half: e.activation(out=wsT[:, l, half * 4:(half + 1) * 4, :],
                                                                             in_=banks[b][:, :].rearrange("p (h t) -> p h t", h=4), func=AF.Copy),
                          reads=[bk(b)], writes=[("wsT", l)])
                if not sample:
                    P.add("dve", lambda e, l=l: e.memset(wsT[64:128, l, :, 0:64], 0.0), reads=[("wsT", l)], writes=[("wsT", l)])
                bt = yst
                btv = bt[:, :].rearrange("p (h t) -> p h t", h=8)
                if not sample:
                    P.add(dq, lambda e, l=l, btv=btv: e.dma_start(out=btv, in_=b_s[l].partition_broadcast(128)),
                          writes=["yst"], dma="yst")
                else:
                    for q in range(2):
                        P.add(dq, lambda e, l=l, btv=btv, q=q: e.dma_start(out=btv[:, :, q * 64:(q + 1) * 64], in_=b_s[l][:, 0:64].partition_broadcast(128)),
                              writes=["yst"], dma="yst")
                ti = tmp_next()
                hiv = tmps[ti][:, :].bitcast(BF16)[:, 0:1024]
                P.add("dve", lambda e, hiv=hiv, bt=bt: e.tensor_copy(out=hiv, in_=bt[:, :]), reads=["yst"], writes=[("tmp", ti)])
                tj = tmp_next()
                tk = tmp_next()
                lov = tmps[tj]
                lov2 = tmps[tk]
                P.add("dve", lambda e, hiv=hiv, bt=bt, lov=lov: e.tensor_tensor(out=lov[:, :], in0=bt[:, 0:512], in1=hiv[:, 0:512], op=ALU.subtract),
                      reads=["yst", ("tmp", ti)], writes=[("tmp", tj)])
                P.add("dve", lambda e, hiv=hiv, bt=bt, lov2=lov2: e.tensor_tensor(out=lov2[:, :], in0=bt[:, 512:1024], in1=hiv[:, 512:1024], op=ALU.subtract),
                      reads=["yst", ("tmp", ti)], writes=[("tmp", tk)])
                Bv = Bs[:, l, :, :].rearrange("p h t -> p (h t)")
                P.add("dve", lambda e, Bv=Bv, hiv=hiv: e.tensor_scalar(out=Bv, in0=hiv, scalar1=ident[:, 0:1], scalar2=None, op0=ALU.mult),
                      reads=[("tmp", ti), "ident"], writes=[("Bs", l)])
                P.add("dve", lambda e, Bv=Bv, lov=lov: e.scalar_tensor_tensor(out=Bv[:, 0:512], in0=lov[:, :], scalar=ident[:, 1:2], in1=Bv[:, 0:512], op0=ALU.mult, op1=ALU.add),
                      reads=[("tmp", tj), ("Bs", l), "ident"], writes=[("Bs", l)])
                P.add("dve", lambda e, Bv=Bv, lov2=lov2: e.scalar_tensor_tensor(out=Bv[:, 512:1024], in0=lov2[:, :], scalar=ident[:, 1:2], in1=Bv[:, 512:1024], op0=ALU.mult, op1=ALU.add),
                      reads=[("tmp", tk), ("Bs", l), "ident"], writes=[("Bs", l)])

        build_spatial(False, "act")

        P.add("dve", lambda e: e.memset(aT[:, :, :], 0.0), writes=["aTh"] + [("aT", j) for j in range(8)])
        P.add("dve", lambda e: e.reduce_sum(out=m4[:, :], in_=ident[:, :].rearrange("p (i c) -> p i c", i=4), axis=mybir.AxisListType.X),
              reads=["ident"], writes=["m4"])
        P.add("dve", lambda e: e.reduce_sum(out=irep[:, :], in_=ident[:, :].rearrange("p (i c) -> p c i", i=4), axis=mybir.AxisListType.X),
              reads=["ident"], writes=["irep"])
        cb_keys = [("cbuf", j) for j in range(8)]
        Rflat = cbuf[:, :, :].rearrange("p a b -> p (a b)")
        Rpad = Rflat.rearrange("p (q g k) -> p q g k", q=4, g=32)
        P.add("dve", lambda e: e.memset(Rflat, 0.0), writes=cb_keys)
        cw_flat = cwT[:, :, :].rearrange("p j x -> p (j x)")
        for q in range(4):
            ti = tmp_next()
            selv = tmps[ti][:, 0:128]
            P.add("dve", lambda e, selv=selv, q=q: e.tensor_copy(out=selv.rearrange("p (i c) -> p i c", i=4),
                                                                 in_=ident[:, 32 * q:32 * q + 32].unsqueeze(1).broadcast_to([128, 4, 32])),
                  reads=["ident"], writes=[("tmp", ti)])
            for hh in range(2):
                b_ = rot()
                P.add("pe", lambda e, b_=b_, selv=selv, hh=hh: e.matmul(banks[b_][:, 0:512], lhsT=selv, rhs=cw_flat[:, hh * 512:(hh + 1) * 512], start=True, stop=True),
                      reads=[("tmp", ti), "cwT"], writes=[bk(b_)])
                P.add("act", lambda e, b_=b_, q=q, hh=hh: e.activation(
                    out=Rpad[:, q, hh * 16:(hh + 1) * 16, 0:CW].rearrange("p (j l) k -> p j l k", j=4),
                    in_=banks[b_][:, 0:512].rearrange("p (j x) -> p j x", j=4)[:, :, 0:NCW].rearrange("p j (l k) -> p j l k", l=DEPTH), func=AF.Copy),
                      reads=[bk(b_)], writes=cb_keys)
        Vflat = cwT[:, :, :].rearrange("p j x -> p (j x)")
        V2 = Vflat.rearrange("p (g m) -> p g m", m=8)
        R4 = Rflat.rearrange("p (g m i) -> p g m i", m=8, i=4)
        P.add("dve", lambda e: e.tensor_scalar(out=V2, in0=R4[:, :, :, 0], scalar1=m4[:, 0:1], scalar2=None, op0=ALU.mult),
              reads=cb_keys + ["m4"], writes=["cwT"])
        for ii in range(1, 4):
            P.add("dve", lambda e, ii=ii: e.scalar_tensor_tensor(out=V2, in0=R4[:, :, :, ii], scalar=m4[:, ii:ii + 1], in1=V2, op0=ALU.mult, op1=ALU.add),
                  reads=cb_keys + ["m4", "cwT"], writes=["cwT"])
        V5 = Vflat.rearrange("p (q j l m) -> p q j l m", q=4, j=8, l=DEPTH)
        mT_flat = mT[:, :, :].rearrange("p a b -> p (a b)")
        nstg = 0
        for l in range(DEPTH):
            for jh in range(2):
                hq = nstg % 2
                nstg += 1
                stg = mT_flat[:, hq * 4096:(hq + 1) * 4096]
                stg5 = stg.rearrange("p (j m q c) -> p j m q c", j=4, m=8, q=4)
                keys = [("mT", jj) for jj in range(hq * 8, hq * 8 + 8)]
                for jj in range(4):
                    j = jh * 4 + jj
                    P.add("pool", lambda e, stg5=stg5, jj=jj, j=j, l=l: e.tensor_tensor(
                        out=stg5[:, jj, :, :, :],
                        in0=V5[:, :, j, l, :].rearrange("p q m -> p m q").unsqueeze(3).broadcast_to([128, 8, 4, 32]),
                        in1=irep[:, :].unsqueeze(1).unsqueeze(1).broadcast_to([128, 8, 4, 32]), op=ALU.mult),
                          reads=["cwT", "irep"], writes=keys)
                P.add("act", lambda e, stg=stg, l=l, jh=jh: e.dma_start(out=wp_bf[l, jh], in_=stg), reads=keys, writes=[("s_wp", l)], dma=("s_wp", l))

        def emit_pass_loads(l, si):
            T = si.T
            nt = T // 128
            if si.kind == "p":
                psrc = pp[l, si.b, si.t0:si.t0 + T, :].rearrange("(i p) e -> p i e", p=128)
                P.add("sp", lambda e, psrc=psrc: e.dma_start(out=pst[:, 0:nt, :], in_=psrc), writes=["pst"], dma="pst")
            else:
                P.add("sp", lambda e: e.dma_start(out=pst[:, 0, :], in_=psm[l].rearrange("b t e -> (b t) e")), writes=["pst"], dma="pst")
            P.add("sp", lambda e: e.dma_start(out=lnv[:, 0, :], in_=ln_v_g[l].partition_broadcast(128)), writes=["lnv"], dma="lnv")
            P.add("sp", lambda e: e.dma_start(out=lnv[:, 1, :], in_=ln_v_b[l].partition_broadcast(128)), writes=["lnv"], dma="lnv")

        def emit_x_load(si, i):
            q = i % 2
            if si.kind == "p":
                src = xp[si.b, si.t0 + i * 128:si.t0 + (i + 1) * 128, :]
            else:
                src = xsm.rearrange("b t d -> (b t) d")
            P.add("sp", lambda e, i=i, src=src: e.dma_start(out=vst[i][:, :], in_=src), writes=["vst%d" % i], dma="vst%d" % i)

        def rsqrt_chain(src_ap, dst_ap, T, scale, rkeys, wkey):
            t0 = tmp_next()
            P.add("act", lambda e, t0=t0: e.activation(out=tmps[t0][:, 0:T], in_=src_ap, func=AF.Ln, bias=epst[:, 0:1], scale=scale),
                  reads=list(rkeys) + ["epst"], writes=[("tmp", t0)])
            P.add("act", lambda e, t0=t0: e.activation(out=dst_ap, in_=tmps[t0][:, 0:T], func=AF.Exp, scale=-0.5),
                  reads=[("tmp", t0)], writes=[wkey])

        def preload_lnexp():
            P.add("act", lambda e: e.activation(out=dummy[:, 0:1], in_=epst[:, 0:1], func=AF.Ln), reads=["epst"], writes=["dummy"])

        def emit_pass(l, si, nxt):
            T = si.T
            nt = T // 128
            inv_d = 1.0 / D
            winl = win_bf[l].rearrange("(k p) e -> p k e", p=128)

            def win_chunk(c0):
                return load_chunk(winl[:, :, c0:c0 + 512], 8, ("s_win", l, c0 // D))

            if si.kind == "p":
                if si.seq_start:
                    P.add("dve", lambda e: e.memset(aT[:, :, 0:HIST], 0.0), writes=["aTh"])
                else:
                    P.add("act", lambda e: e.activation(out=aT[:, :, 0:HIST], in_=tails[:, l, :, :], func=AF.Copy),
                          reads=[("tails", l)], writes=["aTh"])
            else:
                for q, (oc, ln, ac) in enumerate(si.segs):
                    sbuf_q = yst2[q % 2]
                    skey_q = "yst" if q % 2 == 0 else "yst_b"
                    P.add("pool", lambda e, q=q, sbuf_q=sbuf_q: e.dma_start(out=sbuf_q[0:HIST, :], in_=sc[l, q]), writes=[skey_q], dma=skey_q)
                    for half in range(2):
                        b = rot()
                        for kk in range(4):
                            k = half * 4 + kk
                            P.add("pe", lambda e, b=b, k=k, kk=kk, sbuf_q=sbuf_q: e.transpose(banks[b][:, kk * HIST:(kk + 1) * HIST], sbuf_q[0:HIST, k * 128:(k + 1) * 128], ident[0:HIST, 0:HIST]),
                                  reads=[skey_q, "ident"], writes=[bk(b)])
                        P.add("act", lambda e, b=b, half=half, ac=ac: e.activation(out=aT[:, half * 4:(half + 1) * 4, ac:ac + HIST],
                                                                                   in_=banks[b][:, 0:4 * HIST].rearrange("p (k r) -> p k r", k=4), func=AF.Copy),
                              reads=[bk(b)], writes=["aTh"])

            preload_lnexp()
            for k in range(8):
                q = k % 2
                P.add("act", lambda e, k=k, q=q: e.activation(out=sqt[q][:, 0:T], in_=hT[:, k, 0:T], func=AF.Square),
                      reads=[("hT", k)], writes=[("sqt", q)])
                P.add("pe", lambda e, k=k, q=q: e.matmul(banks[6][:, 0:T], lhsT=ones_bf[:], rhs=sqt[q][:, 0:T], start=(k == 0), stop=(k == 7)),
                      reads=[("sqt", q), "ones_bf"], writes=[bk(6)])
            rsqrt_chain(banks[6][:, 0:T], rstdb[:, 0:T], T, inv_d, [bk(6)], "rstdb")
            for kp in range(2):
                b = rot()
                for i in range(nt):
                    P.add("pe", lambda e, b=b, i=i, kp=kp: e.transpose(banks[b][:, i * 128:(i + 1) * 128], pst[:, i, kp * 128:(kp + 1) * 128], ident[:]),
                          reads=["pst", "ident"], writes=[bk(b)])
                P.add("act", lambda e, b=b, kp=kp: e.activation(out=pT[:, kp, 0:T], in_=banks[b][:, 0:T], func=AF.Copy),
                      reads=[bk(b)], writes=[("pT", kp)])

            vch = [win_chunk(D), win_chunk(D + 512)]
            groups = [(i, half) for i in range(nt) for half in range(2)]
            for g0 in range(0, len(groups), 4):
                grp = groups[g0:g0 + 4]
                bs_ = [rot() for _ in grp]
                for k in range(8):
                    if g0 == 0:
                        P.add("dve", lambda e, k=k: e.scalar_tensor_tensor(out=xnT[:, k, 0:T], in0=hT[:, k, 0:T], scalar=vecs[:, k, l:l + 1], in1=rstdb[:, 0:T],
                                                                           op0=ALU.mult, op1=ALU.mult),
                              reads=[("hT", k), "vecs", "rstdb"], writes=[("xnT", k)])
                    for (i, half), b in zip(grp, bs_):
                        s, view = vch[half]
                        P.add("pe", lambda e, b=b, k=k, i=i, view=view: e.matmul(banks[b][:, :], lhsT=xnT[:, k, i * 128:(i + 1) * 128], rhs=view[:, k, :],
                                                                                start=(k == 0), stop=(k == 7)),
                              reads=[("xnT", k), ("ring", s)], writes=[bk(b)])
                for (i, half), b in zip(grp, bs_):
                    P.add("act", lambda e, b=b, i=i, half=half: e.activation(out=vst[i][:, half * 512:(half + 1) * 512], in_=banks[b][:, :], func=AF.Gelu_apprx_tanh),
                          reads=[bk(b)], writes=["vst%d" % i])
                for i in sorted(set(i for i, _ in grp)):
                    vkey = "vst%d" % i
                    for half in range(2):
                        P.add("dve", lambda e, i=i, half=half: e.bn_stats(out=bnst[i][:, half, :], in_=vst[i][:, half * 512:(half + 1) * 512]),
                              reads=[vkey], writes=[("bnst", i)])
                    P.add("dve", lambda e, i=i: e.bn_aggr(out=mv4[:, i, :], in_=bnst[i][:, :, :].rearrange("p a b -> p (a b)")),
                          reads=[("bnst", i)], writes=["mv4"])
            P.add("act", lambda e: e.activation(out=sd4[:, 0:nt], in_=mv4[:, 0:nt, 1], func=AF.Ln, bias=epst[:, 0:1], scale=1.0),
                  reads=["mv4", "epst"], writes=["sd4"])
            P.add("act", lambda e: e.activation(out=rs4[:, 0:nt], in_=sd4[:, 0:nt], func=AF.Exp, scale=-0.5), reads=["sd4"], writes=["rs4"])
            deferred = []
            for i in range(nt):
              def _norm(i=i):
                  vkey = "vst%d" % i
                  P.add("dve", lambda e, i=i: e.tensor_scalar(out=vst[i][:, :], in0=vst[i][:, :], scalar1=mv4[:, i, 0:1], scalar2=rs4[:, i:i + 1],
                                                              op0=ALU.subtract, op1=ALU.mult),
                        reads=[vkey, "mv4", "rs4"], writes=[vkey])
                  P.add("dve", lambda e, i=i: e.tensor_tensor(out=vst[i][:, :], in0=vst[i][:, :], in1=lnv[:, 0, :], op=ALU.mult),
                        reads=[vkey, "lnv"], writes=[vkey])
                  if si.kind == "p":
                      P.add("dve", lambda e, i=i: e.tensor_tensor(out=vhat[:, i, :], in0=vst[i][:, :], in1=lnv[:, 1, :], op=ALU.add),
                            reads=[vkey, "lnv"], writes=[("vhat", i)])
                  else:
                      P.add("dve", lambda e, i=i: e.tensor_tensor(out=vst[i][:, :], in0=vst[i][:, :], in1=lnv[:, 1, :], op=ALU.add),
                            reads=[vkey, "lnv"], writes=[vkey])
                      P.add("pool", lambda e, i=i: e.dma_start(out=vs_o[l].rearrange("b t c -> (b t) c"), in_=vst[i][:, :]),
                            reads=[vkey], dma=("o_vs", i))
                      out_keys.append(("o_vs", i))
                      P.add("act", lambda e, i=i: e.activation(out=vhat[:, i, :], in_=vst[i][:, :], func=AF.Copy),
                            reads=[vkey], writes=[("vhat", i)])
              deferred.append(_norm)

            for j in range(8):
                jj = j % 4
                if jj == 0:
                    lch = win_chunk(3 * D + (j // 4) * 512)
                    gch = win_chunk(4 * D + (j // 4) * 512)
                bl, bg = rot(), rot()
                for (b, (s, view)) in ((bl, lch), (bg, gch)):
                    for k in range(8):
                        P.add("pe", lambda e, b=b, k=k, view=view, jj=jj: e.matmul(banks[b][:, 0:T], lhsT=view[:, k, jj * 128:(jj + 1) * 128], rhs=xnT[:, k, 0:T],
                                                                                 start=(k == 0), stop=(k == 7)),
                              reads=[("xnT", k), ("ring", s)], writes=[bk(b)])
                ta = tmp_next()
                P.add("act", lambda e, bg=bg, ta=ta: e.activation(out=tmps[ta][:, 0:T], in_=banks[bg][:, 0:T], func=AF.Sigmoid),
                      reads=[bk(bg)], writes=[("tmp", ta)])
                for q, (oc, ln, ac) in enumerate(si.segs):
                    P.add("dve", lambda e, bl=bl, ta=ta, j=j, oc=oc, ln=ln, ac=ac: e.tensor_tensor(out=aT[:, j, ac + HIST:ac + HIST + ln], in0=banks[bl][:, oc:oc + ln],
                                                                                                  in1=tmps[ta][:, oc:oc + ln], op=ALU.mult),
                          reads=[bk(bl), ("tmp", ta)], writes=[("aT", j)])
                    if si.final:
                        P.add("dve", lambda e, bl=bl, ta=ta, j=j, oc=oc, ln=ln, q=q: e.tensor_tensor(out=afin[:, j, q, :], in0=banks[bl][:, oc + ln - HIST:oc + ln],
                                                                                                    in1=tmps[ta][:, oc + ln - HIST:oc + ln], op=ALU.mult),
                              reads=[bk(bl), ("tmp", ta)], writes=["afin"])
                if j % 2 == 1 and deferred:
                    deferred.pop(0)()
            while deferred:
                deferred.pop(0)()
            if si.kind == "p" and not si.final:
                P.add("act", lambda e: e.activation(out=tails[:, l, :, :], in_=aT[:, :, T:T + HIST], func=AF.Copy),
                      reads=[("aT", j) for j in range(8)], writes=[("tails", l)])

            for jp in range(4):
                js = (2 * jp, 2 * jp + 1)
                if jp % 2 == 0:
                    uch = win_chunk((jp // 2) * 512)
                    zch = win_chunk(2 * D + (jp // 2) * 512)
                bu = [rot(), rot()]
                bz = [rot(), rot()]
                bm = [rot(), rot()]
                for (bb, (s, view)) in ((bu, uch), (bz, zch)):
                    for idx, j in enumerate(js):
                        jj = j % 4
                        b = bb[idx]
                        for k in range(8):
                            P.add("pe", lambda e, b=b, k=k, view=view, jj=jj: e.matmul(banks[b][:, 0:T], lhsT=view[:, k, jj * 128:(jj + 1) * 128], rhs=xnT[:, k, 0:T],
                                                                                     start=(k == 0), stop=(k == 7)),
                                  reads=[("xnT", k), ("ring", s)], writes=[bk(b)])
                for idx, j in enumerate(js):
                    b = bm[idx]
                    for i in range(nt):
                        P.add("pe", lambda e, b=b, i=i, j=j: e.matmul(banks[b][:, i * 128:(i + 1) * 128], lhsT=vhat[:, i, j * 128:(j + 1) * 128], rhs=wsT[:, l, j, :],
                                                                      start=True, stop=False),
                              reads=[("vhat", i), ("wsT", l)], writes=[bk(b)])
                        P.add("pe", lambda e, b=b, i=i, j=j: e.matmul(banks[b][:, i * 128:(i + 1) * 128], lhsT=ones2[:], rhs=Bs[:, l, j, :],
                                                                      start=False, stop=True),
                              reads=["ones2", ("Bs", l)], writes=[bk(b)])
                tas = [tmp_next(), tmp_next()]
                tbs = [tmp_next(), tmp_next()]
                for idx in range(2):
                    P.add("act", lambda e, b=bu[idx], ta=tas[idx]: e.activation(out=tmps[ta][:, 0:T], in_=banks[b][:, 0:T], func=AF.Gelu_apprx_tanh),
                          reads=[bk(bu[idx])], writes=[("tmp", tas[idx])])
                for idx in range(2):
                    P.add("act", lambda e, b=bz[idx], tb=tbs[idx]: e.activation(out=tmps[tb][:, 0:T], in_=banks[b][:, 0:T], func=AF.Silu),
                          reads=[bk(bz[idx])], writes=[("tmp", tbs[idx])])
                for idx, j in enumerate(js):
                    ta, tb = tas[idx], tbs[idx]
                    P.add("dve", lambda e, ta=ta, tb=tb: e.tensor_tensor(out=tmps[ta][:, 0:T], in0=tmps[ta][:, 0:T], in1=tmps[tb][:, 0:T], op=ALU.mult),
                          reads=[("tmp", ta), ("tmp", tb)], writes=[("tmp", ta)])
                    P.add("dve", lambda e, b=bm[idx], ta=ta, j=j: e.tensor_tensor(out=mT[:, j, 0:T], in0=banks[b][:, 0:T], in1=tmps[ta][:, 0:T], op=ALU.mult),
                          reads=[bk(bm[idx]), ("tmp", ta)], writes=[("mT", j)])

            WA = max(ac + 28 + ln for (oc, ln, ac) in si.segs)
            ones32 = ones_bf[:, 0:32]

            def conv_stats_mm(j):
                dq = j % 2
                for q in range(4):
                    P.add("pe", lambda e, dq=dq, j=j, q=q: e.matmul(banks[6][32 * q:32 * q + 32, 0:T], lhsT=ones32, rhs=cbt[dq][:, 0:T], start=(j == 0), stop=(j == 7),
                                                                   tile_position=(0, 32 * q)),
                          reads=[("cbt", dq), "ones_bf"], writes=[bk(6)])
                for q in range(4):
                    P.add("pe", lambda e, dq=dq, j=j, q=q: e.matmul(banks[7][32 * q:32 * q + 32, 0:T], lhsT=ones32, rhs=sqt[dq][:, 0:T], start=(j == 0), stop=(j == 7),
                                                                   tile_position=(0, 32 * q)),
                          reads=[("sqt", dq), "ones_bf"], writes=[bk(7)])

            def emit_sel(j):
                sl = j % 2
                W0 = min(WA, 512)
                W1 = WA - W0
                bqs = [rot() for _ in range(4)]
                for q in range(4):
                    for i in range(4):
                        P.add("pe", lambda e, b=bqs[q], q=q, i=i, j=j, W0=W0: e.matmul(banks[b][32 * i:32 * i + 32, 0:W0], lhsT=identb[:, 32 * q:32 * q + 32],
                                                                                      rhs=aT[:, j, i:i + W0], start=True, stop=True, tile_position=(0, 32 * i)),
                              reads=["identb", ("aT", j), "aTh"], writes=[bk(bqs[q])])
                bsm = None
                if W1 > 0:
                    bsm = rot()
                    for q in range(4):
                        for i in range(4):
                            P.add("pe", lambda e, b=bsm, q=q, i=i, j=j, W1=W1: e.matmul(banks[b][32 * i:32 * i + 32, W1 * q:W1 * q + W1], lhsT=identb[:, 32 * q:32 * q + 32],
                                                                                       rhs=aT[:, j, 512 + i:512 + i + W1], start=True, stop=True, tile_position=(0, 32 * i)),
                                  reads=["identb", ("aT", j), "aTh"], writes=[bk(bsm)])
                for q in range(4):
                    if q < 2:
                        P.add("act", lambda e, b=bqs[q], q=q, sl=sl, W0=W0: e.activation(out=A4[sl][:, q, 0:W0], in_=banks[b][:, 0:W0], func=AF.Copy),
                              reads=[bk(bqs[q])], writes=[("A4", sl)])
                    else:
                        P.add("dve", lambda e, b=bqs[q], q=q, sl=sl, W0=W0: e.tensor_copy(out=A4[sl][:, q, 0:W0], in_=banks[b][:, 0:W0]),
                              reads=[bk(bqs[q])], writes=[("A4", sl)])
                if W1 > 0:
                    P.add("dve", lambda e, b=bsm, sl=sl, W1=W1: e.tensor_copy(out=A4[sl][:, :, 512:512 + W1], in_=banks[b][:, 0:4 * W1].rearrange("p (q w) -> p q w", q=4)),
                          reads=[bk(bsm)], writes=[("A4", sl)])

            preload_lnexp()
            emit_sel(0)
            for j in range(8):
                dq = j % 2
                sl = j % 2
                jj = j % 4
                if jj == 0:
                    wps, wpv = load_chunk(wp_bf[l, j // 4], "wp", ("s_wp", l))
                if j + 1 < 8:
                    emit_sel(j + 1)
                bc = rot()
                for (oc, ln, ac) in si.segs:
                    for m in range(8):
                        for q in range(4):
                            P.add("pe", lambda e, bc=bc, wpv=wpv, jj=jj, m=m, q=q, oc=oc, ln=ln, ac=ac, sl=sl: e.matmul(
                                banks[bc][32 * q:32 * q + 32, oc:oc + ln], lhsT=wpv[:, jj, m, q, :], rhs=A4[sl][:, q, ac + 4 * m:ac + 4 * m + ln],
                                start=(m == 0), stop=(m == 7), tile_position=(0, 32 * q)),
                                  reads=[("ring", wps), ("A4", sl)], writes=[bk(bc)])
                if j >= 1:
                    conv_stats_mm(j - 1)
                cbias = vecs[:, j, 4 + l:5 + l]
                P.add("act", lambda e, bc=bc, j=j, cbias=cbias: e.activation(out=cbuf[:, j, 0:T], in_=banks[bc][:, 0:T], func=AF.Identity, bias=cbias, scale=1.0),
                      reads=[bk(bc), "vecs"], writes=[("cbuf", j)])
                P.add("act", lambda e, bc=bc, dq=dq, cbias=cbias: e.activation(out=sqt[dq][:, 0:T], in_=banks[bc][:, 0:T], func=AF.Square, bias=cbias, scale=1.0),
                      reads=[bk(bc), "vecs"], writes=[("sqt", dq)])
                P.add("dve", lambda e, dq=dq, j=j: e.tensor_copy(out=cbt[dq][:, 0:T], in_=cbuf[:, j, 0:T]), reads=[("cbuf", j)], writes=[("cbt", dq)])
            conv_stats_mm(7)
            tc, td = tmp_next(), tmp_next()
            P.add("act", lambda e: e.activation(out=meanb[:, 0:T], in_=banks[6][:, 0:T], func=AF.Copy, scale=inv_d), reads=[bk(6)], writes=["meanb"])
            P.add("dve", lambda e, tc=tc: e.tensor_tensor(out=tmps[tc][:, 0:T], in0=meanb[:, 0:T], in1=meanb[:, 0:T], op=ALU.mult),
                  reads=["meanb"], writes=[("tmp", tc)])
            P.add("dve", lambda e, tc=tc, td=td: e.scalar_tensor_tensor(out=tmps[td][:, 0:T], in0=banks[7][:, 0:T], scalar=inv_d, in1=tmps[tc][:, 0:T],
                                                                        op0=ALU.mult, op1=ALU.subtract),
                  reads=[bk(7), ("tmp", tc)], writes=[("tmp", td)])
            rsqrt_chain(tmps[td][:, 0:T], rstdb[:, 0:T], T, 1.0, [("tmp", td)], "rstdb")

            if nxt is not None:
                nl, nsi = nxt
                emit_pass_loads(nl, nsi)
                if nl == 0 and nsi.kind != "s":
                    for i in range(nsi.T // 128):
                        emit_x_load(nsi, i)

            pend = None
            for j in range(8):
                jj = j % 4
                if jj == 0:
                    zbch = win_chunk(5 * D + (j // 4) * 512)
                bz = rot()
                s, view = zbch
                for k in range(8):
                    P.add("pe", lambda e, bz=bz, k=k, view=view, jj=jj: e.matmul(banks[bz][:, 0:T], lhsT=view[:, k, jj * 128:(jj + 1) * 128], rhs=xnT[:, k, 0:T],
                                                                               start=(k == 0), stop=(k == 7)),
                          reads=[("xnT", k), ("ring", s)], writes=[bk(bz)])
                tb, t1, ty = tmp_next(), tmp_next(), tmp_next()
                P.add("act", lambda e, bz=bz, tb=tb: e.activation(out=tmps[tb][:, 0:T], in_=banks[bz][:, 0:T], func=AF.Silu),
                      reads=[bk(bz)], writes=[("tmp", tb)])
                P.add("dve", lambda e, j=j, t1=t1: e.tensor_tensor(out=tmps[t1][:, 0:T], in0=cbuf[:, j, 0:T], in1=meanb[:, 0:T], op=ALU.subtract),
                      reads=[("cbuf", j), "meanb"], writes=[("tmp", t1)])
                P.add("dve", lambda e, t1=t1: e.tensor_tensor(out=tmps[t1][:, 0:T], in0=tmps[t1][:, 0:T], in1=rstdb[:, 0:T], op=ALU.mult),
                      reads=[("tmp", t1), "rstdb"], writes=[("tmp", t1)])
                P.add("act", lambda e, j=j, t1=t1, ty=ty: e.activation(out=tmps[ty][:, 0:T], in_=tmps[t1][:, 0:T], func=AF.Silu,
                                                                       bias=vecs[:, j, 12 + l:13 + l], scale=vecs[:, j, 8 + l:9 + l]),
                      reads=[("tmp", t1), "vecs"], writes=[("tmp", ty)])
                if pend is not None:
                    pj, pty, ptb = pend
                    P.add("dve", lambda e, pj=pj, pty=pty, ptb=ptb: e.tensor_tensor(out=mT[:, 8 + pj, 0:T], in0=tmps[pty][:, 0:T], in1=tmps[ptb][:, 0:T], op=ALU.mult),
                          reads=[("tmp", pty), ("tmp", ptb)], writes=[("mT", 8 + pj)])
                pend = (j, ty, tb)
            pj, pty, ptb = pend
            P.add("dve", lambda e, pj=pj, pty=pty, ptb=ptb: e.tensor_tensor(out=mT[:, 8 + pj, 0:T], in0=tmps[pty][:, 0:T], in1=tmps[ptb][:, 0:T], op=ALU.mult),
                  reads=[("tmp", pty), ("tmp", ptb)], writes=[("mT", 8 + pj)])

            woutl = wout_bf[l].rearrange("(k p) d -> p k d", p=128)
            pending_stats = []
            preload_lnexp()

            def flush_stats():
                for (j, dq) in pending_stats:
                    P.add("pe", lambda e, dq=dq, j=j: e.matmul(banks[6][:, 0:T], lhsT=ones_bf[:], rhs=sqt[dq][:, 0:T], start=(j == 0), stop=(j == 7)),
                          reads=[("sqt", dq), "ones_bf"], writes=[bk(6)])
                del pending_stats[:]

            for dh in range(2):
                woch = [load_chunk(woutl[:, kh * 8:(kh + 1) * 8, dh * 512:(dh + 1) * 512], 8, ("s_wout", l)) for kh in range(2)]
                bo = [rot() for _ in range(4)]
                for k in range(16):
                    s, view = woch[k // 8]
                    for jd in range(4):
                        P.add("pe", lambda e, b=bo[jd], k=k, view=view, jd=jd: e.matmul(banks[b][:, 0:T], lhsT=view[:, k % 8, jd * 128:(jd + 1) * 128], rhs=mT[:, k, 0:T],
                                                                                      start=(k == 0), stop=(k == 15)),
                              reads=[("mT", k), ("ring", s)], writes=[bk(bo[jd])])
                for jd in range(4):
                    j = dh * 4 + jd
                    dq = j % 2
                    if len(pending_stats) >= 2:
                        jj0, dq0 = pending_stats.pop(0)
                        P.add("pe", lambda e, dq=dq0, j=jj0: e.matmul(banks[6][:, 0:T], lhsT=ones_bf[:], rhs=sqt[dq][:, 0:T], start=(j == 0), stop=(j == 7)),
                              reads=[("sqt", dq0), "ones_bf"], writes=[bk(6)])
                    P.add("act", lambda e, b=bo[jd], j=j: e.activation(out=cbuf[:, j, 0:T], in_=banks[b][:, 0:T], func=AF.Copy),
                          reads=[bk(bo[jd])], writes=[("cbuf", j)])
                    P.add("act", lambda e, b=bo[jd], dq=dq: e.activation(out=sqt[dq][:, 0:T], in_=banks[b][:, 0:T], func=AF.Square),
                          reads=[bk(bo[jd])], writes=[("sqt", dq)])
                    pending_stats.append((j, dq))
            flush_stats()
            rsqrt_chain(banks[6][:, 0:T], rstdb[:, 0:T], T, inv_d, [bk(6)], "rstdb")
            for j in range(8):
                t1 = tmp_next()
                P.add("dve", lambda e, j=j, t1=t1: e.tensor_tensor(out=tmps[t1][:, 0:T], in0=cbuf[:, j, 0:T], in1=rstdb[:, 0:T], op=ALU.mult),
                      reads=[("cbuf", j), "rstdb"], writes=[("tmp", t1)])
                P.add("dve", lambda e, j=j, t1=t1: e.scalar_tensor_tensor(out=hT[:, j, 0:T], in0=tmps[t1][:, 0:T], scalar=vecs[:, j, 16 + l:17 + l], in1=hT[:, j, 0:T],
                                                                          op0=ALU.mult, op1=ALU.add),
                      reads=[("tmp", t1), "vecs", ("hT", j)], writes=[("hT", j)])
                P.add("act", lambda e, j=j: e.activation(out=xnT[:, j, 0:T], in_=hT[:, j, 0:T], func=AF.Copy),
                      reads=[("hT", j)], writes=[("xnT", j)])

            wpel = wpe_bf[l].rearrange("(k p) d -> p k d", p=128)
            wpgl = wpg_bf[l].rearrange("(k p) d -> p k d", p=128)
            pes, peview = load_chunk(wpel[:, :, :], 2, ("s_wpe", l))
            for jq in range(2):
                s, view = load_chunk(wpgl[:, :, jq * 512:(jq + 1) * 512], 8, ("s_wpg", l))
                bg = [rot() for _ in range(4)]
                for k in range(8):
                    for jd in range(4):
                        P.add("pe", lambda e, b=bg[jd], k=k, view=view, jd=jd: e.matmul(banks[b][:, 0:T], lhsT=view[:, k, jd * 128:(jd + 1) * 128], rhs=xnT[:, k, 0:T],
                                                                                      start=(k == 0), stop=(k == 7)),
                              reads=[("xnT", k), ("ring", s)], writes=[bk(bg[jd])])
                for jd in range(4):
                    j = jq * 4 + jd
                    bp = rot()
                    for kp in range(2):
                        P.add("pe", lambda e, bp=bp, kp=kp, j=j: e.matmul(banks[bp][:, 0:T], lhsT=peview[:, kp, j * 128:(j + 1) * 128], rhs=pT[:, kp, 0:T],
                                                                          start=(kp == 0), stop=(kp == 1)),
                              reads=[("pT", kp), ("ring", pes)], writes=[bk(bp)])
                    ta = tmp_next()
                    P.add("act", lambda e, b=bg[jd], ta=ta: e.activation(out=tmps[ta][:, 0:T], in_=banks[b][:, 0:T], func=AF.Sigmoid),
                          reads=[bk(bg[jd])], writes=[("tmp", ta)])
                    P.add("dve", lambda e, bp=bp, ta=ta: e.tensor_tensor(out=tmps[ta][:, 0:T], in0=banks[bp][:, 0:T], in1=tmps[ta][:, 0:T], op=ALU.mult),
                          reads=[bk(bp), ("tmp", ta)], writes=[("tmp", ta)])
                    P.add("dve", lambda e, j=j, ta=ta: e.tensor_tensor(out=hT[:, j, 0:T], in0=hT[:, j, 0:T], in1=tmps[ta][:, 0:T], op=ALU.add),
                          reads=[("hT", j), ("tmp", ta)], writes=[("hT", j)])

            if si.final:
                for q, (oc, ln, ac) in enumerate(si.segs):
                    for half in range(2):
                        b = rot()
                        for kk in range(4):
                            j = half * 4 + kk
                            P.add("pe", lambda e, b=b, j=j, kk=kk, q=q: e.transpose(banks[b][0:HIST, kk * 128:(kk + 1) * 128], afin[:, j, q, :], ident[:]),
                                  reads=["afin", "ident"], writes=[bk(b)])
                        P.add("act", lambda e, b=b, half=half, q=q: e.activation(out=yst2[q % 2][0:HIST, half * 512:(half + 1) * 512], in_=banks[b][0:HIST, :], func=AF.Copy),
                              reads=[bk(b)], writes=["yst" if q % 2 == 0 else "yst_b"])
                    if si.kind == "p":
                        dst = ncp[l, si.b]
                    else:
                        dst = ncs[l, q]
                    P.add("pool", lambda e, dst=dst, q=q: e.dma_start(out=dst, in_=yst2[q % 2][0:HIST, :]), reads=["yst" if q % 2 == 0 else "yst_b"], dma="o_cst%d" % (q % 2))
                    out_keys.append("o_cst%d" % (q % 2))

        st_list = []
        for b in range(2):
            for s in range(SEQ // ST_T):
                st_list.append(STInfo("p", b, s * ST_T, ST_T, [(0, ST_T, 0)], s == 0, s == SEQ // ST_T - 1))
        st_list.append(STInfo("s", 0, 0, 128, [(0, 64, 0), (64, 64, HIST + 64)], True, True))

        for idx_st, si in enumerate(st_list):
            T = si.T
            nt = T // 128
            if si.kind == "s":
                build_spatial(True)
            if idx_st == 0:
                emit_pass_loads(0, si)
            for i in range(nt):
                q = i
                xkey = "vst%d" % i
                if idx_st == 0 or si.kind == "s":
                    emit_x_load(si, i)
                for half in range(2):
                    b = rot()
                    for kk in range(4):
                        k = half * 4 + kk
                        P.add("pe", lambda e, b=b, k=k, kk=kk, q=q: e.transpose(banks[b][:, kk * 128:(kk + 1) * 128], vst[q][:, k * 128:(k + 1) * 128], ident[:]),
                              reads=[xkey, "ident"], writes=[bk(b)])
                    eng = "act" if half == 0 else "dve"
                    if eng == "act":
                        P.add("act", lambda e, b=b, half=half, i=i: e.activation(out=hT[:, half * 4:(half + 1) * 4, i * 128:(i + 1) * 128],
                                                                                 in_=banks[b][:, :].rearrange("p (k t) -> p k t", k=4), func=AF.Copy),
                              reads=[bk(b)], writes=[("hT", half * 4 + kk) for kk in range(4)])
                    else:
                        P.add("dve", lambda e, b=b, half=half, i=i: e.tensor_copy(out=hT[:, half * 4:(half + 1) * 4, i * 128:(i + 1) * 128],
                                                                                  in_=banks[b][:, :].rearrange("p (k t) -> p k t", k=4)),
                              reads=[bk(b)], writes=[("hT", half * 4 + kk) for kk in range(4)])
            for l in range(DEPTH):
                if l < DEPTH - 1:
                    nxt = (l + 1, si)
                elif idx_st + 1 < len(st_list):
                    nxt = (0, st_list[idx_st + 1])
                else:
                    nxt = None
                emit_pass(l, si, nxt)
            for i in range(nt):
                yb = yst2[i % 2]
                ykey = "yst" if i % 2 == 0 else "yst_b"
                for half in range(2):
                    b = rot()
                    for kk in range(4):
                        k = half * 4 + kk
                        P.add("pe", lambda e, b=b, k=k, kk=kk, i=i: e.transpose(banks[b][:, kk * 128:(kk + 1) * 128], hT[:, k, i * 128:(i + 1) * 128], ident[:]),
                              reads=[("hT", k), "ident"], writes=[bk(b)])
                    if half == 0:
                        P.add("act", lambda e, b=b, half=half, yb=yb: e.activation(out=yb[:, half * 512:(half + 1) * 512], in_=banks[b][:, :], func=AF.Copy),
                              reads=[bk(b)], writes=[ykey])
                    else:
                        P.add("dve", lambda e, b=b, half=half, yb=yb: e.tensor_copy(out=yb[:, half * 512:(half + 1) * 512], in_=banks[b][:, :]),
                              reads=[bk(b)], writes=[ykey])
                if si.kind == "p":
                    dst = y_p[si.b, si.t0 + i * 128:si.t0 + (i + 1) * 128, :]
                else:
                    dst = y_s.rearrange("b t d -> (b t) d")
                P.add("pool", lambda e, dst=dst, yb=yb: e.dma_start(out=dst, in_=yb[:, :]), reads=[ykey], dma="o_y%d" % (i % 2))
                out_keys.append("o_y%d" % (i % 2))

        P.emit(final_wait_bufs=sorted(set(out_keys), key=str))
    return nc


_CACHE = {}


def kernel(x_prompt, x_sample, state_conv, p_prompt, p_sample, g_pre, w_in, ln_v_g, ln_v_b, w_s, b_s,
           conv_w, conv_b, ln_c_g, ln_c_b, w_out, g_post, w_pe, w_pg):
    f = lambda a: np.ascontiguousarray(np.asarray(a, dtype=np.float32))
    x_prompt, x_sample, state_conv, p_prompt, p_sample = map(f, (x_prompt, x_sample, state_conv, p_prompt, p_sample))
    shared = {"g_pre": f(g_pre), "w_in": f(w_in), "ln_v_g": f(ln_v_g), "ln_v_b": f(ln_v_b), "w_s": f(w_s), "b_s": f(b_s),
              "conv_w": f(conv_w), "conv_b": f(conv_b), "ln_c_g": f(ln_c_g), "ln_c_b": f(ln_c_b), "w_out": f(w_out),
              "g_post": f(g_post), "w_pe": f(w_pe), "w_pg": f(w_pg)}
    if "nc" not in _CACHE:
        _CACHE["nc"] = build_program()
    nc = _CACHE["nc"]
    in_maps = []
    for c in range(NCORES):
        sl = slice(2 * c, 2 * c + 2)
        m = dict(shared)
        m["xp"] = np.ascontiguousarray(x_prompt[sl])
        m["xsm"] = np.ascontiguousarray(x_sample[sl])
        m["sc"] = np.ascontiguousarray(state_conv[:, sl])
        m["pp"] = np.ascontiguousarray(p_prompt[:, sl])
        m["psm"] = np.ascontiguousarray(p_sample[:, sl])
        in_maps.append(m)
    res = run_bass_kernel_spmd(nc, in_maps, core_ids=list(range(NCORES)))
    rs = res.results
    y_prompt = np.concatenate([np.asarray(r["y_p"], dtype=np.float32) for r in rs], axis=0)
    y_sample = np.concatenate([np.asarray(r["y_s"], dtype=np.float32) for r in rs], axis=0)
    new_conv_prompt = np.concatenate([np.asarray(r["ncp"], dtype=np.float32) for r in rs], axis=1)
    new_conv_sample = np.concatenate([np.asarray(r["ncs"], dtype=np.float32) for r in rs], axis=1)
    new_gmlp_v_sample = np.concatenate([np.asarray(r["vs_o"], dtype=np.float32) for r in rs], axis=1)
    return (y_prompt, y_sample, new_conv_prompt, new_conv_sample, new_gmlp_v_sample)
```
